# Optimizing a Trainium2 kernel written in Bass

```python
import math
import jax, jax.numpy as jnp
from jax import lax
import numpy as np

D_MODEL = 1024
BATCH = 8
SEQ = 2048
DEPTH = 2

RET_HEADS = 4
RET_DK = 256
RET_DV = 512
RET_CHUNK = 128
MLA_HEADS = 8
MLA_NOPE = 128
MLA_ROPE = 64
MLA_V = 128
MLA_Q_RANK = 256
MLA_KV_RANK = 256
ATTN_Q_BLOCK = 128
MOBA_HEADS = 16
MOBA_DH = D_MODEL // MOBA_HEADS
MOBA_BLOCK = 256
MOBA_TOPK = 3
MOBA_Q_CHUNK = 32

ROPE_THETA = 10000.0
LN_EPS = 1e-5
RMS_EPS = 1e-6
NEG = -1e30
DN_ALPHA = (2.0 * DEPTH) ** 0.25
DN_BETA = (8.0 * DEPTH) ** -0.25
N_EVEN = (DEPTH + 1) // 2
N_ODD = DEPTH // 2

RET_W = RET_HEADS * RET_DV
MLA_W = MLA_HEADS * MLA_V
MOBA_W = MOBA_HEADS * MOBA_DH
EVEN_SPLITS = (RET_HEADS * RET_DK, RET_HEADS * RET_DK, RET_W, RET_W,
               MLA_Q_RANK, MLA_KV_RANK, MLA_ROPE, MLA_W)
EVEN_IN = sum(EVEN_SPLITS)
ODD_SPLITS = (MOBA_W, MOBA_W, MOBA_W, MOBA_W)
ODD_IN = sum(ODD_SPLITS)

kernel_name = "hybrid_retention_mla_moba_deepnorm"


def _split(h, sizes):
    offs = np.cumsum(sizes)[:-1].tolist()
    return jnp.split(h, offs, axis=-1)


def _heads(t, n):
    B, S, W = t.shape
    return t.reshape(B, S, n, W // n).transpose(0, 2, 1, 3)


def _merge(t):
    B, H, S, d = t.shape
    return t.transpose(0, 2, 1, 3).reshape(B, S, H * d)


def layer_norm(x, g, b):
    xf = x.astype(jnp.float32)
    mu = jnp.mean(xf, -1, keepdims=True)
    xc = xf - mu
    var = jnp.mean(xc * xc, -1, keepdims=True)
    return (xc * lax.rsqrt(var + LN_EPS) * g + b).astype(x.dtype)


def rms_norm(x, g):
    xf = x.astype(jnp.float32)
    return (xf * lax.rsqrt(jnp.mean(xf * xf, -1, keepdims=True) + RMS_EPS) * g).astype(x.dtype)


def rope(x, pos):
    d = x.shape[-1]
    inv = 1.0 / (ROPE_THETA ** (jnp.arange(0, d, 2, dtype=jnp.float32) / d))
    ang = pos.astype(jnp.float32)[:, None, :, None] * inv
    cos, sin = jnp.cos(ang), jnp.sin(ang)
    x1, x2 = x[..., : d // 2], x[..., d // 2:]
    return jnp.concatenate([x1 * cos - x2 * sin, x1 * sin + x2 * cos], -1).astype(x.dtype)


def retnet_rotate(x, pos):
    d = x.shape[-1]
    inv = 1.0 / (ROPE_THETA ** jnp.linspace(0.0, 1.0, d // 2, dtype=jnp.float32))
    ang = pos.astype(jnp.float32)[:, None, :, None] * inv
    cos, sin = jnp.cos(ang), jnp.sin(ang)
    x1, x2 = x[..., 0::2], x[..., 1::2]
    out = jnp.stack([x1 * cos - x2 * sin, x2 * cos + x1 * sin], -1)
    return out.reshape(x.shape).astype(x.dtype)


def retention_chunkwise(q, k, v):
    B, H, S, dk = q.shape
    dv = v.shape[-1]
    C = RET_CHUNK
    n = S // C
    q = q.astype(jnp.float32)
    k = k.astype(jnp.float32) * (dk ** -0.5)
    v = v.astype(jnp.float32)
    log_g = jnp.log(1.0 - 2.0 ** (-5.0 - jnp.arange(H, dtype=jnp.float32)))
    qc = q.reshape(B, H, n, C, dk)
    kc = k.reshape(B, H, n, C, dk)
    vc = v.reshape(B, H, n, C, dv)
    idx = jnp.arange(C, dtype=jnp.float32)
    rel = idx[:, None] - idx[None, :]
    decay = jnp.where(rel >= 0, jnp.exp(log_g[:, None, None] * jnp.maximum(rel, 0.0)), 0.0)
    inner = jnp.einsum('bhncd,bhnkd->bhnck', qc, kc) * decay[None, :, None]
    o_inner = jnp.einsum('bhnck,bhnkv->bhncv', inner, vc)
    zeta = jnp.exp(log_g[:, None] * (C - 1.0 - idx))
    kv_chunk = jnp.einsum('bhnkd,bhnkv->bhndv', kc * zeta[None, :, None, :, None], vc)
    g_chunk = jnp.exp(log_g * C)[None, :, None, None]

    def step(R, kv):
        return R * g_chunk + kv, R

    _, r_prev = lax.scan(step, jnp.zeros((B, H, dk, dv), jnp.float32), jnp.moveaxis(kv_chunk, 2, 0))
    r_prev = jnp.moveaxis(r_prev, 0, 2)
    xi = jnp.exp(log_g[:, None] * (idx + 1.0))
    o_cross = jnp.einsum('bhncd,bhndv->bhncv', qc, r_prev) * xi[None, :, None, :, None]
    return (o_inner + o_cross).reshape(B, H, S, dv)


def head_group_norm(o):
    of = o.astype(jnp.float32)
    mu = jnp.mean(of, -1, keepdims=True)
    oc = of - mu
    var = jnp.mean(oc * oc, -1, keepdims=True)
    return oc * lax.rsqrt(var + LN_EPS)


def causal_attention(q, k, v, scale):
    B, H, S, d = q.shape
    dv = v.shape[-1]
    nq = S // ATTN_Q_BLOCK
    qb = q.reshape(B, H, nq, ATTN_Q_BLOCK, d).transpose(2, 0, 1, 3, 4)
    kpos = jnp.arange(S)

    def one(args):
        qi, i = args
        s = jnp.einsum('bhqd,bhkd->bhqk', qi, k).astype(jnp.float32) * scale
        qpos = i * ATTN_Q_BLOCK + jnp.arange(ATTN_Q_BLOCK)
        s = jnp.where(kpos[None, :] <= qpos[:, None], s, NEG)
        p = jax.nn.softmax(s, axis=-1).astype(v.dtype)
        return jnp.einsum('bhqk,bhkd->bhqd', p, v)

    out = lax.map(one, (qb, jnp.arange(nq)))
    return out.transpose(1, 2, 0, 3, 4).reshape(B, H, S, dv)


def moba_attention(q, k, v):
    B, H, S, dh = q.shape
    L = MOBA_BLOCK
    nb = -(-S // L)
    pad = nb * L - S
    kp = jnp.pad(k, ((0, 0), (0, 0), (0, pad), (0, 0)))
    vp = jnp.pad(v, ((0, 0), (0, 0), (0, pad), (0, 0)))
    kb = kp.reshape(B, H, nb, L, dh)
    vb = vp.reshape(B, H, nb, L, dh)
    kmean = jnp.mean(kb.astype(jnp.float32), axis=3)
    qblk = jnp.arange(S) // L
    gate = jnp.einsum('bhsd,bhnd->bhsn', q.astype(jnp.float32), kmean)
    past = jnp.arange(nb)[None, :] < qblk[:, None]
    gate = jnp.where(past, gate, NEG)
    k_sel = min(MOBA_TOPK, nb)
    _, sel = lax.top_k(gate, k_sel)
    valid = jnp.arange(k_sel)[None, :] < qblk[:, None]
    scale = dh ** -0.5
    Qc = MOBA_Q_CHUNK
    nc = S // Qc
    qch = jnp.moveaxis(q.reshape(B, H, nc, Qc, dh), 2, 0)
    selch = jnp.moveaxis(sel.reshape(B, H, nc, Qc, k_sel), 2, 0)
    validch = valid.reshape(nc, Qc, k_sel)
    bi = jnp.arange(B)[:, None, None, None]
    hi = jnp.arange(H)[None, :, None, None]

    def one(args):
        qi, si, vi, c = args
        q0 = c * Qc
        qpos = q0 + jnp.arange(Qc)
        own = q0 // L
        kg = kb[bi, hi, si]
        vg = vb[bi, hi, si]
        s_g = jnp.einsum('bhqd,bhqnld->bhqnl', qi, kg).astype(jnp.float32) * scale
        s_g = jnp.where(vi[None, None, :, :, None], s_g, NEG)
        ko = lax.dynamic_slice_in_dim(kp, own * L, L, axis=2)
        vo = lax.dynamic_slice_in_dim(vp, own * L, L, axis=2)
        s_o = jnp.einsum('bhqd,bhld->bhql', qi, ko).astype(jnp.float32) * scale
        kpos = own * L + jnp.arange(L)
        s_o = jnp.where(kpos[None, :] <= qpos[:, None], s_o, NEG)
        s = jnp.concatenate([s_g.reshape(B, H, Qc, k_sel * L), s_o], axis=-1)
        p = jax.nn.softmax(s, axis=-1).astype(v.dtype)
        p_g = p[..., : k_sel * L].reshape(B, H, Qc, k_sel, L)
        p_o = p[..., k_sel * L:]
        return (jnp.einsum('bhqnl,bhqnld->bhqd', p_g, vg)
                + jnp.einsum('bhql,bhld->bhqd', p_o, vo))

    out = lax.map(one, (qch, selch, validch, jnp.arange(nc)))
    return jnp.moveaxis(out, 0, 2).reshape(B, H, S, dh)


def retention_mla_layer(x, pos, w_in, q_norm_g, w_uq, kv_norm_g, w_ukv, w_out):
    B, S, _ = x.shape
    h = x @ w_in
    rq, rk, rv, rg, mq, mkv, mkr, mg = _split(h, EVEN_SPLITS)
    q_r = retnet_rotate(_heads(rq, RET_HEADS), pos)
    k_r = retnet_rotate(_heads(rk, RET_HEADS), pos)
    o_r = head_group_norm(retention_chunkwise(q_r, k_r, _heads(rv, RET_HEADS)))
    ret_out = (_merge(o_r) * jax.nn.silu(rg.astype(jnp.float32))).astype(x.dtype)
    q_m = (rms_norm(mq, q_norm_g) @ w_uq).reshape(B, S, MLA_HEADS, MLA_NOPE + MLA_ROPE).transpose(0, 2, 1, 3)
    q_nope, q_pe = q_m[..., :MLA_NOPE], rope(q_m[..., MLA_NOPE:], pos)
    kv = (rms_norm(mkv, kv_norm_g) @ w_ukv).reshape(B, S, MLA_HEADS, MLA_NOPE + MLA_V).transpose(0, 2, 1, 3)
    k_nope, v_m = kv[..., :MLA_NOPE], kv[..., MLA_NOPE:]
    k_pe = rope(mkr[:, None], pos)
    q_full = jnp.concatenate([q_nope, q_pe], -1)
    k_full = jnp.concatenate([k_nope, jnp.broadcast_to(k_pe, (B, MLA_HEADS, S, MLA_ROPE))], -1)
    o_m = causal_attention(q_full, k_full, v_m, (MLA_NOPE + MLA_ROPE) ** -0.5)
    mla_out = (_merge(o_m) * jax.nn.silu(mg)).astype(x.dtype)
    return jnp.concatenate([ret_out, mla_out], axis=-1) @ w_out


def moba_layer(x, pos, w_in, w_out):
    h = x @ w_in
    q, k, v, g = _split(h, ODD_SPLITS)
    q = rope(_heads(q, MOBA_HEADS), pos)
    k = rope(_heads(k, MOBA_HEADS), pos)
    o = moba_attention(q, k, _heads(v, MOBA_HEADS))
    return (_merge(o) * jax.nn.silu(g)) @ w_out


def setup_inputs(seed: int = 0) -> dict:
    key = jax.random.key(seed)
    ks = jax.random.split(key, 16)
    f32 = jnp.float32
    nrm = lambda k, shape, fan_in, gain=1.0: jax.random.normal(k, shape, f32) * (gain * fan_in ** -0.5)
    x = jax.random.normal(ks[0], (BATCH, SEQ, D_MODEL), f32)
    offs = jax.random.randint(ks[1], (BATCH, 1), 0, 1024, dtype=jnp.int32)
    positions = (offs + jnp.arange(SEQ, dtype=jnp.int32)[None, :]).astype(jnp.int32)
    return {
        "x": x,
        "positions": positions,
        "w_in_even": nrm(ks[2], (N_EVEN, D_MODEL, EVEN_IN), D_MODEL),
        "q_norm_even": 1.0 + 0.02 * jax.random.normal(ks[3], (N_EVEN, MLA_Q_RANK), f32),
        "w_uq_even": nrm(ks[4], (N_EVEN, MLA_Q_RANK, MLA_HEADS * (MLA_NOPE + MLA_ROPE)), MLA_Q_RANK),
        "kv_norm_even": 1.0 + 0.02 * jax.random.normal(ks[5], (N_EVEN, MLA_KV_RANK), f32),
        "w_ukv_even": nrm(ks[6], (N_EVEN, MLA_KV_RANK, MLA_HEADS * (MLA_NOPE + MLA_V)), MLA_KV_RANK),
        "w_out_even": nrm(ks[7], (N_EVEN, RET_W + MLA_W, D_MODEL), RET_W + MLA_W, DN_BETA),
        "w_in_odd": nrm(ks[8], (N_ODD, D_MODEL, ODD_IN), D_MODEL),
        "w_out_odd": nrm(ks[9], (N_ODD, MOBA_W, D_MODEL), MOBA_W, DN_BETA),
        "ln_g": 1.0 + 0.02 * jax.random.normal(ks[10], (DEPTH, D_MODEL), f32),
        "ln_b": 0.02 * jax.random.normal(ks[11], (DEPTH, D_MODEL), f32),
    }


def reference(x, positions, w_in_even, q_norm_even, w_uq_even, kv_norm_even, w_ukv_even,
              w_out_even, w_in_odd, w_out_odd, ln_g, ln_b):
    for i in range(DEPTH):
        j = i // 2
        if i % 2 == 0:
            y = retention_mla_layer(x, positions, w_in_even[j], q_norm_even[j], w_uq_even[j],
                                    kv_norm_even[j], w_ukv_even[j], w_out_even[j])
        else:
            y = moba_layer(x, positions, w_in_odd[j], w_out_odd[j])
        x = layer_norm(DN_ALPHA * x + y, ln_g[i], ln_b[i])
    return x
```

```python
import math
import os
from contextlib import ExitStack

import numpy as np
import concourse.bass as bass
import concourse.mybir as mybir
from concourse.bass_utils import run_bass_kernel_spmd

F32 = mybir.dt.float32
BF16 = mybir.dt.bfloat16
I32 = mybir.dt.int32
ALU = mybir.AluOpType
AF = mybir.ActivationFunctionType
AX = mybir.AxisListType

S_LEN = 2048
D = 1024
NT = 16
ALPHA = 4.0 ** 0.25
LN_EPS = 1e-5
RMS_EPS = 1e-6
NEGB = 30000.0


class Sched:
    ENGS = ("pe", "act", "dve", "pool", "sp")

    def __init__(self, nc):
        self.nc = nc
        self.ops = []
        self.last_w = {}
        self.readers = {}
        self.eng_sems = {}
        self.dma_sems = {}
        self.dma_cnt = {}
        self._stack = []
        self.barrier_deps = set()
        self.last_on_eng = {}
        self.last_dma = {}
        for e in ("pe", "act", "dve", "pool"):
            self.eng_sems[e] = self.sem("s_" + e)

    def sem(self, name):
        cm = self.nc.semaphore(name)
        s = cm.__enter__()
        self._stack.append(cm)
        return s

    def add(self, eng, fn, reads=(), writes=(), dma=None):
        idx = len(self.ops)
        deps = set(self.barrier_deps)
        for k in reads:
            w = self.last_w.get(k)
            if w is not None:
                deps.add(w)
        for k in writes:
            w = self.last_w.get(k)
            if w is not None:
                deps.add(w)
            for r in self.readers.get(k, ()):
                deps.add(r)
        op = dict(eng=eng, fn=fn, deps=deps, dma=dma, idx=idx, signal=False, tok=None)
        self.ops.append(op)
        for k in reads:
            self.readers.setdefault(k, []).append(idx)
        for k in writes:
            self.last_w[k] = idx
            self.readers[k] = []
        if dma is not None:
            self.last_dma[dma] = idx
        else:
            self.last_on_eng[eng] = idx
        return idx

    def barrier(self):
        self.barrier_deps = set(self.last_on_eng.values()) | set(self.last_dma.values())

    def finalize(self):
        ops = self.ops
        for op in ops:
            for d in op["deps"]:
                p = ops[d]
                if p["dma"] is not None:
                    continue
                if p["eng"] == "pe" and op["eng"] == "pe" and op["dma"] is None:
                    continue
                p["signal"] = True
        cnt = {e: 0 for e in self.eng_sems}
        for op in ops:
            if op["dma"] is not None:
                key = op["dma"]
                if key not in self.dma_sems:
                    self.dma_sems[key] = self.sem("d_%d" % len(self.dma_sems))
                    self.dma_cnt[key] = 0
                self.dma_cnt[key] += 16
                op["tok"] = (self.dma_sems[key], self.dma_cnt[key])
            elif op["signal"]:
                cnt[op["eng"]] += 1
                op["tok"] = (self.eng_sems[op["eng"]], cnt[op["eng"]])
        streams = {e: [] for e in self.ENGS}
        for op in ops:
            streams[op["eng"]].append(op)
        sched = self

        def emit(engname, engine):
            known = {}
            for op in streams[engname]:
                need = {}
                for d in op["deps"]:
                    p = ops[d]
                    if p["tok"] is None:
                        continue
                    if p["dma"] is None and p["eng"] == "pe" and engname == "pe" and op["dma"] is None:
                        continue
                    s, v = p["tok"]
                    if need.get(id(s), (None, 0))[1] < v:
                        need[id(s)] = (s, v)
                for sid, (s, v) in need.items():
                    if known.get(sid, 0) >= v:
                        continue
                    engine.wait_ge(s, v)
                    known[sid] = v
                ins = op["fn"](engine)
                if op["dma"] is not None:
                    ins.then_inc(op["tok"][0], 16)
                elif op["signal"]:
                    ins.then_inc(op["tok"][0], 1)
            if engname == "sp":
                for key, s in sched.dma_sems.items():
                    engine.wait_ge(s, sched.dma_cnt[key])
                for e, s in sched.eng_sems.items():
                    if cnt[e] > 0:
                        engine.wait_ge(s, cnt[e])

        with self.nc.Block() as block:
            @block.tensor
            def _(e):
                emit("pe", e)

            @block.scalar
            def _(e):
                emit("act", e)

            @block.vector
            def _(e):
                emit("dve", e)

            @block.gpsimd
            def _(e):
                emit("pool", e)

            @block.sync
            def _(e):
                emit("sp", e)

    def close(self):
        while self._stack:
            self._stack.pop().__exit__(None, None, None)

    def dma(self, q, out, in_, reads, writes, key):
        return self.add(q, lambda e: e.dma_start(out=out, in_=in_), reads=reads, writes=writes, dma=key)

    def mms(self, specs, reads, writes):
        def fn(e):
            ins = None
            for (o, l, r, st, sp) in specs:
                ins = e.matmul(o, lhsT=l, rhs=r, start=st, stop=sp)
            return ins
        return self.add("pe", fn, reads=reads, writes=writes)

    def transposes(self, specs, ident, reads, writes):
        def fn(e):
            ins = None
            for (o, i) in specs:
                ins = e.transpose(out=o, in_=i, identity=ident)
            return ins
        return self.add("pe", fn, reads=reads, writes=writes)

    def act(self, out, in_, func, reads, writes, scale=1.0, bias=None, eng="act"):
        if bias is None:
            return self.add(eng, lambda e: e.activation(out=out, in_=in_, func=func, scale=scale), reads=reads, writes=writes)
        return self.add(eng, lambda e: e.activation(out=out, in_=in_, func=func, scale=scale, bias=bias), reads=reads, writes=writes)

    def tt(self, eng, out, in0, in1, op, reads, writes):
        return self.add(eng, lambda e: e.tensor_tensor(out=out, in0=in0, in1=in1, op=op), reads=reads, writes=writes)

    def ts(self, eng, out, in0, s1, s2, op0, op1, reads, writes):
        if s2 is None:
            return self.add(eng, lambda e: e.tensor_scalar(out=out, in0=in0, scalar1=s1, scalar2=None, op0=op0), reads=reads, writes=writes)
        return self.add(eng, lambda e: e.tensor_scalar(out=out, in0=in0, scalar1=s1, scalar2=s2, op0=op0, op1=op1), reads=reads, writes=writes)

    def stt(self, eng, out, in0, scalar, in1, op0, op1, reads, writes):
        return self.add(eng, lambda e: e.scalar_tensor_tensor(out=out, in0=in0, scalar=scalar, in1=in1, op0=op0, op1=op1), reads=reads, writes=writes)

    def copy(self, eng, out, in_, reads, writes):
        if eng == "act":
            return self.add(eng, lambda e: e.copy(out=out, in_=in_), reads=reads, writes=writes)
        return self.add(eng, lambda e: e.tensor_copy(out=out, in_=in_), reads=reads, writes=writes)


def build_program(n_layers=2, debug_out=None):
    nc = bass.Bass("TRN2", target_bir_lowering=False)

    def dram_in(name, shape, dt=F32):
        return nc.dram_tensor(name, list(shape), dt, kind="ExternalInput").ap()

    x_d = dram_in("x", [S_LEN, D])
    pos_d = dram_in("pos", [1, S_LEN], I32)
    w_ret = dram_in("w_ret", [4, D, 1536])
    w_mla = dram_in("w_mla", [D, 1792])
    w_uq = dram_in("w_uq", [256, 2048])
    w_ukv = dram_in("w_ukv", [256, 2048])
    w_oe = dram_in("w_oe", [3072, D])
    w_l1 = dram_in("w_l1", [D, 6144])
    w_oo = dram_in("w_oo", [D, D])
    qn_d = dram_in("qn", [128, 2])
    kvn_d = dram_in("kvn", [128, 2])
    lng_d = dram_in("lng", [2, D])
    lnb_d = dram_in("lnb", [2, D])
    c_ident = dram_in("c_ident", [128, 128])
    c_mask = dram_in("c_mask", [128, 128])
    c_vec = dram_in("c_vec", [128, 16])
    c_xi2 = dram_in("c_xi2", [128, 4])
    c_E = dram_in("c_E", [8, 1024])
    c_past = dram_in("c_past", [128, 128])
    c_own = dram_in("c_own", [128, 128])
    c_negoff = dram_in("c_negoff", [128, 128])
    out_d = nc.dram_tensor("out", [S_LEN, D], F32, kind="ExternalOutput").ap()

    S = Sched(nc)
    top = ExitStack()

    def sb(es, name, shape, dt):
        return es.enter_context(nc.sbuf_tensor("sb_" + name, list(shape), dt))

    yacc = sb(top, "yacc", [128, NT, D], F32)
    xT = sb(top, "xT", [128, 8, S_LEN], BF16)
    wring = sb(top, "wring", [128, 8, 1024], BF16)
    wo = sb(top, "wo", [128, 4, D], BF16)
    ident = sb(top, "ident", [128, 128], BF16)
    maskT = sb(top, "maskT", [128, 128], BF16)
    ones_b = sb(top, "ones_b", [128, 128], BF16)
    cvec = sb(top, "cvec", [128, 16], F32)
    xi2 = sb(top, "xi2", [128, 4], F32)
    qn = sb(top, "qn", [128, 2], F32)
    kvn = sb(top, "kvn", [128, 2], F32)
    Eall = sb(top, "Eall", [8, 1024], BF16)
    past01 = sb(top, "past01", [128, 128], F32)
    own01 = sb(top, "own01", [128, 128], F32)
    negoff = sb(top, "negoff", [128, 128], F32)
    tmpA = sb(top, "tmpA", [128, 512], F32)
    tmpB = sb(top, "tmpB", [128, 512], F32)
    tmpI = sb(top, "tmpI", [128, 512], I32)

    PA = top.enter_context(nc.psum_tensor("PA", [128, 1024], F32))
    PB = top.enter_context(nc.psum_tensor("PB", [128, 1024], F32))
    PC = top.enter_context(nc.psum_tensor("PC", [128, 1024], F32))
    PD = top.enter_context(nc.psum_tensor("PD", [128, 1024], F32))

    inv_r = cvec[:, 0:1]
    inv64 = cvec[:, 1:2]
    sgn64 = cvec[:, 2:3]
    eps_ln = cvec[:, 3:4]
    eps_rms = cvec[:, 4:5]

    ring_pos = [0]

    def load_w(src2d, ncols):
        nsl = ncols // 128
        s0 = ring_pos[0]
        if s0 + nsl > 8:
            s0 = 0
        ring_pos[0] = (s0 + nsl) % 8
        kc = src2d.shape[0] // 128
        keys = [("wr", s) for s in range(s0, s0 + nsl)]
        S.dma("pool", wring[:, 0:kc, s0 * 128:(s0 + nsl) * 128], src2d.rearrange("(c p) n -> p c n", p=128),
              reads=[], writes=keys, key=("wr", s0))
        return s0, keys

    def wslice(s0, kc, c0=0, ncols=128):
        return wring[:, kc, s0 * 128 + c0: s0 * 128 + c0 + ncols]

    S.dma("pool", ident[:], c_ident, [], ["ident"], "c0")
    S.dma("pool", maskT[:], c_mask, [], ["maskT"], "c1")
    S.dma("pool", Eall[:], c_E, [], ["Eall"], "c2")
    S.dma("sp", cvec[:], c_vec, [], ["cvec"], "c3")
    S.dma("sp", xi2[:], c_xi2, [], ["xi2"], "c4")
    S.dma("sp", qn[:], qn_d, [], ["qn"], "c5")
    S.dma("sp", kvn[:], kvn_d, [], ["kvn"], "c6")
    S.dma("sp", past01[:], c_past, [], ["past01"], "c7")
    S.dma("sp", own01[:], c_own, [], ["own01"], "c8")
    S.dma("sp", negoff[:], c_negoff, [], ["negoff"], "c9")
    S.add("pool", lambda e: e.memset(ones_b[:], 1.0), writes=["ones_b"])

    def gen_tables(tab, inv_ap, sgn_ap):
        for j in range(4):
            cs = slice(j * 512, (j + 1) * 512)
            S.dma("sp", tmpI[:], bass.AP(pos_d.tensor, j * 512, [[0, 128], [1, 512]]), [], ["tmpI"], "posld")
            S.copy("dve", tmpA[:], tmpI[:], ["tmpI"], ["tmpA"])
            S.ts("dve", tmpA[:], tmpA[:], inv_ap, None, ALU.mult, None, ["tmpA", "cvec"], ["tmpA"])
            for which in (1, 0):
                shift = 0.0 if which == 1 else 0.25
                S.ts("dve", tmpI[:], tmpA[:], 1.0 / (2 * math.pi), shift, ALU.mult, ALU.add, ["tmpA"], ["tmpI"])
                S.copy("dve", tmpB[:], tmpI[:], ["tmpI"], ["tmpB"])
                S.stt("dve", tmpB[:], tmpB[:], -2 * math.pi, tmpA[:], ALU.mult, ALU.add, ["tmpA", "tmpB"], ["tmpB"])
                if which == 0:
                    S.ts("dve", tmpB[:], tmpB[:], math.pi / 2, None, ALU.add, None, ["tmpB"], ["tmpB"])
                S.ts("dve", tmpB[:], tmpB[:], -3.141592, 3.141592, ALU.max, ALU.min, ["tmpB"], ["tmpB"])
                if which == 1 and sgn_ap is not None:
                    S.act(tab[:, 1, cs], tmpB[:], AF.Sin, ["tmpB", "cvec"], [("tab", j)], scale=sgn_ap)
                else:
                    S.act(tab[:, which, cs], tmpB[:], AF.Sin, ["tmpB"], [("tab", j)])

    with ExitStack() as es0:
        xs = sb(es0, "xs", [128, 2, D], F32)
        xb = sb(es0, "xb", [128, 2, D], BF16)
        for t in range(NT):
            s = t % 2
            S.dma("sp", xs[:, s, :], x_d[t * 128:(t + 1) * 128, :], [], [("xs", s)], ("xs", s))
            S.add("act", (lambda s=s, t=t: (lambda e: e.mul(out=yacc[:, t, :], in_=xs[:, s, :], mul=ALPHA)))(),
                  reads=[("xs", s)], writes=[("yacc", t)])
            S.copy("dve", xb[:, s, :], xs[:, s, :], [("xs", s)], [("xb", s)])
            pt = PA[:].bitcast(BF16) if s == 0 else PB[:].bitcast(BF16)
            S.transposes([(pt[:, c * 128:(c + 1) * 128], xb[:, s, c * 128:(c + 1) * 128]) for c in range(8)], ident[:],
                         reads=[("xb", s), "ident"], writes=[("P", s, 0)])
            S.copy("dve" if s == 0 else "act", xT[:, :, t * 128:(t + 1) * 128],
                   pt[:, 0:1024].rearrange("p (c k) -> p c k", k=128), [("P", s, 0)], [("xT", t)])
        S.barrier()
    XT_ALL = [("xT", t) for t in range(NT)]

    def xt_keys(j):
        return [("xT", 4 * j + i) for i in range(4)]

    def layer_norm(layer, last):
        with ExitStack() as es:
            gb = sb(es, "gb%d" % layer, [128, 2, D], F32)
            mv = sb(es, "mv%d" % layer, [128, NT, 2], F32)
            st = sb(es, "st%d" % layer, [128, 2, 6], F32)
            rstd = sb(es, "rstd%d" % layer, [128, NT], F32)
            nb = sb(es, "nb%d" % layer, [128, NT], F32)
            x1 = sb(es, "x1_%d" % layer, [128, 2, D], F32)
            x1b = sb(es, "x1b_%d" % layer, [128, 2, D], BF16)
            S.dma("sp", gb[:, 0, :], bass.AP(lng_d.tensor, layer * D, [[0, 128], [1, D]]), [], ["gb0"], "gb0")
            S.dma("sp", gb[:, 1, :], bass.AP(lnb_d.tensor, layer * D, [[0, 128], [1, D]]), [], ["gb1"], "gb1")
            for t in range(NT):
                for hh in range(2):
                    S.add("dve", (lambda t=t, hh=hh: (lambda e: e.bn_stats(out=st[:, hh, :], in_=yacc[:, t, hh * 512:(hh + 1) * 512])))(),
                          reads=[("yacc", t)], writes=[("st", hh)])
                S.add("dve", (lambda t=t: (lambda e: e.bn_aggr(out=mv[:, t, :], in_=st[:].rearrange("p a b -> p (a b)"))))(),
                      reads=[("st", 0), ("st", 1)], writes=["mv"])
            S.act(rstd[:], mv[:, :, 1], AF.Sqrt, ["mv", "cvec"], ["rstd"], bias=eps_ln)
            S.add("dve", lambda e: e.reciprocal(out=rstd[:], in_=rstd[:]), reads=["rstd"], writes=["rstd"])
            S.stt("dve", nb[:], mv[:, :, 0], -1.0, rstd[:], ALU.mult, ALU.mult, ["mv", "rstd"], ["nb"])
            for t in range(NT):
                s = t % 2
                S.act(x1[:, s, :], yacc[:, t, :], AF.Identity, [("yacc", t), "rstd", "nb"], [("x1", s)],
                      scale=rstd[:, t:t + 1], bias=nb[:, t:t + 1])
                S.tt("dve", x1[:, s, :], x1[:, s, :], gb[:, 0, :], ALU.mult, [("x1", s), "gb0"], [("x1", s)])
                S.tt("pool", x1[:, s, :], x1[:, s, :], gb[:, 1, :], ALU.add, [("x1", s), "gb1"], [("x1", s)])
                if last:
                    S.dma("sp", out_d[t * 128:(t + 1) * 128, :], x1[:, s, :], [("x1", s)], [("out", t)], ("out", s))
                else:
                    S.add("act", (lambda s=s, t=t: (lambda e: e.mul(out=yacc[:, t, :], in_=x1[:, s, :], mul=ALPHA)))(),
                          reads=[("x1", s)], writes=[("yacc", t)])
                    S.copy("pool", x1b[:, s, :], x1[:, s, :], [("x1", s)], [("x1b", s)])
                    pt = PA[:].bitcast(BF16) if s == 0 else PB[:].bitcast(BF16)
                    S.transposes([(pt[:, c * 128:(c + 1) * 128], x1b[:, s, c * 128:(c + 1) * 128]) for c in range(8)], ident[:],
                                 reads=[("x1b", s), "ident"], writes=[("P", s, 0)])
                    S.copy("dve", xT[:, :, t * 128:(t + 1) * 128],
                           pt[:, 0:1024].rearrange("p (c k) -> p c k", k=128), [("P", s, 0)], [("xT", t)])
            S.barrier()

    def proj_fm(slot, wkc, rhs_fn, j, out_ps, pkey, rkeys, wkeys, c0=0):
        specs = []
        for kc in range(wkc):
            specs.append((out_ps, wslice(slot, kc, c0), rhs_fn(kc, j), kc == 0, kc == wkc - 1))
        S.mms(specs, reads=list(rkeys) + list(wkeys), writes=[pkey])

    def xT_rhs(kc, j):
        return xT[:, kc, j * 512:(j + 1) * 512]

    def yacc_accumulate(cat, nslots, catkey_fn, wo_keys, t):
        specs = []
        for half in range(2):
            for s in range(nslots):
                specs.append((PA[:, half * 512:(half + 1) * 512], cat[:, s, t * 128:(t + 1) * 128],
                              wo[:, s, half * 512:(half + 1) * 512], s == 0, s == nslots - 1))
        S.mms(specs, reads=[catkey_fn(t)] + list(wo_keys), writes=[("P", 0, 0), ("P", 0, 1)])
        S.tt("dve", yacc[:, t, :], yacc[:, t, :], PA[:], ALU.add, [("P", 0, 0), ("P", 0, 1), ("yacc", t)], [("yacc", t)])

    def attention_core(name, qk_list, bias_fn, Vfn, dv, scale, on_out, on_key, ebuf, rs_t):
        ei = [0]
        for jg in range(4):
            nkt = 4 * jg + 4
            for kt in range(nkt):
                c0 = max(0, kt - 4 * jg)
                q_lo = jg * 512 + c0 * 128
                q_hi = (jg + 1) * 512
                half = kt % 2
                ps = PC[:, half * 512 + c0 * 128: (half + 1) * 512]
                specs = []
                rk = []
                items = list(qk_list)
                b = bias_fn(kt, jg, q_lo, q_hi) if bias_fn is not None else None
                if b is not None:
                    items = items + [b]
                for i, (kf, qf, keys) in enumerate(items):
                    specs.append((ps, kf(kt), qf(q_lo, q_hi), i == 0, i == len(items) - 1))
                    rk += keys
                S.mms(specs, reads=rk, writes=[("P", 2, half)])
                es_ = ei[0] % 3
                ei[0] += 1
                eb = ebuf[:, es_, c0 * 128:512]
                S.act(eb, ps, AF.Exp, [("P", 2, half)], [(name + "e", es_)], scale=scale)
                if kt >= 4 * jg:
                    S.tt("dve", ebuf[:, es_, c0 * 128:(c0 + 1) * 128], ebuf[:, es_, c0 * 128:(c0 + 1) * 128], maskT[:], ALU.mult,
                         [(name + "e", es_), "maskT"], [(name + "e", es_)])
                pv = []
                for qi in range(c0, 4):
                    o_ps = PD[:, (qi // 2) * 512 + (qi % 2) * (dv + 1): (qi // 2) * 512 + (qi % 2) * (dv + 1) + dv + 1]
                    pv.append((o_ps, ebuf[:, es_, qi * 128:(qi + 1) * 128], Vfn(kt), kt == 0 and qi % 2 == 0, kt == 4 * jg + qi and qi % 2 == 1))
                S.mms(pv, reads=[(name + "e", es_), name + "V"], writes=[("P", 3, 0), ("P", 3, 1)])
            for qi in range(4):
                base = (qi // 2) * 512 + (qi % 2) * (dv + 1)
                t = jg * 4 + qi
                S.add("dve", (lambda base=base, qi=qi: (lambda e: e.reciprocal(out=rs_t[:, qi:qi + 1], in_=PD[:, base + dv: base + dv + 1])))(),
                      reads=[("P", 3, 0), ("P", 3, 1)], writes=[(name + "rs", qi)])
                S.act(on_out(t), PD[:, base: base + dv], AF.Copy, [("P", 3, 0), ("P", 3, 1), (name + "rs", qi)], [on_key(t)],
                      scale=rs_t[:, qi:qi + 1])

    with ExitStack() as esR:
        tabR = sb(esR, "tabR", [128, 2, S_LEN], F32)
        qT = sb(esR, "qT", [128, 2, S_LEN], BF16)
        kT = sb(esR, "kT", [128, 2, S_LEN], BF16)
        vt = sb(esR, "vt", [128, NT, 512], BF16)
        gT = sb(esR, "gT", [128, 4, S_LEN], BF16)
        st32 = sb(esR, "st32", [128, 2, 512], F32)
        stb = sb(esR, "stb", [128, 2, 512], BF16)
        ktok = sb(esR, "ktok", [128, 2, 256], BF16)
        innT = sb(esR, "innT", [128, 2, 128], BF16)
        onb = sb(esR, "onb", [128, 2, 512], BF16)
        catc = sb(esR, "catc", [128, 2, 4, 128], BF16)
        bst = sb(esR, "bst", [128, 6], F32)
        bmv = sb(esR, "bmv", [128, 2], F32)
        sm = sb(esR, "sm", [128, 4], F32)

        gen_tables(tabR, inv_r, None)
        TAB = [("tab", j) for j in range(4)]

        for h in range(4):
            g = 1.0 - 2.0 ** (-5.0 - h)
            gC = g ** 128
            xi_h = cvec[:, 8 + h:9 + h]
            vs_h = cvec[:, 12 + h:13 + h]
            for qi, dst in enumerate((qT, kT)):
                dname = "qT" if qi == 0 else "kT"
                sE, kE = load_w(w_ret[h, :, qi * 256: qi * 256 + 128], 128)
                sO, kO = load_w(w_ret[h, :, qi * 256 + 128: qi * 256 + 256], 128)
                for j in range(4):
                    hf = j % 2
                    psA = PA[:, hf * 512:(hf + 1) * 512]
                    psB = PB[:, hf * 512:(hf + 1) * 512]
                    proj_fm(sE, 8, xT_rhs, j, psA, ("P", 0, hf), xt_keys(j), kE)
                    proj_fm(sO, 8, xT_rhs, j, psB, ("P", 1, hf), xt_keys(j), kO)
                    cs = slice(j * 512, (j + 1) * 512)
                    cosj = tabR[:, 0, cs]
                    sinj = tabR[:, 1, cs]
                    S.tt("dve", tmpA[:], psA, cosj, ALU.mult, [("P", 0, hf), ("tab", j)], ["tmpA"])
                    S.tt("dve", tmpB[:], psB, sinj, ALU.mult, [("P", 1, hf), ("tab", j)], ["tmpB"])
                    S.tt("dve", dst[:, 0, cs], tmpA[:], tmpB[:], ALU.subtract, ["tmpA", "tmpB"], [(dname, 0, j)])
                    S.tt("dve", tmpA[:], psB, cosj, ALU.mult, [("P", 1, hf), ("tab", j)], ["tmpA"])
                    S.tt("dve", tmpB[:], psA, sinj, ALU.mult, [("P", 0, hf), ("tab", j)], ["tmpB"])
                    S.tt("dve", dst[:, 1, cs], tmpA[:], tmpB[:], ALU.add, ["tmpA", "tmpB"], [(dname, 1, j)])
            sV, kV = load_w(w_ret[h, :, 512:1024], 512)
            for t in range(NT):
                hf = t % 2
                ps = PA[:, hf * 512:(hf + 1) * 512]
                S.mms([(ps, xT[:, kc, t * 128:(t + 1) * 128], wring[:, kc, sV * 128:(sV + 4) * 128], kc == 0, kc == 7) for kc in range(8)],
                      reads=[("xT", t)] + kV, writes=[("P", 0, hf)])
                S.act(vt[:, t, :], ps, AF.Copy, [("P", 0, hf), "cvec"], [("vt", t)], scale=vs_h)
            sG, kG = load_w(w_ret[h, :, 1024:1536], 512)
            for c in range(4):
                for j in range(4):
                    hf = j % 2
                    ps = PB[:, hf * 512:(hf + 1) * 512]
                    proj_fm(sG, 8, xT_rhs, j, ps, ("P", 1, hf), xt_keys(j), kG, c0=c * 128)
                    S.act(gT[:, c, j * 512:(j + 1) * 512], ps, AF.Silu, [("P", 1, hf)], [("gT", c, j)])
            S.dma("pool", wo[:], w_oe[h * 512:(h + 1) * 512, :].rearrange("(c p) n -> p c n", p=128), [], ["wo"], "wo")
            PDb = PD[:].bitcast(BF16)
            for n in range(NT):
                j = n // 4
                cs = slice(n * 128, (n + 1) * 128)
                d2 = n % 2
                qk = [("qT", 0, j), ("qT", 1, j)]
                kk = [("kT", 0, j), ("kT", 1, j)]
                st_ps = PC[:, d2 * 128:(d2 + 1) * 128]
                S.mms([(st_ps, kT[:, c, cs], qT[:, c, cs], c == 0, c == 1) for c in range(2)], reads=qk + kk, writes=[("P", 2, 0, d2)])
                S.tt("dve", innT[:, d2, :], st_ps, maskT[:], ALU.mult, [("P", 2, 0, d2), "maskT"], [("innT", d2)])
                o_ps = PC[:, 512:1024]
                specs = [(o_ps, innT[:, d2, :], vt[:, n, :], True, n == 0)]
                rk = [("innT", d2), ("vt", n)]
                if n > 0:
                    for c in range(2):
                        specs.append((o_ps, qT[:, c, cs], stb[:, c, :], False, c == 1))
                    rk += qk + ["stb"]
                S.mms(specs, reads=rk, writes=[("P", 2, 1)])
                if n < NT - 1:
                    kt_ps = PDb[:, 0:256]
                    S.transposes([(kt_ps[:, c * 128:(c + 1) * 128], kT[:, c, cs]) for c in range(2)], ident[:],
                                 reads=kk + ["ident"], writes=[("P", 3, 0, "kt")])
                    S.act(ktok[:, d2, :], kt_ps, AF.Copy, [("P", 3, 0, "kt")], [("ktok", d2)], scale=gC)
                    S.mms([(PD[:, 512:1024], ktok[:, d2, 0:128], vt[:, n, :], True, True)], reads=[("ktok", d2), ("vt", n)], writes=[("P", 3, 1)])
                    S.mms([(PB[:, 0:512], ktok[:, d2, 128:256], vt[:, n, :], True, True)], reads=[("ktok", d2), ("vt", n)], writes=[("P", 1, 0)])
                    if n == 0:
                        S.copy("dve", st32[:, 0, :], PD[:, 512:1024], [("P", 3, 1)], [("st32", 0)])
                        S.copy("dve", st32[:, 1, :], PB[:, 0:512], [("P", 1, 0)], [("st32", 1)])
                    else:
                        S.stt("dve", st32[:, 0, :], st32[:, 0, :], gC, PD[:, 512:1024], ALU.mult, ALU.add, [("P", 3, 1), ("st32", 0)], [("st32", 0)])
                        S.stt("dve", st32[:, 1, :], st32[:, 1, :], gC, PB[:, 0:512], ALU.mult, ALU.add, [("P", 1, 0), ("st32", 1)], [("st32", 1)])
                    S.copy("pool", stb[:], st32[:], [("st32", 0), ("st32", 1)], ["stb"])
                S.add("dve", lambda e: e.bn_stats(out=bst[:], in_=PC[:, 512:1024]), reads=[("P", 2, 1)], writes=["bst"])
                S.add("dve", lambda e: e.bn_aggr(out=bmv[:], in_=bst[:]), reads=["bst"], writes=["bmv"])
                S.ts("dve", sm[:, 0:1], bmv[:, 1:2], xi2[:, h:h + 1], LN_EPS, ALU.mult, ALU.add, ["bmv", "xi2"], [("sm", 0)])
                S.act(sm[:, 1:2], sm[:, 0:1], AF.Sqrt, [("sm", 0)], [("sm", 1)])
                S.add("dve", lambda e: e.reciprocal(out=sm[:, 1:2], in_=sm[:, 1:2]), reads=[("sm", 1)], writes=[("sm", 1)])
                S.ts("dve", sm[:, 2:3], sm[:, 1:2], xi_h, None, ALU.mult, None, [("sm", 1), "cvec"], [("sm", 2)])
                S.stt("dve", sm[:, 3:4], bmv[:, 0:1], -1.0, sm[:, 2:3], ALU.mult, ALU.mult, ["bmv", ("sm", 2)], [("sm", 3)])
                S.act(onb[:, d2, :], PC[:, 512:1024], AF.Identity, [("P", 2, 1), ("sm", 2), ("sm", 3)], [("onb", d2)],
                      scale=sm[:, 2:3], bias=sm[:, 3:4])
                ct_ps = PDb[:, 512:1024]
                S.transposes([(ct_ps[:, c * 128:(c + 1) * 128], onb[:, d2, c * 128:(c + 1) * 128]) for c in range(4)], ident[:],
                             reads=[("onb", d2), "ident"], writes=[("P", 3, 0, "ct")])
                S.tt("dve", catc[:, d2, :, :], ct_ps.rearrange("p (c k) -> p c k", k=128), gT[:, :, cs], ALU.mult,
                     [("P", 3, 0, "ct")] + [("gT", c, j) for c in range(4)], [("catc", d2)])
                specs = []
                for half in range(2):
                    for c in range(4):
                        specs.append((PA[:, half * 512:(half + 1) * 512], catc[:, d2, c, :], wo[:, c, half * 512:(half + 1) * 512], c == 0, c == 3))
                S.mms(specs, reads=[("catc", d2), "wo"], writes=[("P", 0, 0), ("P", 0, 1)])
                S.tt("dve", yacc[:, n, :], yacc[:, n, :], PA[:], ALU.add, [("P", 0, 0), ("P", 0, 1), ("yacc", n)], [("yacc", n)])
        S.barrier()

    if debug_out == "ret":
        for t in range(NT):
            S.dma("sp", out_d[t * 128:(t + 1) * 128, :], yacc[:, t, :], [("yacc", t)], [("out", t)], "out")
        S.finalize(); top.close(); S.close()
        return nc

    with ExitStack() as esT:
        tab64 = sb(esT, "tab64", [128, 2, S_LEN], F32)
        gen_tables(tab64, inv64, sgn64)
        ebuf = sb(esT, "ebuf", [128, 3, 512], BF16)
        rs_t = sb(esT, "rs_t", [128, 4], F32)
        S.barrier()

        def rope_pair(psA, psB, kA, kB, j, out_ap, out_key, eng_out="dve"):
            cs = slice(j * 512, (j + 1) * 512)
            S.tt("dve", tmpA[:], psA, tab64[:, 0, cs], ALU.mult, [kA, ("tab", j)], ["tmpA"])
            S.tt("dve", tmpB[:], psB, tab64[:, 1, cs], ALU.mult, [kB, ("tab", j)], ["tmpB"])
            S.tt("dve", out_ap, tmpA[:], tmpB[:], ALU.add, ["tmpA", "tmpB"], [out_key])

        with ExitStack() as esM:
            mqT = sb(esM, "mqT", [128, 2, S_LEN], BF16)
            mkvT = sb(esM, "mkvT", [128, 2, S_LEN], BF16)
            kpeT = sb(esM, "kpeT", [128, S_LEN], BF16)
            qpeT = sb(esM, "qpeT", [128, S_LEN], BF16)
            qnT = sb(esM, "qnT", [128, S_LEN], BF16)
            knT = sb(esM, "knT", [128, S_LEN], BF16)
            Vaug = sb(esM, "Vaug", [128, NT, 129], BF16)
            gmT = sb(esM, "gmT", [128, S_LEN], BF16)
            catM = sb(esM, "catM", [128, 2, S_LEN], BF16)
            onM = sb(esM, "onM", [128, NT, 128], BF16)
            sqb = sb(esM, "sqb", [128, 2, 512], BF16)
            rawg = sb(esM, "rawg", [128, 2, 512], F32)
            S.add("pool", lambda e: e.memset(Vaug[:, :, 128:129], 1.0), writes=["VaugOnes"])

            for which, (dstT, gvec, col0) in enumerate(((mqT, qn, 0), (mkvT, kvn, 256))):
                dn = "mqT" if which == 0 else "mkvT"
                s0, k0 = load_w(w_mla[:, col0:col0 + 128], 128)
                s1, k1 = load_w(w_mla[:, col0 + 128:col0 + 256], 128)
                for j in range(4):
                    hf = j % 2
                    psl = [PA[:, hf * 512:(hf + 1) * 512], PB[:, hf * 512:(hf + 1) * 512]]
                    pk = [("P", 0, hf), ("P", 1, hf)]
                    proj_fm(s0, 8, xT_rhs, j, psl[0], pk[0], xt_keys(j), k0)
                    proj_fm(s1, 8, xT_rhs, j, psl[1], pk[1], xt_keys(j), k1)
                    for c in range(2):
                        S.act(sqb[:, c, :], psl[c], AF.Square, [pk[c]], [("sqb", c)])
                        S.act(rawg[:, c, :], psl[c], AF.Copy, [pk[c], "qn", "kvn"], [("rawg", c)], scale=gvec[:, c:c + 1])
                    ss_ps = PC[:, 0:512]
                    S.mms([(ss_ps, ones_b[:], sqb[:, c, :], c == 0, c == 1) for c in range(2)],
                          reads=[("sqb", 0), ("sqb", 1), "ones_b"], writes=[("P", 2, 0)])
                    S.act(tmpA[:], ss_ps, AF.Sqrt, [("P", 2, 0), "cvec"], ["tmpA"], scale=1.0 / 256.0, bias=eps_rms)
                    S.add("dve", lambda e: e.reciprocal(out=tmpA[:], in_=tmpA[:]), reads=["tmpA"], writes=["tmpA"])
                    for c in range(2):
                        S.tt("dve", dstT[:, c, j * 512:(j + 1) * 512], rawg[:, c, :], tmpA[:], ALU.mult,
                             [("rawg", c), "tmpA"], [(dn, j)])
            MQ = [("mqT", j) for j in range(4)]
            MKV = [("mkvT", j) for j in range(4)]
            sA, kA_ = load_w(w_mla[:, 512:640], 128)
            sB, kB_ = load_w(w_mla[:, 640:768], 128)
            for j in range(4):
                hf = j % 2
                psA = PA[:, hf * 512:(hf + 1) * 512]
                psB = PB[:, hf * 512:(hf + 1) * 512]
                proj_fm(sA, 8, xT_rhs, j, psA, ("P", 0, hf), xt_keys(j), kA_)
                proj_fm(sB, 8, xT_rhs, j, psB, ("P", 1, hf), xt_keys(j), kB_)
                rope_pair(psA, psB, ("P", 0, hf), ("P", 1, hf), j, kpeT[:, j * 512:(j + 1) * 512], ("kpeT", j))
            KPE = [("kpeT", j) for j in range(4)]

            def mq_rhs(kc, j):
                return mqT[:, kc, j * 512:(j + 1) * 512]

            def mkv_rhs(kc, j):
                return mkvT[:, kc, j * 512:(j + 1) * 512]

            for hp in range(4):
                sA, kA_ = load_w(w_uq[:, 1024 + hp * 128: 1024 + (hp + 1) * 128], 128)
                sB, kB_ = load_w(w_uq[:, 1536 + hp * 128: 1536 + (hp + 1) * 128], 128)
                for j in range(4):
                    hf = j % 2
                    psA = PA[:, hf * 512:(hf + 1) * 512]
                    psB = PB[:, hf * 512:(hf + 1) * 512]
                    proj_fm(sA, 2, mq_rhs, j, psA, ("P", 0, hf), [("mqT", j)], kA_)
                    proj_fm(sB, 2, mq_rhs, j, psB, ("P", 1, hf), [("mqT", j)], kB_)
                    rope_pair(psA, psB, ("P", 0, hf), ("P", 1, hf), j, qpeT[:, j * 512:(j + 1) * 512], ("qpeT", j))
                QPE = [("qpeT", j) for j in range(4)]
                S.dma("pool", wo[:, 0:2, :], w_oe[2048 + hp * 256: 2048 + (hp + 1) * 256, :].rearrange("(c p) n -> p c n", p=128),
                      [], ["wo"], "wo")
                for e2 in range(2):
                    h = hp * 2 + e2
                    r0 = 64 * e2
                    sQ, kQ = load_w(w_uq[:, h * 128:(h + 1) * 128], 128)
                    sK, kK = load_w(w_ukv[:, h * 128:(h + 1) * 128], 128)
                    sVv, kVv = load_w(w_ukv[:, 1024 + h * 128: 1024 + (h + 1) * 128], 128)
                    sG, kG = load_w(w_mla[:, 768 + h * 128: 768 + (h + 1) * 128], 128)
                    for j in range(4):
                        hf = j % 2
                        psA = PA[:, hf * 512:(hf + 1) * 512]
                        psB = PB[:, hf * 512:(hf + 1) * 512]
                        proj_fm(sQ, 2, mq_rhs, j, psA, ("P", 0, hf), [("mqT", j)], kQ)
                        S.copy("act", qnT[:, j * 512:(j + 1) * 512], psA, [("P", 0, hf)], [("qnT", j)])
                        proj_fm(sK, 2, mkv_rhs, j, psB, ("P", 1, hf), [("mkvT", j)], kK)
                        S.copy("act", knT[:, j * 512:(j + 1) * 512], psB, [("P", 1, hf)], [("knT", j)])
                    for j in range(4):
                        hf = j % 2
                        psA = PA[:, hf * 512:(hf + 1) * 512]
                        proj_fm(sG, 8, xT_rhs, j, psA, ("P", 0, hf), xt_keys(j), kG)
                        S.act(gmT[:, j * 512:(j + 1) * 512], psA, AF.Silu, [("P", 0, hf)], [("gmT", j)])
                    for t in range(NT):
                        hf = t % 2
                        ps = PB[:, hf * 512: hf * 512 + 128]
                        S.mms([(ps, mkvT[:, kc, t * 128:(t + 1) * 128], wslice(sVv, kc), kc == 0, kc == 1) for kc in range(2)],
                              reads=[("mkvT", t // 4)] + kVv, writes=[("P", 1, hf)])
                        S.copy("act", Vaug[:, t, 0:128], ps, [("P", 1, hf), "VaugOnes"], ["mV"])
                    QN = [("qnT", j) for j in range(4)]
                    KN = [("knT", j) for j in range(4)]
                    qk_list = [
                        (lambda kt: knT[:, kt * 128:(kt + 1) * 128], lambda lo, hi: qnT[:, lo:hi], QN + KN),
                        (lambda kt, r0=r0: kpeT[r0:r0 + 64, kt * 128:(kt + 1) * 128], lambda lo, hi, r0=r0: qpeT[r0:r0 + 64, lo:hi], QPE + KPE),
                    ]
                    attention_core("m", qk_list, None, lambda kt: Vaug[:, kt, :], 128, 192.0 ** -0.5,
                                   lambda t: onM[:, t, :], lambda t: ("onM", t), ebuf, rs_t)
                    for jg in range(4):
                        ct_ps = PB[:].bitcast(BF16)[:, 0:512]
                        S.transposes([(ct_ps[:, i * 128:(i + 1) * 128], onM[:, jg * 4 + i, :]) for i in range(4)], ident[:],
                                     reads=[("onM", jg * 4 + i) for i in range(4)] + ["ident"], writes=[("P", 1, 0)])
                        S.tt("dve", catM[:, e2, jg * 512:(jg + 1) * 512], ct_ps, gmT[:, jg * 512:(jg + 1) * 512], ALU.mult,
                             [("P", 1, 0), ("gmT", jg)], [("catM", e2, jg)])
                for t in range(NT):
                    yacc_accumulate(catM, 2, lambda t: ("catM", 0, t // 4), ["wo"] + [("catM", 1, j) for j in range(4)], t)
            S.barrier()

        layer_norm(0, last=(n_layers == 1))
        if n_layers == 1:
            S.finalize(); esT.close(); top.close(); S.close()
            return nc

        with ExitStack() as esL:
            qTl = sb(esL, "qTl", [128, S_LEN], BF16)
            kmh = sb(esL, "kmh", [128, 8], BF16)
            kml = sb(esL, "kml", [128, 8], BF16)
            kmr = sb(esL, "kmr", [128, 8], F32)
            qTb = sb(esL, "qTb", [128, S_LEN], BF16)
            kTb = sb(esL, "kTb", [128, S_LEN], BF16)
            kmean = sb(esL, "kmean", [128, 8], F32)
            Vp = sb(esL, "Vp", [128, NT, 2, 65], BF16)
            gmT = sb(esL, "gmT1", [128, S_LEN], BF16)
            nselT = sb(esL, "nselT", [8, 2, S_LEN], BF16)
            catL = sb(esL, "catL", [128, 2, S_LEN], BF16)
            onL = sb(esL, "onL", [128, NT, 128], BF16)
            gm = sb(esL, "gm", [128, 128], F32)
            sel = sb(esL, "sel", [128, 128], F32)
            top8 = sb(esL, "top8", [128, 8], F32)
            biasb = sb(esL, "biasb", [128, 128], BF16)
            S.add("pool", lambda e: e.memset(Vp[:, :, :, 64:65], 1.0), writes=["VpOnes"])

            for hp in range(8):
                sl = hp % 2
                for qi in range(2):
                    base = qi * 2048
                    sA, kA_ = load_w(w_l1[:, base + hp * 128: base + (hp + 1) * 128], 128)
                    sB, kB_ = load_w(w_l1[:, base + 1024 + hp * 128: base + 1024 + (hp + 1) * 128], 128)
                    for j in range(4):
                        hf = j % 2
                        psA = PA[:, hf * 512:(hf + 1) * 512]
                        psB = PB[:, hf * 512:(hf + 1) * 512]
                        proj_fm(sA, 8, xT_rhs, j, psA, ("P", 0, hf), xt_keys(j), kA_)
                        proj_fm(sB, 8, xT_rhs, j, psB, ("P", 1, hf), xt_keys(j), kB_)
                        cs = slice(j * 512, (j + 1) * 512)
                        if qi == 0:
                            S.tt("dve", tmpA[:], psA, tab64[:, 0, cs], ALU.mult, [("P", 0, hf), ("tab", j)], ["tmpA"])
                            S.tt("dve", tmpB[:], psB, tab64[:, 1, cs], ALU.mult, [("P", 1, hf), ("tab", j)], ["tmpB"])
                            S.tt("dve", tmpA[:], tmpA[:], tmpB[:], ALU.add, ["tmpA", "tmpB"], ["tmpA"])
                            S.copy("act", qTb[:, cs], tmpA[:], ["tmpA"], [("qTb", j)])
                            S.tt("dve", qTl[:, cs], tmpA[:], qTb[:, cs], ALU.subtract, ["tmpA", ("qTb", j)], [("qTl", j)])
                        else:
                            S.tt("dve", tmpA[:], psA, tab64[:, 0, cs], ALU.mult, [("P", 0, hf), ("tab", j)], ["tmpA"])
                            S.tt("dve", tmpB[:], psB, tab64[:, 1, cs], ALU.mult, [("P", 1, hf), ("tab", j)], ["tmpB"])
                            S.tt("dve", tmpA[:], tmpA[:], tmpB[:], ALU.add, ["tmpA", "tmpB"], ["tmpA"])
                            S.copy("act", kTb[:, cs], tmpA[:], ["tmpA"], [("kTb", j)])
                            S.add("dve", (lambda j=j: (lambda e: e.tensor_reduce(out=kmean[:, 2 * j:2 * j + 2],
                                                                                 in_=tmpA[:].rearrange("p (b l) -> p b l", l=256),
                                                                                 axis=AX.X, op=ALU.add)))(),
                                  reads=["tmpA"], writes=["kmean"])
                QB = [("qTb", j) for j in range(4)]
                QL = [("qTl", j) for j in range(4)]
                S.copy("dve", kmh[:], kmean[:], ["kmean"], ["kmh"])
                S.tt("dve", kmr[:], kmean[:], kmh[:], ALU.subtract, ["kmean", "kmh"], ["kmr"])
                S.copy("dve", kml[:], kmr[:], ["kmr"], ["kml"])
                KB = [("kTb", j) for j in range(4)]
                sVv, kVv = load_w(w_l1[:, 4096 + hp * 128: 4096 + (hp + 1) * 128], 128)
                sG, kG = load_w(w_l1[:, 5120 + hp * 128: 5120 + (hp + 1) * 128], 128)
                for t in range(NT):
                    hf = t % 2
                    ps = PB[:, hf * 512: hf * 512 + 128]
                    S.mms([(ps, xT[:, kc, t * 128:(t + 1) * 128], wslice(sVv, kc), kc == 0, kc == 7) for kc in range(8)],
                          reads=[("xT", t)] + kVv, writes=[("P", 1, hf)])
                    S.copy("act", Vp[:, t, :, 0:64], ps.rearrange("p (a b) -> p a b", b=64), [("P", 1, hf), "VpOnes"], ["lV"])
                for j in range(4):
                    hf = j % 2
                    psA = PA[:, hf * 512:(hf + 1) * 512]
                    proj_fm(sG, 8, xT_rhs, j, psA, ("P", 0, hf), xt_keys(j), kG)
                    S.act(gmT[:, j * 512:(j + 1) * 512], psA, AF.Silu, [("P", 0, hf)], [("gmT", j)])
                if hp % 2 == 0:
                    S.dma("pool", wo[:, 0:2, :], w_oo[hp * 128:(hp + 2) * 128, :].rearrange("(c p) n -> p c n", p=128), [], ["wo"], "wo")
                for e2 in range(2):
                    r0 = 64 * e2
                    g_ps = PC[:, 0:128]
                    gspecs = []
                    for t in range(NT):
                        ts_ = slice(t * 128, (t + 1) * 128)
                        gspecs.append((g_ps[:, t * 8:(t + 1) * 8], qTb[r0:r0 + 64, ts_], kmh[r0:r0 + 64, :], True, False))
                        gspecs.append((g_ps[:, t * 8:(t + 1) * 8], qTl[r0:r0 + 64, ts_], kmh[r0:r0 + 64, :], False, False))
                        gspecs.append((g_ps[:, t * 8:(t + 1) * 8], qTb[r0:r0 + 64, ts_], kml[r0:r0 + 64, :], False, True))
                    S.mms(gspecs, reads=QB + QL + ["kmh", "kml"], writes=[("P", 2, 0)])
                    S.tt("dve", gm[:], g_ps, past01[:], ALU.mult, [("P", 2, 0), "past01"], ["gm"])
                    S.tt("dve", gm[:], gm[:], negoff[:], ALU.add, ["gm", "negoff"], ["gm"])
                    for t in range(NT):
                        S.add("dve", (lambda t=t: (lambda e: e.max(out=top8[:], in_=gm[:, t * 8:(t + 1) * 8])))(), reads=["gm"], writes=["top8"])
                        S.ts("dve", sel[:, t * 8:(t + 1) * 8], gm[:, t * 8:(t + 1) * 8], top8[:, 2:3], None, ALU.is_ge, None, ["gm", "top8"], ["sel"])
                    S.tt("dve", sel[:], sel[:], past01[:], ALU.mult, ["sel", "past01"], ["sel"])
                    S.tt("dve", sel[:], sel[:], own01[:], ALU.add, ["sel", "own01"], ["sel"])
                    S.ts("dve", biasb[:], sel[:], NEGB, -NEGB, ALU.mult, ALU.add, ["sel"], ["biasb"])
                    bt_ps = PC[:].bitcast(BF16)[0:8, 1024:2048]
                    for q4 in range(4):
                        S.transposes([(bt_ps[:, i * 128:(i + 1) * 128], biasb[:, (q4 * 4 + i) * 8:(q4 * 4 + i + 1) * 8]) for i in range(4)], ident[:],
                                     reads=["biasb", "ident"], writes=[("P", 2, 1)])
                        S.copy("act", nselT[:, e2, q4 * 512:(q4 + 1) * 512], bt_ps[:, 0:512], [("P", 2, 1)], [("nselT", e2, q4)])
                    NS = [("nselT", e2, j) for j in range(4)]

                    def bias_fn(kt, jg, lo, hi, e2=e2, NS=NS):
                        nb_ = kt // 2
                        if nb_ == 2 * jg + 1:
                            return None
                        return (lambda kt_, nb_=nb_: Eall[0:8, nb_ * 128:(nb_ + 1) * 128],
                                lambda lo_, hi_, e2=e2: nselT[0:8, e2, lo_:hi_], NS + ["Eall"])

                    qk_list = [(lambda kt, r0=r0: kTb[r0:r0 + 64, kt * 128:(kt + 1) * 128],
                                lambda lo, hi, r0=r0: qTb[r0:r0 + 64, lo:hi], QB + KB)]
                    attention_core("l", qk_list, bias_fn, (lambda kt, e2=e2: Vp[:, kt, e2, :]), 64, 0.125,
                                   (lambda t, e2=e2: onL[:, t, e2 * 64:(e2 + 1) * 64]), (lambda t, e2=e2: ("onL", t, e2)), ebuf, rs_t)
                for jg in range(4):
                    ct_ps = PB[:].bitcast(BF16)[:, 0:512]
                    S.transposes([(ct_ps[:, i * 128:(i + 1) * 128], onL[:, jg * 4 + i, :]) for i in range(4)], ident[:],
                                 reads=[("onL", jg * 4 + i, e) for i in range(4) for e in range(2)] + ["ident"], writes=[("P", 1, 0)])
                    S.tt("dve", catL[:, sl, jg * 512:(jg + 1) * 512], ct_ps, gmT[:, jg * 512:(jg + 1) * 512], ALU.mult,
                         [("P", 1, 0), ("gmT", jg)], [("catL", sl, jg)])
                if hp % 2 == 1:
                    for t in range(NT):
                        yacc_accumulate(catL, 2, lambda t: ("catL", 0, t // 4), ["wo"] + [("catL", 1, j) for j in range(4)], t)
            S.barrier()
        layer_norm(1, last=True)

    S.finalize()
    top.close()
    S.close()
    return nc


def _host_constants():
    c = {}
    c["c_ident"] = np.eye(128, dtype=np.float32)
    k = np.arange(128)
    c["c_mask"] = (k[None, :] >= k[:, None]).astype(np.float32)
    vec = np.zeros((128, 16), np.float32)
    vec[:, 0] = 1.0 / (10000.0 ** np.linspace(0.0, 1.0, 128, dtype=np.float32))
    inv64 = 1.0 / (10000.0 ** (np.arange(0, 64, 2, dtype=np.float32) / 64.0))
    vec[:, 1] = inv64[np.arange(128) % 32]
    vec[:, 2] = np.where((np.arange(128) % 64) < 32, -1.0, 1.0)
    vec[:, 3] = LN_EPS
    vec[:, 4] = RMS_EPS
    xi2 = np.zeros((128, 4), np.float32)
    for h in range(4):
        g = 1.0 - 2.0 ** (-5.0 - h)
        xi = (g ** (k + 1.0)) * (256.0 ** -0.5)
        vec[:, 8 + h] = xi
        vec[:, 12 + h] = g ** (-(k + 1.0))
        xi2[:, h] = xi * xi
    c["c_vec"] = vec
    c["c_xi2"] = xi2
    E = np.zeros((8, 8, 128), np.float32)
    for n in range(8):
        E[n, n, :] = 1.0
    c["c_E"] = E.reshape(8, 1024)
    past = np.zeros((16, 8), np.float32)
    own = np.zeros((16, 8), np.float32)
    for t in range(16):
        qb = t // 2
        past[t, :qb] = 1.0
        own[t, qb] = 1.0
    c["c_past"] = np.broadcast_to(past.reshape(1, 128), (128, 128)).copy()
    c["c_own"] = np.broadcast_to(own.reshape(1, 128), (128, 128)).copy()
    c["c_negoff"] = ((c["c_past"] - 1.0) * 1e30).astype(np.float32)
    return c


def _layout_weights(w_in_even, q_norm_even, w_uq_even, kv_norm_even, w_ukv_even, w_out_even, w_in_odd, w_out_odd):
    wi = np.asarray(w_in_even[0])
    o = {}
    ev = np.arange(0, 256, 2)
    od = np.arange(1, 256, 2)
    w_ret = np.empty((4, 1024, 1536), np.float32)
    for h in range(4):
        rq = wi[:, h * 256:(h + 1) * 256]
        rk = wi[:, 1024 + h * 256: 1024 + (h + 1) * 256]
        w_ret[h, :, 0:128] = rq[:, ev]
        w_ret[h, :, 128:256] = rq[:, od]
        w_ret[h, :, 256:384] = rk[:, ev]
        w_ret[h, :, 384:512] = rk[:, od]
        w_ret[h, :, 512:1024] = wi[:, 2048 + h * 512: 2048 + (h + 1) * 512]
        w_ret[h, :, 1024:1536] = wi[:, 4096 + h * 512: 4096 + (h + 1) * 512]
    o["w_ret"] = w_ret
    b = 6144
    mq = wi[:, b:b + 256]
    mkv = wi[:, b + 256:b + 512]
    mkr = wi[:, b + 512:b + 576]
    mg = wi[:, b + 576:b + 1600]
    swap64 = np.concatenate([np.arange(32, 64), np.arange(0, 32)])
    o["w_mla"] = np.ascontiguousarray(np.concatenate([mq, mkv, mkr, mkr, mkr[:, swap64], mkr[:, swap64], mg], axis=1))
    wuq = np.asarray(w_uq_even[0])
    nope = np.concatenate([wuq[:, h * 192: h * 192 + 128] for h in range(8)], axis=1)
    pe1 = np.concatenate([wuq[:, h * 192 + 128: h * 192 + 192] for h in range(8)], axis=1)
    pe2 = np.concatenate([wuq[:, h * 192 + 128: h * 192 + 192][:, swap64] for h in range(8)], axis=1)
    o["w_uq"] = np.ascontiguousarray(np.concatenate([nope, pe1, pe2], axis=1))
    wukv = np.asarray(w_ukv_even[0])
    kn = np.concatenate([wukv[:, h * 256: h * 256 + 128] for h in range(8)], axis=1)
    vv = np.concatenate([wukv[:, h * 256 + 128: h * 256 + 256] for h in range(8)], axis=1)
    o["w_ukv"] = np.ascontiguousarray(np.concatenate([kn, vv], axis=1))
    o["w_oe"] = np.ascontiguousarray(np.asarray(w_out_even[0]))
    wo_ = np.asarray(w_in_odd[0])
    sw = np.concatenate([h * 64 + swap64 for h in range(16)])
    q = wo_[:, 0:1024]
    kk = wo_[:, 1024:2048]
    o["w_l1"] = np.ascontiguousarray(np.concatenate([q, q[:, sw], kk, kk[:, sw], wo_[:, 2048:3072], wo_[:, 3072:4096]], axis=1))
    o["w_oo"] = np.ascontiguousarray(np.asarray(w_out_odd[0]))
    o["qn"] = np.ascontiguousarray(np.asarray(q_norm_even[0]).reshape(2, 128).T)
    o["kvn"] = np.ascontiguousarray(np.asarray(kv_norm_even[0]).reshape(2, 128).T)
    return o


def make_in_maps(x, positions, w_in_even, q_norm_even, w_uq_even, kv_norm_even, w_ukv_even,
                 w_out_even, w_in_odd, w_out_odd, ln_g, ln_b):
    shared = _host_constants()
    shared.update(_layout_weights(w_in_even, q_norm_even, w_uq_even, kv_norm_even, w_ukv_even, w_out_even, w_in_odd, w_out_odd))
    shared["lng"] = np.ascontiguousarray(np.asarray(ln_g, dtype=np.float32))
    shared["lnb"] = np.ascontiguousarray(np.asarray(ln_b, dtype=np.float32))
    x = np.asarray(x)
    positions = np.asarray(positions)
    in_maps = []
    for b in range(8):
        m = dict(shared)
        m["x"] = np.ascontiguousarray(x[b])
        m["pos"] = np.ascontiguousarray(positions[b].astype(np.int32).reshape(1, S_LEN))
        in_maps.append(m)
    return in_maps


def kernel(x, positions, w_in_even, q_norm_even, w_uq_even, kv_norm_even, w_ukv_even,
           w_out_even, w_in_odd, w_out_odd, ln_g, ln_b):
    in_maps = make_in_maps(x, positions, w_in_even, q_norm_even, w_uq_even, kv_norm_even, w_ukv_even,
                           w_out_even, w_in_odd, w_out_odd, ln_g, ln_b)
    nc = build_program(2)
    res = run_bass_kernel_spmd(nc, in_maps, core_ids=list(range(8)))
    out = np.stack([np.asarray(r["out"]) for r in res.results], axis=0).astype(np.float32)
    if os.environ.get("K_DUMP"):
        np.save(os.environ["K_DUMP"], out)
    return out
```

```python
import math
import os
from contextlib import ExitStack

import numpy as np
import concourse.bass as bass
import concourse.mybir as mybir
from concourse.bass_utils import run_bass_kernel_spmd

F32 = mybir.dt.float32
BF16 = mybir.dt.bfloat16
I32 = mybir.dt.int32
ALU = mybir.AluOpType
AF = mybir.ActivationFunctionType
AX = mybir.AxisListType

S_LEN = 2048
D = 1024
NT = 16
ALPHA = 4.0 ** 0.25
LN_EPS = 1e-5
RMS_EPS = 1e-6
NEGB = 30000.0


class Sched:
    ENGS = ("pe", "act", "dve", "pool", "sp")

    def __init__(self, nc):
        self.nc = nc
        self.ops = []
        self.last_w = {}
        self.readers = {}
        self.eng_sems = {}
        self.dma_sems = {}
        self.dma_cnt = {}
        self._stack = []
        self.barrier_deps = set()
        self.last_on_eng = {}
        self.last_dma = {}
        for e in ("pe", "act", "dve", "pool"):
            self.eng_sems[e] = self.sem("s_" + e)

    def sem(self, name):
        cm = self.nc.semaphore(name)
        s = cm.__enter__()
        self._stack.append(cm)
        return s

    def add(self, eng, fn, reads=(), writes=(), dma=None):
        idx = len(self.ops)
        deps = set(self.barrier_deps)
        for k in reads:
            w = self.last_w.get(k)
            if w is not None:
                deps.add(w)
        for k in writes:
            w = self.last_w.get(k)
            if w is not None:
                deps.add(w)
            for r in self.readers.get(k, ()):
                deps.add(r)
        op = dict(eng=eng, fn=fn, deps=deps, dma=dma, idx=idx, signal=False, tok=None)
        self.ops.append(op)
        for k in reads:
            self.readers.setdefault(k, []).append(idx)
        for k in writes:
            self.last_w[k] = idx
            self.readers[k] = []
        if dma is not None:
            self.last_dma[dma] = idx
        else:
            self.last_on_eng[eng] = idx
        return idx

    def barrier(self):
        self.barrier_deps = set(self.last_on_eng.values()) | set(self.last_dma.values())

    def finalize(self):
        ops = self.ops
        for op in ops:
            for d in op["deps"]:
                p = ops[d]
                if p["dma"] is not None:
                    continue
                if p["eng"] == "pe" and op["eng"] == "pe" and op["dma"] is None:
                    continue
                p["signal"] = True
        cnt = {e: 0 for e in self.eng_sems}
        for op in ops:
            if op["dma"] is not None:
                key = op["dma"]
                if key not in self.dma_sems:
                    self.dma_sems[key] = self.sem("d_%d" % len(self.dma_sems))
                    self.dma_cnt[key] = 0
                self.dma_cnt[key] += 16
                op["tok"] = (self.dma_sems[key], self.dma_cnt[key])
            elif op["signal"]:
                cnt[op["eng"]] += 1
                op["tok"] = (self.eng_sems[op["eng"]], cnt[op["eng"]])
        streams = {e: [] for e in self.ENGS}
        for op in ops:
            streams[op["eng"]].append(op)
        sched = self

        def emit(engname, engine):
            known = {}
            for op in streams[engname]:
                need = {}
                for d in op["deps"]:
                    p = ops[d]
                    if p["tok"] is None:
                        continue
                    if p["dma"] is None and p["eng"] == "pe" and engname == "pe" and op["dma"] is None:
                        continue
                    s, v = p["tok"]
                    if need.get(id(s), (None, 0))[1] < v:
                        need[id(s)] = (s, v)
                for sid, (s, v) in need.items():
                    if known.get(sid, 0) >= v:
                        continue
                    engine.wait_ge(s, v)
                    known[sid] = v
                ins = op["fn"](engine)
                if op["dma"] is not None:
                    ins.then_inc(op["tok"][0], 16)
                elif op["signal"]:
                    ins.then_inc(op["tok"][0], 1)
            if engname == "sp":
                for key, s in sched.dma_sems.items():
                    engine.wait_ge(s, sched.dma_cnt[key])
                for e, s in sched.eng_sems.items():
                    if cnt[e] > 0:
                        engine.wait_ge(s, cnt[e])

        with self.nc.Block() as block:
            @block.tensor
            def _(e):
                emit("pe", e)

            @block.scalar
            def _(e):
                emit("act", e)

            @block.vector
            def _(e):
                emit("dve", e)

            @block.gpsimd
            def _(e):
                emit("pool", e)

            @block.sync
            def _(e):
                emit("sp", e)

    def close(self):
        while self._stack:
            self._stack.pop().__exit__(None, None, None)

    def dma(self, q, out, in_, reads, writes, key):
        return self.add(q, lambda e: e.dma_start(out=out, in_=in_), reads=reads, writes=writes, dma=key)

    def mms(self, specs, reads, writes):
        def fn(e):
            ins = None
            for (o, l, r, st, sp) in specs:
                ins = e.matmul(o, lhsT=l, rhs=r, start=st, stop=sp)
            return ins
        return self.add("pe", fn, reads=reads, writes=writes)

    def transposes(self, specs, ident, reads, writes):
        def fn(e):
            ins = None
            for (o, i) in specs:
                ins = e.transpose(out=o, in_=i, identity=ident)
            return ins
        return self.add("pe", fn, reads=reads, writes=writes)

    def act(self, out, in_, func, reads, writes, scale=1.0, bias=None, eng="act"):
        if bias is None:
            return self.add(eng, lambda e: e.activation(out=out, in_=in_, func=func, scale=scale), reads=reads, writes=writes)
        return self.add(eng, lambda e: e.activation(out=out, in_=in_, func=func, scale=scale, bias=bias), reads=reads, writes=writes)

    def tt(self, eng, out, in0, in1, op, reads, writes):
        return self.add(eng, lambda e: e.tensor_tensor(out=out, in0=in0, in1=in1, op=op), reads=reads, writes=writes)

    def ts(self, eng, out, in0, s1, s2, op0, op1, reads, writes):
        if s2 is None:
            return self.add(eng, lambda e: e.tensor_scalar(out=out, in0=in0, scalar1=s1, scalar2=None, op0=op0), reads=reads, writes=writes)
        return self.add(eng, lambda e: e.tensor_scalar(out=out, in0=in0, scalar1=s1, scalar2=s2, op0=op0, op1=op1), reads=reads, writes=writes)

    def stt(self, eng, out, in0, scalar, in1, op0, op1, reads, writes):
        return self.add(eng, lambda e: e.scalar_tensor_tensor(out=out, in0=in0, scalar=scalar, in1=in1, op0=op0, op1=op1), reads=reads, writes=writes)

    def copy(self, eng, out, in_, reads, writes):
        if eng == "act":
            return self.add(eng, lambda e: e.copy(out=out, in_=in_), reads=reads, writes=writes)
        return self.add(eng, lambda e: e.tensor_copy(out=out, in_=in_), reads=reads, writes=writes)


def build_program(n_layers=2, debug_out=None):
    nc = bass.Bass("TRN2", target_bir_lowering=False)

    def dram_in(name, shape, dt=F32):
        return nc.dram_tensor(name, list(shape), dt, kind="ExternalInput").ap()

    x_d = dram_in("x", [S_LEN, D])
    pos_d = dram_in("pos", [1, S_LEN], I32)
    w_ret = dram_in("w_ret", [4, D, 1536])
    w_mla = dram_in("w_mla", [D, 1792])
    w_uq = dram_in("w_uq", [256, 2048])
    w_ukv = dram_in("w_ukv", [256, 2048])
    w_oe = dram_in("w_oe", [3072, D])
    w_l1 = dram_in("w_l1", [D, 6144])
    w_oo = dram_in("w_oo", [D, D])
    qn_d = dram_in("qn", [128, 2])
    kvn_d = dram_in("kvn", [128, 2])
    lng_d = dram_in("lng", [2, D])
    lnb_d = dram_in("lnb", [2, D])
    c_ident = dram_in("c_ident", [128, 128])
    c_mask = dram_in("c_mask", [128, 128])
    c_vec = dram_in("c_vec", [128, 16])
    c_xi2 = dram_in("c_xi2", [128, 4])
    c_E = dram_in("c_E", [8, 1024])
    c_past = dram_in("c_past", [128, 128])
    c_own = dram_in("c_own", [128, 128])
    c_negoff = dram_in("c_negoff", [128, 128])
    out_d = nc.dram_tensor("out", [S_LEN, D], F32, kind="ExternalOutput").ap()

    S = Sched(nc)
    top = ExitStack()

    def sb(es, name, shape, dt):
        return es.enter_context(nc.sbuf_tensor("sb_" + name, list(shape), dt))

    yacc = sb(top, "yacc", [128, NT, D], F32)
    xT = sb(top, "xT", [128, 8, S_LEN], BF16)
    wring = sb(top, "wring", [128, 8, 1024], BF16)
    wo = sb(top, "wo", [128, 4, D], BF16)
    ident = sb(top, "ident", [128, 128], BF16)
    maskT = sb(top, "maskT", [128, 128], BF16)
    ones_b = sb(top, "ones_b", [128, 128], BF16)
    cvec = sb(top, "cvec", [128, 16], F32)
    xi2 = sb(top, "xi2", [128, 4], F32)
    qn = sb(top, "qn", [128, 2], F32)
    kvn = sb(top, "kvn", [128, 2], F32)
    Eall = sb(top, "Eall", [8, 1024], BF16)
    past01 = sb(top, "past01", [128, 128], F32)
    own01 = sb(top, "own01", [128, 128], F32)
    negoff = sb(top, "negoff", [128, 128], F32)
    tmpA = sb(top, "tmpA", [128, 512], F32)
    tmpB = sb(top, "tmpB", [128, 512], F32)
    tmpI = sb(top, "tmpI", [128, 512], I32)

    PA = top.enter_context(nc.psum_tensor("PA", [128, 1024], F32))
    PB = top.enter_context(nc.psum_tensor("PB", [128, 1024], F32))
    PC = top.enter_context(nc.psum_tensor("PC", [128, 1024], F32))
    PD = top.enter_context(nc.psum_tensor("PD", [128, 1024], F32))

    inv_r = cvec[:, 0:1]
    inv64 = cvec[:, 1:2]
    sgn64 = cvec[:, 2:3]
    eps_ln = cvec[:, 3:4]
    eps_rms = cvec[:, 4:5]

    ring_pos = [0]

    def load_w(src2d, ncols):
        nsl = ncols // 128
        s0 = ring_pos[0]
        if s0 + nsl > 8:
            s0 = 0
        ring_pos[0] = (s0 + nsl) % 8
        kc = src2d.shape[0] // 128
        keys = [("wr", s) for s in range(s0, s0 + nsl)]
        S.dma("pool", wring[:, 0:kc, s0 * 128:(s0 + nsl) * 128], src2d.rearrange("(c p) n -> p c n", p=128),
              reads=[], writes=keys, key=("wr", s0))
        return s0, keys

    def wslice(s0, kc, c0=0, ncols=128):
        return wring[:, kc, s0 * 128 + c0: s0 * 128 + c0 + ncols]

    S.dma("pool", ident[:], c_ident, [], ["ident"], "c0")
    S.dma("pool", maskT[:], c_mask, [], ["maskT"], "c1")
    S.dma("pool", Eall[:], c_E, [], ["Eall"], "c2")
    S.dma("sp", cvec[:], c_vec, [], ["cvec"], "c3")
    S.dma("sp", xi2[:], c_xi2, [], ["xi2"], "c4")
    S.dma("sp", qn[:], qn_d, [], ["qn"], "c5")
    S.dma("sp", kvn[:], kvn_d, [], ["kvn"], "c6")
    S.dma("sp", past01[:], c_past, [], ["past01"], "c7")
    S.dma("sp", own01[:], c_own, [], ["own01"], "c8")
    S.dma("sp", negoff[:], c_negoff, [], ["negoff"], "c9")
    S.add("pool", lambda e: e.memset(ones_b[:], 1.0), writes=["ones_b"])

    def gen_tables(tab, inv_ap, sgn_ap):
        for j in range(4):
            cs = slice(j * 512, (j + 1) * 512)
            S.dma("sp", tmpI[:], bass.AP(pos_d.tensor, j * 512, [[0, 128], [1, 512]]), [], ["tmpI"], "posld")
            S.copy("dve", tmpA[:], tmpI[:], ["tmpI"], ["tmpA"])
            S.ts("dve", tmpA[:], tmpA[:], inv_ap, None, ALU.mult, None, ["tmpA", "cvec"], ["tmpA"])
            for which in (1, 0):
                shift = 0.0 if which == 1 else 0.25
                S.ts("dve", tmpI[:], tmpA[:], 1.0 / (2 * math.pi), shift, ALU.mult, ALU.add, ["tmpA"], ["tmpI"])
                S.copy("dve", tmpB[:], tmpI[:], ["tmpI"], ["tmpB"])
                S.stt("dve", tmpB[:], tmpB[:], -2 * math.pi, tmpA[:], ALU.mult, ALU.add, ["tmpA", "tmpB"], ["tmpB"])
                if which == 0:
                    S.ts("dve", tmpB[:], tmpB[:], math.pi / 2, None, ALU.add, None, ["tmpB"], ["tmpB"])
                S.ts("dve", tmpB[:], tmpB[:], -3.141592, 3.141592, ALU.max, ALU.min, ["tmpB"], ["tmpB"])
                if which == 1 and sgn_ap is not None:
                    S.act(tab[:, 1, cs], tmpB[:], AF.Sin, ["tmpB", "cvec"], [("tab", j)], scale=sgn_ap)
                else:
                    S.act(tab[:, which, cs], tmpB[:], AF.Sin, ["tmpB"], [("tab", j)])

    with ExitStack() as es0:
        xs = sb(es0, "xs", [128, 2, D], F32)
        xb = sb(es0, "xb", [128, 2, D], BF16)
        for t in range(NT):
            s = t % 2
            S.dma("sp", xs[:, s, :], x_d[t * 128:(t + 1) * 128, :], [], [("xs", s)], ("xs", s))
            S.add("act", (lambda s=s, t=t: (lambda e: e.mul(out=yacc[:, t, :], in_=xs[:, s, :], mul=ALPHA)))(),
                  reads=[("xs", s)], writes=[("yacc", t)])
            S.copy("dve", xb[:, s, :], xs[:, s, :], [("xs", s)], [("xb", s)])
            pt = PA[:].bitcast(BF16) if s == 0 else PB[:].bitcast(BF16)
            S.transposes([(pt[:, c * 128:(c + 1) * 128], xb[:, s, c * 128:(c + 1) * 128]) for c in range(8)], ident[:],
                         reads=[("xb", s), "ident"], writes=[("P", s, 0)])
            S.copy("dve" if s == 0 else "act", xT[:, :, t * 128:(t + 1) * 128],
                   pt[:, 0:1024].rearrange("p (c k) -> p c k", k=128), [("P", s, 0)], [("xT", t)])
        S.barrier()
    XT_ALL = [("xT", t) for t in range(NT)]

    def xt_keys(j):
        return [("xT", 4 * j + i) for i in range(4)]

    def layer_norm(layer, last):
        with ExitStack() as es:
            gb = sb(es, "gb%d" % layer, [128, 2, D], F32)
            mv = sb(es, "mv%d" % layer, [128, NT, 2], F32)
            st = sb(es, "st%d" % layer, [128, 2, 6], F32)
            rstd = sb(es, "rstd%d" % layer, [128, NT], F32)
            nb = sb(es, "nb%d" % layer, [128, NT], F32)
            x1 = sb(es, "x1_%d" % layer, [128, 2, D], F32)
            x1b = sb(es, "x1b_%d" % layer, [128, 2, D], BF16)
            S.dma("sp", gb[:, 0, :], bass.AP(lng_d.tensor, layer * D, [[0, 128], [1, D]]), [], ["gb0"], "gb0")
            S.dma("sp", gb[:, 1, :], bass.AP(lnb_d.tensor, layer * D, [[0, 128], [1, D]]), [], ["gb1"], "gb1")
            for t in range(NT):
                for hh in range(2):
                    S.add("dve", (lambda t=t, hh=hh: (lambda e: e.bn_stats(out=st[:, hh, :], in_=yacc[:, t, hh * 512:(hh + 1) * 512])))(),
                          reads=[("yacc", t)], writes=[("st", hh)])
                S.add("dve", (lambda t=t: (lambda e: e.bn_aggr(out=mv[:, t, :], in_=st[:].rearrange("p a b -> p (a b)"))))(),
                      reads=[("st", 0), ("st", 1)], writes=["mv"])
            S.act(rstd[:], mv[:, :, 1], AF.Sqrt, ["mv", "cvec"], ["rstd"], bias=eps_ln)
            S.add("dve", lambda e: e.reciprocal(out=rstd[:], in_=rstd[:]), reads=["rstd"], writes=["rstd"])
            S.stt("dve", nb[:], mv[:, :, 0], -1.0, rstd[:], ALU.mult, ALU.mult, ["mv", "rstd"], ["nb"])
            for t in range(NT):
                s = t % 2
                S.act(x1[:, s, :], yacc[:, t, :], AF.Identity, [("yacc", t), "rstd", "nb"], [("x1", s)],
                      scale=rstd[:, t:t + 1], bias=nb[:, t:t + 1])
                S.tt("dve", x1[:, s, :], x1[:, s, :], gb[:, 0, :], ALU.mult, [("x1", s), "gb0"], [("x1", s)])
                S.tt("pool", x1[:, s, :], x1[:, s, :], gb[:, 1, :], ALU.add, [("x1", s), "gb1"], [("x1", s)])
                if last:
                    S.dma("sp", out_d[t * 128:(t + 1) * 128, :], x1[:, s, :], [("x1", s)], [("out", t)], ("out", s))
                else:
                    S.add("act", (lambda s=s, t=t: (lambda e: e.mul(out=yacc[:, t, :], in_=x1[:, s, :], mul=ALPHA)))(),
                          reads=[("x1", s)], writes=[("yacc", t)])
                    S.copy("pool", x1b[:, s, :], x1[:, s, :], [("x1", s)], [("x1b", s)])
                    pt = PA[:].bitcast(BF16) if s == 0 else PB[:].bitcast(BF16)
                    S.transposes([(pt[:, c * 128:(c + 1) * 128], x1b[:, s, c * 128:(c + 1) * 128]) for c in range(8)], ident[:],
                                 reads=[("x1b", s), "ident"], writes=[("P", s, 0)])
                    S.copy("dve", xT[:, :, t * 128:(t + 1) * 128],
                           pt[:, 0:1024].rearrange("p (c k) -> p c k", k=128), [("P", s, 0)], [("xT", t)])
            S.barrier()

    def proj_fm(slot, wkc, rhs_fn, j, out_ps, pkey, rkeys, wkeys, c0=0):
        specs = []
        for kc in range(wkc):
            specs.append((out_ps, wslice(slot, kc, c0), rhs_fn(kc, j), kc == 0, kc == wkc - 1))
        S.mms(specs, reads=list(rkeys) + list(wkeys), writes=[pkey])

    def xT_rhs(kc, j):
        return xT[:, kc, j * 512:(j + 1) * 512]

    def yacc_accumulate(cat, nslots, catkey_fn, wo_keys, t):
        specs = []
        for half in range(2):
            for s in range(nslots):
                specs.append((PA[:, half * 512:(half + 1) * 512], cat[:, s, t * 128:(t + 1) * 128],
                              wo[:, s, half * 512:(half + 1) * 512], s == 0, s == nslots - 1))
        S.mms(specs, reads=[catkey_fn(t)] + list(wo_keys), writes=[("P", 0, 0), ("P", 0, 1)])
        S.tt("dve", yacc[:, t, :], yacc[:, t, :], PA[:], ALU.add, [("P", 0, 0), ("P", 0, 1), ("yacc", t)], [("yacc", t)])

    def attention_core(name, qk_list, bias_fn, Vfn, dv, scale, on_out, on_key, ebuf, rs_t):
        iters = [(jg, kt) for jg in range(4) for kt in range(4 * jg + 4)]

        def acc_bank(jg):
            return (PD, 3) if jg % 2 == 0 else (PA, 0)

        def emit_score(i):
            jg, kt = iters[i]
            c0 = max(0, kt - 4 * jg)
            q_lo = jg * 512 + c0 * 128
            q_hi = (jg + 1) * 512
            half = i % 2
            ps = PC[:, half * 512 + c0 * 128: (half + 1) * 512]
            specs = []
            rk = []
            items = list(qk_list)
            b = bias_fn(kt, jg, q_lo, q_hi) if bias_fn is not None else None
            if b is not None:
                items = items + [b]
            for ii, (kf, qf, keys) in enumerate(items):
                specs.append((ps, kf(kt), qf(q_lo, q_hi), ii == 0, ii == len(items) - 1))
                rk += keys
            S.mms(specs, reads=rk, writes=[("P", 2, half)])
            es_ = i % 3
            S.act(ebuf[:, es_, c0 * 128:512], ps, AF.Exp, [("P", 2, half)], [(name + "e", es_)], scale=scale)
            if kt >= 4 * jg:
                S.tt("dve", ebuf[:, es_, c0 * 128:(c0 + 1) * 128], ebuf[:, es_, c0 * 128:(c0 + 1) * 128], maskT[:], ALU.mult,
                     [(name + "e", es_), "maskT"], [(name + "e", es_)])

        def emit_pv(i):
            jg, kt = iters[i]
            c0 = max(0, kt - 4 * jg)
            es_ = i % 3
            PT, pk = acc_bank(jg)
            pv = []
            for qi in range(c0, 4):
                base = (qi // 2) * 512 + (qi % 2) * (dv + 1)
                pv.append((PT[:, base: base + dv + 1], ebuf[:, es_, qi * 128:(qi + 1) * 128], Vfn(kt),
                           kt == 0 and qi % 2 == 0, kt == 4 * jg + qi and qi % 2 == 1))
            S.mms(pv, reads=[(name + "e", es_), name + "V"], writes=[("P", pk, 0), ("P", pk, 1)])
            if kt == 4 * jg + 3:
                for qi in range(4):
                    base = (qi // 2) * 512 + (qi % 2) * (dv + 1)
                    t = jg * 4 + qi
                    rsl = rs_t[:, (jg % 2) * 4 + qi:(jg % 2) * 4 + qi + 1]
                    S.add("dve", (lambda base=base, rsl=rsl, PT=PT: (lambda e: e.reciprocal(out=rsl, in_=PT[:, base + dv: base + dv + 1])))(),
                          reads=[("P", pk, 0), ("P", pk, 1)], writes=[(name + "rs", jg % 2, qi)])
                    S.act(on_out(t), PT[:, base: base + dv], AF.Copy, [("P", pk, 0), ("P", pk, 1), (name + "rs", jg % 2, qi)], [on_key(t)],
                          scale=rsl)

        emit_score(0)
        for i in range(len(iters)):
            if i + 1 < len(iters):
                emit_score(i + 1)
            emit_pv(i)

    with ExitStack() as esR:
        tabR = sb(esR, "tabR", [128, 2, S_LEN], F32)
        qT = sb(esR, "qT", [128, 2, S_LEN], BF16)
        kT = sb(esR, "kT", [128, 2, S_LEN], BF16)
        vt = sb(esR, "vt", [128, NT, 512], BF16)
        gT = sb(esR, "gT", [128, 4, S_LEN], BF16)
        st32 = sb(esR, "st32", [128, 2, 512], F32)
        stb = sb(esR, "stb", [128, 2, 512], BF16)
        ktok = sb(esR, "ktok", [128, 2, 256], BF16)
        innT = sb(esR, "innT", [128, 2, 128], BF16)
        onb = sb(esR, "onb", [128, 2, 512], BF16)
        catc = sb(esR, "catc", [128, 2, 4, 128], BF16)
        bst = sb(esR, "bst", [128, 2, 6], F32)
        bmv = sb(esR, "bmv", [128, 2, 2], F32)
        sm = sb(esR, "sm", [128, 2, 4], F32)

        gen_tables(tabR, inv_r, None)
        TAB = [("tab", j) for j in range(4)]

        for h in range(4):
            g = 1.0 - 2.0 ** (-5.0 - h)
            gC = g ** 128
            xi_h = cvec[:, 8 + h:9 + h]
            vs_h = cvec[:, 12 + h:13 + h]
            for qi, dst in enumerate((qT, kT)):
                dname = "qT" if qi == 0 else "kT"
                sE, kE = load_w(w_ret[h, :, qi * 256: qi * 256 + 128], 128)
                sO, kO = load_w(w_ret[h, :, qi * 256 + 128: qi * 256 + 256], 128)
                for j in range(4):
                    hf = j % 2
                    psA = PA[:, hf * 512:(hf + 1) * 512]
                    psB = PB[:, hf * 512:(hf + 1) * 512]
                    proj_fm(sE, 8, xT_rhs, j, psA, ("P", 0, hf), xt_keys(j), kE)
                    proj_fm(sO, 8, xT_rhs, j, psB, ("P", 1, hf), xt_keys(j), kO)
                    cs = slice(j * 512, (j + 1) * 512)
                    cosj = tabR[:, 0, cs]
                    sinj = tabR[:, 1, cs]
                    S.tt("dve", tmpA[:], psA, cosj, ALU.mult, [("P", 0, hf), ("tab", j)], ["tmpA"])
                    S.tt("dve", tmpB[:], psB, sinj, ALU.mult, [("P", 1, hf), ("tab", j)], ["tmpB"])
                    S.tt("dve", dst[:, 0, cs], tmpA[:], tmpB[:], ALU.subtract, ["tmpA", "tmpB"], [(dname, 0, j)])
                    S.tt("dve", tmpA[:], psB, cosj, ALU.mult, [("P", 1, hf), ("tab", j)], ["tmpA"])
                    S.tt("dve", tmpB[:], psA, sinj, ALU.mult, [("P", 0, hf), ("tab", j)], ["tmpB"])
                    S.tt("dve", dst[:, 1, cs], tmpA[:], tmpB[:], ALU.add, ["tmpA", "tmpB"], [(dname, 1, j)])
            sV, kV = load_w(w_ret[h, :, 512:1024], 512)
            for t in range(NT):
                hf = t % 2
                ps = PA[:, hf * 512:(hf + 1) * 512]
                S.mms([(ps, xT[:, kc, t * 128:(t + 1) * 128], wring[:, kc, sV * 128:(sV + 4) * 128], kc == 0, kc == 7) for kc in range(8)],
                      reads=[("xT", t)] + kV, writes=[("P", 0, hf)])
                S.act(vt[:, t, :], ps, AF.Copy, [("P", 0, hf), "cvec"], [("vt", t)], scale=vs_h)
            sG, kG = load_w(w_ret[h, :, 1024:1536], 512)
            for c in range(4):
                for j in range(4):
                    hf = j % 2
                    ps = PB[:, hf * 512:(hf + 1) * 512]
                    proj_fm(sG, 8, xT_rhs, j, ps, ("P", 1, hf), xt_keys(j), kG, c0=c * 128)
                    S.act(gT[:, c, j * 512:(j + 1) * 512], ps, AF.Silu, [("P", 1, hf)], [("gT", c, j)])
            S.dma("pool", wo[:], w_oe[h * 512:(h + 1) * 512, :].rearrange("(c p) n -> p c n", p=128), [], ["wo"], "wo")
            PDb = PD[:].bitcast(BF16)

            def o_bank(n):
                return (PC[:, 512:1024], ("P", 2, 1)) if n % 2 == 0 else (PB[:, 512:1024], ("P", 1, 1))

            def head_part(n, h=h, gC=gC, xi_h=xi_h):
                j = n // 4
                cs = slice(n * 128, (n + 1) * 128)
                d2 = n % 2
                qk = [("qT", 0, j), ("qT", 1, j)]
                kk = [("kT", 0, j), ("kT", 1, j)]
                st_ps = PC[:, d2 * 128:(d2 + 1) * 128]
                S.mms([(st_ps, kT[:, c, cs], qT[:, c, cs], c == 0, c == 1) for c in range(2)], reads=qk + kk, writes=[("P", 2, 0, d2)])
                S.tt("dve", innT[:, d2, :], st_ps, maskT[:], ALU.mult, [("P", 2, 0, d2), "maskT"], [("innT", d2)])
                if n < NT - 1:
                    kt_ps = PDb[:, 0:256]
                    S.transposes([(kt_ps[:, c * 128:(c + 1) * 128], kT[:, c, cs]) for c in range(2)], ident[:],
                                 reads=kk + ["ident"], writes=[("P", 3, 0, "kt")])
                    S.act(ktok[:, d2, :], kt_ps, AF.Copy, [("P", 3, 0, "kt")], [("ktok", d2)], scale=gC)
                o_ps, o_key = o_bank(n)
                specs = [(o_ps, innT[:, d2, :], vt[:, n, :], True, n == 0)]
                rk = [("innT", d2), ("vt", n)]
                if n > 0:
                    for c in range(2):
                        specs.append((o_ps, qT[:, c, cs], stb[:, c, :], False, c == 1))
                    rk += qk + [("stb", 0), ("stb", 1)]
                S.mms(specs, reads=rk, writes=[o_key])
                if n < NT - 1:
                    S.mms([(PD[:, 512:1024], ktok[:, d2, 0:128], vt[:, n, :], True, True)], reads=[("ktok", d2), ("vt", n)], writes=[("P", 3, 1)])
                    S.mms([(PB[:, 0:512], ktok[:, d2, 128:256], vt[:, n, :], True, True)], reads=[("ktok", d2), ("vt", n)], writes=[("P", 1, 0)])
                    if n == 0:
                        S.copy("dve", st32[:, 0, :], PD[:, 512:1024], [("P", 3, 1)], [("st32", 0)])
                        S.copy("dve", st32[:, 1, :], PB[:, 0:512], [("P", 1, 0)], [("st32", 1)])
                    else:
                        S.stt("dve", st32[:, 0, :], st32[:, 0, :], gC, PD[:, 512:1024], ALU.mult, ALU.add, [("P", 3, 1), ("st32", 0)], [("st32", 0)])
                        S.stt("dve", st32[:, 1, :], st32[:, 1, :], gC, PB[:, 0:512], ALU.mult, ALU.add, [("P", 1, 0), ("st32", 1)], [("st32", 1)])
                    S.copy("act", stb[:, 0, :], st32[:, 0, :], [("st32", 0)], [("stb", 0)])
                    S.copy("pool", stb[:, 1, :], st32[:, 1, :], [("st32", 1)], [("stb", 1)])
                S.add("dve", (lambda o_ps=o_ps: (lambda e: e.bn_stats(out=bst[:, d2, :], in_=o_ps)))(), reads=[o_key], writes=[("bst", d2)])
                S.add("dve", lambda e: e.bn_aggr(out=bmv[:, d2, :], in_=bst[:, d2, :]), reads=[("bst", d2)], writes=[("bmv", d2)])
                S.ts("dve", sm[:, d2, 0:1], bmv[:, d2, 1:2], xi2[:, h:h + 1], LN_EPS, ALU.mult, ALU.add, [("bmv", d2), "xi2"], [("sm", d2, 0)])
                S.act(sm[:, d2, 1:2], sm[:, d2, 0:1], AF.Sqrt, [("sm", d2, 0)], [("sm", d2, 1)])
                S.add("dve", lambda e: e.reciprocal(out=sm[:, d2, 1:2], in_=sm[:, d2, 1:2]), reads=[("sm", d2, 1)], writes=[("sm", d2, 1)])
                S.ts("dve", sm[:, d2, 2:3], sm[:, d2, 1:2], xi_h, None, ALU.mult, None, [("sm", d2, 1), "cvec"], [("sm", d2, 2)])
                S.stt("dve", sm[:, d2, 3:4], bmv[:, d2, 0:1], -1.0, sm[:, d2, 2:3], ALU.mult, ALU.mult, [("bmv", d2), ("sm", d2, 2)], [("sm", d2, 3)])
                S.act(onb[:, d2, :], o_ps, AF.Identity, [o_key, ("sm", d2, 2), ("sm", d2, 3)], [("onb", d2)],
                      scale=sm[:, d2, 2:3], bias=sm[:, d2, 3:4])

            def tail_part(n):
                j = n // 4
                cs = slice(n * 128, (n + 1) * 128)
                d2 = n % 2
                ct_ps = PDb[:, 512:1024]
                S.transposes([(ct_ps[:, c * 128:(c + 1) * 128], onb[:, d2, c * 128:(c + 1) * 128]) for c in range(4)], ident[:],
                             reads=[("onb", d2), "ident"], writes=[("P", 3, 0, "ct")])
                S.tt("dve", catc[:, d2, :, :], ct_ps.rearrange("p (c k) -> p c k", k=128), gT[:, :, cs], ALU.mult,
                     [("P", 3, 0, "ct")] + [("gT", c, j) for c in range(4)], [("catc", d2)])
                specs = []
                for half in range(2):
                    for c in range(4):
                        specs.append((PA[:, half * 512:(half + 1) * 512], catc[:, d2, c, :], wo[:, c, half * 512:(half + 1) * 512], c == 0, c == 3))
                S.mms(specs, reads=[("catc", d2), "wo"], writes=[("P", 0, 0), ("P", 0, 1)])
                S.tt("dve", yacc[:, n, :], yacc[:, n, :], PA[:], ALU.add, [("P", 0, 0), ("P", 0, 1), ("yacc", n)], [("yacc", n)])

            head_part(0)
            for n in range(NT):
                if n + 1 < NT:
                    head_part(n + 1)
                tail_part(n)
        S.barrier()

    if debug_out == "ret":
        for t in range(NT):
            S.dma("sp", out_d[t * 128:(t + 1) * 128, :], yacc[:, t, :], [("yacc", t)], [("out", t)], "out")
        S.finalize(); top.close(); S.close()
        return nc

    with ExitStack() as esT:
        tab64 = sb(esT, "tab64", [128, 2, S_LEN], F32)
        gen_tables(tab64, inv64, sgn64)
        ebuf = sb(esT, "ebuf", [128, 3, 512], BF16)
        rs_t = sb(esT, "rs_t", [128, 8], F32)
        S.barrier()

        def rope_pair(psA, psB, kA, kB, j, out_ap, out_key, eng_out="dve"):
            cs = slice(j * 512, (j + 1) * 512)
            S.tt("dve", tmpA[:], psA, tab64[:, 0, cs], ALU.mult, [kA, ("tab", j)], ["tmpA"])
            S.tt("dve", tmpB[:], psB, tab64[:, 1, cs], ALU.mult, [kB, ("tab", j)], ["tmpB"])
            S.tt("dve", out_ap, tmpA[:], tmpB[:], ALU.add, ["tmpA", "tmpB"], [out_key])

        with ExitStack() as esM:
            mqT = sb(esM, "mqT", [128, 2, S_LEN], BF16)
            mkvT = sb(esM, "mkvT", [128, 2, S_LEN], BF16)
            kpeT = sb(esM, "kpeT", [128, S_LEN], BF16)
            qpeT = sb(esM, "qpeT", [128, S_LEN], BF16)
            qnT = sb(esM, "qnT", [128, S_LEN], BF16)
            knT = sb(esM, "knT", [128, S_LEN], BF16)
            Vaug = sb(esM, "Vaug", [128, NT, 129], BF16)
            gmT = sb(esM, "gmT", [128, S_LEN], BF16)
            catM = sb(esM, "catM", [128, 2, S_LEN], BF16)
            onM = sb(esM, "onM", [128, NT, 128], BF16)
            sqb = sb(esM, "sqb", [128, 2, 512], BF16)
            rawg = sb(esM, "rawg", [128, 2, 512], F32)
            S.add("pool", lambda e: e.memset(Vaug[:, :, 128:129], 1.0), writes=["VaugOnes"])

            for which, (dstT, gvec, col0) in enumerate(((mqT, qn, 0), (mkvT, kvn, 256))):
                dn = "mqT" if which == 0 else "mkvT"
                s0, k0 = load_w(w_mla[:, col0:col0 + 128], 128)
                s1, k1 = load_w(w_mla[:, col0 + 128:col0 + 256], 128)
                for j in range(4):
                    hf = j % 2
                    psl = [PA[:, hf * 512:(hf + 1) * 512], PB[:, hf * 512:(hf + 1) * 512]]
                    pk = [("P", 0, hf), ("P", 1, hf)]
                    proj_fm(s0, 8, xT_rhs, j, psl[0], pk[0], xt_keys(j), k0)
                    proj_fm(s1, 8, xT_rhs, j, psl[1], pk[1], xt_keys(j), k1)
                    for c in range(2):
                        S.act(sqb[:, c, :], psl[c], AF.Square, [pk[c]], [("sqb", c)])
                        S.act(rawg[:, c, :], psl[c], AF.Copy, [pk[c], "qn", "kvn"], [("rawg", c)], scale=gvec[:, c:c + 1])
                    ss_ps = PC[:, 0:512]
                    S.mms([(ss_ps, ones_b[:], sqb[:, c, :], c == 0, c == 1) for c in range(2)],
                          reads=[("sqb", 0), ("sqb", 1), "ones_b"], writes=[("P", 2, 0)])
                    S.act(tmpA[:], ss_ps, AF.Sqrt, [("P", 2, 0), "cvec"], ["tmpA"], scale=1.0 / 256.0, bias=eps_rms)
                    S.add("dve", lambda e: e.reciprocal(out=tmpA[:], in_=tmpA[:]), reads=["tmpA"], writes=["tmpA"])
                    for c in range(2):
                        S.tt("dve", dstT[:, c, j * 512:(j + 1) * 512], rawg[:, c, :], tmpA[:], ALU.mult,
                             [("rawg", c), "tmpA"], [(dn, j)])
            MQ = [("mqT", j) for j in range(4)]
            MKV = [("mkvT", j) for j in range(4)]
            sA, kA_ = load_w(w_mla[:, 512:640], 128)
            sB, kB_ = load_w(w_mla[:, 640:768], 128)
            for j in range(4):
                hf = j % 2
                psA = PA[:, hf * 512:(hf + 1) * 512]
                psB = PB[:, hf * 512:(hf + 1) * 512]
                proj_fm(sA, 8, xT_rhs, j, psA, ("P", 0, hf), xt_keys(j), kA_)
                proj_fm(sB, 8, xT_rhs, j, psB, ("P", 1, hf), xt_keys(j), kB_)
                rope_pair(psA, psB, ("P", 0, hf), ("P", 1, hf), j, kpeT[:, j * 512:(j + 1) * 512], ("kpeT", j))
            KPE = [("kpeT", j) for j in range(4)]

            def mq_rhs(kc, j):
                return mqT[:, kc, j * 512:(j + 1) * 512]

            def mkv_rhs(kc, j):
                return mkvT[:, kc, j * 512:(j + 1) * 512]

            for hp in range(4):
                sA, kA_ = load_w(w_uq[:, 1024 + hp * 128: 1024 + (hp + 1) * 128], 128)
                sB, kB_ = load_w(w_uq[:, 1536 + hp * 128: 1536 + (hp + 1) * 128], 128)
                for j in range(4):
                    hf = j % 2
                    psA = PA[:, hf * 512:(hf + 1) * 512]
                    psB = PB[:, hf * 512:(hf + 1) * 512]
                    proj_fm(sA, 2, mq_rhs, j, psA, ("P", 0, hf), [("mqT", j)], kA_)
                    proj_fm(sB, 2, mq_rhs, j, psB, ("P", 1, hf), [("mqT", j)], kB_)
                    rope_pair(psA, psB, ("P", 0, hf), ("P", 1, hf), j, qpeT[:, j * 512:(j + 1) * 512], ("qpeT", j))
                QPE = [("qpeT", j) for j in range(4)]
                S.dma("pool", wo[:, 0:2, :], w_oe[2048 + hp * 256: 2048 + (hp + 1) * 256, :].rearrange("(c p) n -> p c n", p=128),
                      [], ["wo"], "wo")
                for e2 in range(2):
                    h = hp * 2 + e2
                    r0 = 64 * e2
                    sQ, kQ = load_w(w_uq[:, h * 128:(h + 1) * 128], 128)
                    sK, kK = load_w(w_ukv[:, h * 128:(h + 1) * 128], 128)
                    sVv, kVv = load_w(w_ukv[:, 1024 + h * 128: 1024 + (h + 1) * 128], 128)
                    sG, kG = load_w(w_mla[:, 768 + h * 128: 768 + (h + 1) * 128], 128)
                    for j in range(4):
                        hf = j % 2
                        psA = PA[:, hf * 512:(hf + 1) * 512]
                        psB = PB[:, hf * 512:(hf + 1) * 512]
                        proj_fm(sQ, 2, mq_rhs, j, psA, ("P", 0, hf), [("mqT", j)], kQ)
                        S.copy("act", qnT[:, j * 512:(j + 1) * 512], psA, [("P", 0, hf)], [("qnT", j)])
                        proj_fm(sK, 2, mkv_rhs, j, psB, ("P", 1, hf), [("mkvT", j)], kK)
                        S.copy("act", knT[:, j * 512:(j + 1) * 512], psB, [("P", 1, hf)], [("knT", j)])
                    for j in range(4):
                        hf = j % 2
                        psA = PA[:, hf * 512:(hf + 1) * 512]
                        proj_fm(sG, 8, xT_rhs, j, psA, ("P", 0, hf), xt_keys(j), kG)
                        S.act(gmT[:, j * 512:(j + 1) * 512], psA, AF.Silu, [("P", 0, hf)], [("gmT", j)])
                    for t in range(NT):
                        hf = t % 2
                        ps = PB[:, hf * 512: hf * 512 + 128]
                        S.mms([(ps, mkvT[:, kc, t * 128:(t + 1) * 128], wslice(sVv, kc), kc == 0, kc == 1) for kc in range(2)],
                              reads=[("mkvT", t // 4)] + kVv, writes=[("P", 1, hf)])
                        S.copy("act", Vaug[:, t, 0:128], ps, [("P", 1, hf), "VaugOnes"], ["mV"])
                    QN = [("qnT", j) for j in range(4)]
                    KN = [("knT", j) for j in range(4)]
                    qk_list = [
                        (lambda kt: knT[:, kt * 128:(kt + 1) * 128], lambda lo, hi: qnT[:, lo:hi], QN + KN),
                        (lambda kt, r0=r0: kpeT[r0:r0 + 64, kt * 128:(kt + 1) * 128], lambda lo, hi, r0=r0: qpeT[r0:r0 + 64, lo:hi], QPE + KPE),
                    ]
                    attention_core("m", qk_list, None, lambda kt: Vaug[:, kt, :], 128, 192.0 ** -0.5,
                                   lambda t: onM[:, t, :], lambda t: ("onM", t), ebuf, rs_t)
                    for jg in range(4):
                        ct_ps = PB[:].bitcast(BF16)[:, 0:512]
                        S.transposes([(ct_ps[:, i * 128:(i + 1) * 128], onM[:, jg * 4 + i, :]) for i in range(4)], ident[:],
                                     reads=[("onM", jg * 4 + i) for i in range(4)] + ["ident"], writes=[("P", 1, 0)])
                        S.tt("dve", catM[:, e2, jg * 512:(jg + 1) * 512], ct_ps, gmT[:, jg * 512:(jg + 1) * 512], ALU.mult,
                             [("P", 1, 0), ("gmT", jg)], [("catM", e2, jg)])
                for t in range(NT):
                    yacc_accumulate(catM, 2, lambda t: ("catM", 0, t // 4), ["wo"] + [("catM", 1, j) for j in range(4)], t)
            S.barrier()

        layer_norm(0, last=(n_layers == 1))
        if n_layers == 1:
            S.finalize(); esT.close(); top.close(); S.close()
            return nc

        with ExitStack() as esL:
            qTl = sb(esL, "qTl", [128, S_LEN], BF16)
            kmh = sb(esL, "kmh", [128, 8], BF16)
            kml = sb(esL, "kml", [128, 8], BF16)
            kmr = sb(esL, "kmr", [128, 8], F32)
            qTb = sb(esL, "qTb", [128, S_LEN], BF16)
            kTb = sb(esL, "kTb", [128, S_LEN], BF16)
            kmean = sb(esL, "kmean", [128, 8], F32)
            Vp = sb(esL, "Vp", [128, NT, 2, 65], BF16)
            gmT = sb(esL, "gmT1", [128, S_LEN], BF16)
            nselT = sb(esL, "nselT", [8, 2, S_LEN], BF16)
            catL = sb(esL, "catL", [128, 2, S_LEN], BF16)
            onL = sb(esL, "onL", [128, NT, 128], BF16)
            gm = sb(esL, "gm", [128, 128], F32)
            sel = sb(esL, "sel", [128, 128], F32)
            top8 = sb(esL, "top8", [128, 8], F32)
            biasb = sb(esL, "biasb", [128, 128], BF16)
            S.add("pool", lambda e: e.memset(Vp[:, :, :, 64:65], 1.0), writes=["VpOnes"])

            for hp in range(8):
                sl = hp % 2
                for qi in range(2):
                    base = qi * 2048
                    sA, kA_ = load_w(w_l1[:, base + hp * 128: base + (hp + 1) * 128], 128)
                    sB, kB_ = load_w(w_l1[:, base + 1024 + hp * 128: base + 1024 + (hp + 1) * 128], 128)
                    for j in range(4):
                        hf = j % 2
                        psA = PA[:, hf * 512:(hf + 1) * 512]
                        psB = PB[:, hf * 512:(hf + 1) * 512]
                        proj_fm(sA, 8, xT_rhs, j, psA, ("P", 0, hf), xt_keys(j), kA_)
                        proj_fm(sB, 8, xT_rhs, j, psB, ("P", 1, hf), xt_keys(j), kB_)
                        cs = slice(j * 512, (j + 1) * 512)
                        if qi == 0:
                            S.tt("dve", tmpA[:], psA, tab64[:, 0, cs], ALU.mult, [("P", 0, hf), ("tab", j)], ["tmpA"])
                            S.tt("dve", tmpB[:], psB, tab64[:, 1, cs], ALU.mult, [("P", 1, hf), ("tab", j)], ["tmpB"])
                            S.tt("dve", tmpA[:], tmpA[:], tmpB[:], ALU.add, ["tmpA", "tmpB"], ["tmpA"])
                            S.copy("act", qTb[:, cs], tmpA[:], ["tmpA"], [("qTb", j)])
                            S.tt("dve", qTl[:, cs], tmpA[:], qTb[:, cs], ALU.subtract, ["tmpA", ("qTb", j)], [("qTl", j)])
                        else:
                            S.tt("dve", tmpA[:], psA, tab64[:, 0, cs], ALU.mult, [("P", 0, hf), ("tab", j)], ["tmpA"])
                            S.tt("dve", tmpB[:], psB, tab64[:, 1, cs], ALU.mult, [("P", 1, hf), ("tab", j)], ["tmpB"])
                            S.tt("dve", tmpA[:], tmpA[:], tmpB[:], ALU.add, ["tmpA", "tmpB"], ["tmpA"])
                            S.copy("act", kTb[:, cs], tmpA[:], ["tmpA"], [("kTb", j)])
                            S.add("dve", (lambda j=j: (lambda e: e.tensor_reduce(out=kmean[:, 2 * j:2 * j + 2],
                                                                                 in_=tmpA[:].rearrange("p (b l) -> p b l", l=256),
                                                                                 axis=AX.X, op=ALU.add)))(),
                                  reads=["tmpA"], writes=["kmean"])
                QB = [("qTb", j) for j in range(4)]
                QL = [("qTl", j) for j in range(4)]
                S.copy("dve", kmh[:], kmean[:], ["kmean"], ["kmh"])
                S.tt("dve", kmr[:], kmean[:], kmh[:], ALU.subtract, ["kmean", "kmh"], ["kmr"])
                S.copy("dve", kml[:], kmr[:], ["kmr"], ["kml"])
                KB = [("kTb", j) for j in range(4)]
                sVv, kVv = load_w(w_l1[:, 4096 + hp * 128: 4096 + (hp + 1) * 128], 128)
                sG, kG = load_w(w_l1[:, 5120 + hp * 128: 5120 + (hp + 1) * 128], 128)
                for t in range(NT):
                    hf = t % 2
                    ps = PB[:, hf * 512: hf * 512 + 128]
                    S.mms([(ps, xT[:, kc, t * 128:(t + 1) * 128], wslice(sVv, kc), kc == 0, kc == 7) for kc in range(8)],
                          reads=[("xT", t)] + kVv, writes=[("P", 1, hf)])
                    S.copy("act", Vp[:, t, :, 0:64], ps.rearrange("p (a b) -> p a b", b=64), [("P", 1, hf), "VpOnes"], ["lV"])
                for j in range(4):
                    hf = j % 2
                    psA = PA[:, hf * 512:(hf + 1) * 512]
                    proj_fm(sG, 8, xT_rhs, j, psA, ("P", 0, hf), xt_keys(j), kG)
                    S.act(gmT[:, j * 512:(j + 1) * 512], psA, AF.Silu, [("P", 0, hf)], [("gmT", j)])
                if hp % 2 == 0:
                    S.dma("pool", wo[:, 0:2, :], w_oo[hp * 128:(hp + 2) * 128, :].rearrange("(c p) n -> p c n", p=128), [], ["wo"], "wo")
                for e2 in range(2):
                    r0 = 64 * e2
                    g_ps = PC[:, 0:128]
                    gspecs = []
                    for t in range(NT):
                        ts_ = slice(t * 128, (t + 1) * 128)
                        gspecs.append((g_ps[:, t * 8:(t + 1) * 8], qTb[r0:r0 + 64, ts_], kmh[r0:r0 + 64, :], True, False))
                        gspecs.append((g_ps[:, t * 8:(t + 1) * 8], qTl[r0:r0 + 64, ts_], kmh[r0:r0 + 64, :], False, False))
                        gspecs.append((g_ps[:, t * 8:(t + 1) * 8], qTb[r0:r0 + 64, ts_], kml[r0:r0 + 64, :], False, True))
                    S.mms(gspecs, reads=QB + QL + ["kmh", "kml"], writes=[("P", 2, 0)])
                    S.tt("dve", gm[:], g_ps, past01[:], ALU.mult, [("P", 2, 0), "past01"], ["gm"])
                    S.tt("dve", gm[:], gm[:], negoff[:], ALU.add, ["gm", "negoff"], ["gm"])
                    for t in range(NT):
                        S.add("dve", (lambda t=t: (lambda e: e.max(out=top8[:], in_=gm[:, t * 8:(t + 1) * 8])))(), reads=["gm"], writes=["top8"])
                        S.ts("dve", sel[:, t * 8:(t + 1) * 8], gm[:, t * 8:(t + 1) * 8], top8[:, 2:3], None, ALU.is_ge, None, ["gm", "top8"], ["sel"])
                    S.tt("dve", sel[:], sel[:], past01[:], ALU.mult, ["sel", "past01"], ["sel"])
                    S.tt("dve", sel[:], sel[:], own01[:], ALU.add, ["sel", "own01"], ["sel"])
                    S.ts("dve", biasb[:], sel[:], NEGB, -NEGB, ALU.mult, ALU.add, ["sel"], ["biasb"])
                    bt_ps = PC[:].bitcast(BF16)[0:8, 1024:2048]
                    for q4 in range(4):
                        S.transposes([(bt_ps[:, i * 128:(i + 1) * 128], biasb[:, (q4 * 4 + i) * 8:(q4 * 4 + i + 1) * 8]) for i in range(4)], ident[:],
                                     reads=["biasb", "ident"], writes=[("P", 2, 1)])
                        S.copy("act", nselT[:, e2, q4 * 512:(q4 + 1) * 512], bt_ps[:, 0:512], [("P", 2, 1)], [("nselT", e2, q4)])
                    NS = [("nselT", e2, j) for j in range(4)]

                    def bias_fn(kt, jg, lo, hi, e2=e2, NS=NS):
                        nb_ = kt // 2
                        if nb_ == 2 * jg + 1:
                            return None
                        return (lambda kt_, nb_=nb_: Eall[0:8, nb_ * 128:(nb_ + 1) * 128],
                                lambda lo_, hi_, e2=e2: nselT[0:8, e2, lo_:hi_], NS + ["Eall"])

                    qk_list = [(lambda kt, r0=r0: kTb[r0:r0 + 64, kt * 128:(kt + 1) * 128],
                                lambda lo, hi, r0=r0: qTb[r0:r0 + 64, lo:hi], QB + KB)]
                    attention_core("l", qk_list, bias_fn, (lambda kt, e2=e2: Vp[:, kt, e2, :]), 64, 0.125,
                                   (lambda t, e2=e2: onL[:, t, e2 * 64:(e2 + 1) * 64]), (lambda t, e2=e2: ("onL", t, e2)), ebuf, rs_t)
                for jg in range(4):
                    ct_ps = PB[:].bitcast(BF16)[:, 0:512]
                    S.transposes([(ct_ps[:, i * 128:(i + 1) * 128], onL[:, jg * 4 + i, :]) for i in range(4)], ident[:],
                                 reads=[("onL", jg * 4 + i, e) for i in range(4) for e in range(2)] + ["ident"], writes=[("P", 1, 0)])
                    S.tt("dve", catL[:, sl, jg * 512:(jg + 1) * 512], ct_ps, gmT[:, jg * 512:(jg + 1) * 512], ALU.mult,
                         [("P", 1, 0), ("gmT", jg)], [("catL", sl, jg)])
                if hp % 2 == 1:
                    for t in range(NT):
                        yacc_accumulate(catL, 2, lambda t: ("catL", 0, t // 4), ["wo"] + [("catL", 1, j) for j in range(4)], t)
            S.barrier()
        layer_norm(1, last=True)

    S.finalize()
    top.close()
    S.close()
    return nc


def _host_constants():
    c = {}
    c["c_ident"] = np.eye(128, dtype=np.float32)
    k = np.arange(128)
    c["c_mask"] = (k[None, :] >= k[:, None]).astype(np.float32)
    vec = np.zeros((128, 16), np.float32)
    vec[:, 0] = 1.0 / (10000.0 ** np.linspace(0.0, 1.0, 128, dtype=np.float32))
    inv64 = 1.0 / (10000.0 ** (np.arange(0, 64, 2, dtype=np.float32) / 64.0))
    vec[:, 1] = inv64[np.arange(128) % 32]
    vec[:, 2] = np.where((np.arange(128) % 64) < 32, -1.0, 1.0)
    vec[:, 3] = LN_EPS
    vec[:, 4] = RMS_EPS
    xi2 = np.zeros((128, 4), np.float32)
    for h in range(4):
        g = 1.0 - 2.0 ** (-5.0 - h)
        xi = (g ** (k + 1.0)) * (256.0 ** -0.5)
        vec[:, 8 + h] = xi
        vec[:, 12 + h] = g ** (-(k + 1.0))
        xi2[:, h] = xi * xi
    c["c_vec"] = vec
    c["c_xi2"] = xi2
    E = np.zeros((8, 8, 128), np.float32)
    for n in range(8):
        E[n, n, :] = 1.0
    c["c_E"] = E.reshape(8, 1024)
    past = np.zeros((16, 8), np.float32)
    own = np.zeros((16, 8), np.float32)
    for t in range(16):
        qb = t // 2
        past[t, :qb] = 1.0
        own[t, qb] = 1.0
    c["c_past"] = np.broadcast_to(past.reshape(1, 128), (128, 128)).copy()
    c["c_own"] = np.broadcast_to(own.reshape(1, 128), (128, 128)).copy()
    c["c_negoff"] = ((c["c_past"] - 1.0) * 1e30).astype(np.float32)
    return c


def _layout_weights(w_in_even, q_norm_even, w_uq_even, kv_norm_even, w_ukv_even, w_out_even, w_in_odd, w_out_odd):
    wi = np.asarray(w_in_even[0])
    o = {}
    ev = np.arange(0, 256, 2)
    od = np.arange(1, 256, 2)
    w_ret = np.empty((4, 1024, 1536), np.float32)
    for h in range(4):
        rq = wi[:, h * 256:(h + 1) * 256]
        rk = wi[:, 1024 + h * 256: 1024 + (h + 1) * 256]
        w_ret[h, :, 0:128] = rq[:, ev]
        w_ret[h, :, 128:256] = rq[:, od]
        w_ret[h, :, 256:384] = rk[:, ev]
        w_ret[h, :, 384:512] = rk[:, od]
        w_ret[h, :, 512:1024] = wi[:, 2048 + h * 512: 2048 + (h + 1) * 512]
        w_ret[h, :, 1024:1536] = wi[:, 4096 + h * 512: 4096 + (h + 1) * 512]
    o["w_ret"] = w_ret
    b = 6144
    mq = wi[:, b:b + 256]
    mkv = wi[:, b + 256:b + 512]
    mkr = wi[:, b + 512:b + 576]
    mg = wi[:, b + 576:b + 1600]
    swap64 = np.concatenate([np.arange(32, 64), np.arange(0, 32)])
    o["w_mla"] = np.ascontiguousarray(np.concatenate([mq, mkv, mkr, mkr, mkr[:, swap64], mkr[:, swap64], mg], axis=1))
    wuq = np.asarray(w_uq_even[0])
    nope = np.concatenate([wuq[:, h * 192: h * 192 + 128] for h in range(8)], axis=1)
    pe1 = np.concatenate([wuq[:, h * 192 + 128: h * 192 + 192] for h in range(8)], axis=1)
    pe2 = np.concatenate([wuq[:, h * 192 + 128: h * 192 + 192][:, swap64] for h in range(8)], axis=1)
    o["w_uq"] = np.ascontiguousarray(np.concatenate([nope, pe1, pe2], axis=1))
    wukv = np.asarray(w_ukv_even[0])
    kn = np.concatenate([wukv[:, h * 256: h * 256 + 128] for h in range(8)], axis=1)
    vv = np.concatenate([wukv[:, h * 256 + 128: h * 256 + 256] for h in range(8)], axis=1)
    o["w_ukv"] = np.ascontiguousarray(np.concatenate([kn, vv], axis=1))
    o["w_oe"] = np.ascontiguousarray(np.asarray(w_out_even[0]))
    wo_ = np.asarray(w_in_odd[0])
    sw = np.concatenate([h * 64 + swap64 for h in range(16)])
    q = wo_[:, 0:1024]
    kk = wo_[:, 1024:2048]
    o["w_l1"] = np.ascontiguousarray(np.concatenate([q, q[:, sw], kk, kk[:, sw], wo_[:, 2048:3072], wo_[:, 3072:4096]], axis=1))
    o["w_oo"] = np.ascontiguousarray(np.asarray(w_out_odd[0]))
    o["qn"] = np.ascontiguousarray(np.asarray(q_norm_even[0]).reshape(2, 128).T)
    o["kvn"] = np.ascontiguousarray(np.asarray(kv_norm_even[0]).reshape(2, 128).T)
    return o


def make_in_maps(x, positions, w_in_even, q_norm_even, w_uq_even, kv_norm_even, w_ukv_even,
                 w_out_even, w_in_odd, w_out_odd, ln_g, ln_b):
    shared = _host_constants()
    shared.update(_layout_weights(w_in_even, q_norm_even, w_uq_even, kv_norm_even, w_ukv_even, w_out_even, w_in_odd, w_out_odd))
    shared["lng"] = np.ascontiguousarray(np.asarray(ln_g, dtype=np.float32))
    shared["lnb"] = np.ascontiguousarray(np.asarray(ln_b, dtype=np.float32))
    x = np.asarray(x)
    positions = np.asarray(positions)
    in_maps = []
    for b in range(8):
        m = dict(shared)
        m["x"] = np.ascontiguousarray(x[b])
        m["pos"] = np.ascontiguousarray(positions[b].astype(np.int32).reshape(1, S_LEN))
        in_maps.append(m)
    return in_maps


def kernel(x, positions, w_in_even, q_norm_even, w_uq_even, kv_norm_even, w_ukv_even,
           w_out_even, w_in_odd, w_out_odd, ln_g, ln_b):
    in_maps = make_in_maps(x, positions, w_in_even, q_norm_even, w_uq_even, kv_norm_even, w_ukv_even,
                           w_out_even, w_in_odd, w_out_odd, ln_g, ln_b)
    nc = build_program(2)
    res = run_bass_kernel_spmd(nc, in_maps, core_ids=list(range(8)))
    out = np.stack([np.asarray(r["out"]) for r in res.results], axis=0).astype(np.float32)
    if os.environ.get("K_DUMP"):
        np.save(os.environ["K_DUMP"], out)
    return out
```

```python
import math
import os
from contextlib import ExitStack

import numpy as np
import concourse.bass as bass
import concourse.mybir as mybir
from concourse.bass_utils import run_bass_kernel_spmd

F32 = mybir.dt.float32
BF16 = mybir.dt.bfloat16
I32 = mybir.dt.int32
ALU = mybir.AluOpType
AF = mybir.ActivationFunctionType
AX = mybir.AxisListType

S_LEN = 2048
D = 1024
NT = 16
ALPHA = 4.0 ** 0.25
LN_EPS = 1e-5
RMS_EPS = 1e-6
NEGB = 30000.0


class Sched:
    ENGS = ("pe", "act", "dve", "pool", "sp")

    def __init__(self, nc):
        self.nc = nc
        self.ops = []
        self.last_w = {}
        self.readers = {}
        self.eng_sems = {}
        self.dma_sems = {}
        self.dma_cnt = {}
        self._stack = []
        self.barrier_deps = set()
        self.last_on_eng = {}
        self.last_dma = {}
        for e in ("pe", "act", "dve", "pool"):
            self.eng_sems[e] = self.sem("s_" + e)

    def sem(self, name):
        cm = self.nc.semaphore(name)
        s = cm.__enter__()
        self._stack.append(cm)
        return s

    def add(self, eng, fn, reads=(), writes=(), dma=None):
        idx = len(self.ops)
        deps = set(self.barrier_deps)
        for k in reads:
            w = self.last_w.get(k)
            if w is not None:
                deps.add(w)
        for k in writes:
            w = self.last_w.get(k)
            if w is not None:
                deps.add(w)
            for r in self.readers.get(k, ()):
                deps.add(r)
        op = dict(eng=eng, fn=fn, deps=deps, dma=dma, idx=idx, signal=False, tok=None)
        self.ops.append(op)
        for k in reads:
            self.readers.setdefault(k, []).append(idx)
        for k in writes:
            self.last_w[k] = idx
            self.readers[k] = []
        if dma is not None:
            self.last_dma[dma] = idx
        else:
            self.last_on_eng[eng] = idx
        return idx

    def barrier(self):
        self.barrier_deps = set(self.last_on_eng.values()) | set(self.last_dma.values())

    def finalize(self):
        ops = self.ops
        for op in ops:
            for d in op["deps"]:
                p = ops[d]
                if p["dma"] is not None:
                    continue
                if p["eng"] == "pe" and op["eng"] == "pe" and op["dma"] is None:
                    continue
                p["signal"] = True
        cnt = {e: 0 for e in self.eng_sems}
        for op in ops:
            if op["dma"] is not None:
                key = op["dma"]
                if key not in self.dma_sems:
                    self.dma_sems[key] = self.sem("d_%d" % len(self.dma_sems))
                    self.dma_cnt[key] = 0
                self.dma_cnt[key] += 16
                op["tok"] = (self.dma_sems[key], self.dma_cnt[key])
            elif op["signal"]:
                cnt[op["eng"]] += 1
                op["tok"] = (self.eng_sems[op["eng"]], cnt[op["eng"]])
        streams = {e: [] for e in self.ENGS}
        for op in ops:
            streams[op["eng"]].append(op)
        sched = self

        def emit(engname, engine):
            known = {}
            for op in streams[engname]:
                need = {}
                for d in op["deps"]:
                    p = ops[d]
                    if p["tok"] is None:
                        continue
                    if p["dma"] is None and p["eng"] == "pe" and engname == "pe" and op["dma"] is None:
                        continue
                    s, v = p["tok"]
                    if need.get(id(s), (None, 0))[1] < v:
                        need[id(s)] = (s, v)
                for sid, (s, v) in need.items():
                    if known.get(sid, 0) >= v:
                        continue
                    engine.wait_ge(s, v)
                    known[sid] = v
                ins = op["fn"](engine)
                if op["dma"] is not None:
                    ins.then_inc(op["tok"][0], 16)
                elif op["signal"]:
                    ins.then_inc(op["tok"][0], 1)
            if engname == "sp":
                for key, s in sched.dma_sems.items():
                    engine.wait_ge(s, sched.dma_cnt[key])
                for e, s in sched.eng_sems.items():
                    if cnt[e] > 0:
                        engine.wait_ge(s, cnt[e])

        with self.nc.Block() as block:
            @block.tensor
            def _(e):
                emit("pe", e)

            @block.scalar
            def _(e):
                emit("act", e)

            @block.vector
            def _(e):
                emit("dve", e)

            @block.gpsimd
            def _(e):
                emit("pool", e)

            @block.sync
            def _(e):
                emit("sp", e)

    def close(self):
        while self._stack:
            self._stack.pop().__exit__(None, None, None)

    def dma(self, q, out, in_, reads, writes, key):
        return self.add(q, lambda e: e.dma_start(out=out, in_=in_), reads=reads, writes=writes, dma=key)

    def mms(self, specs, reads, writes):
        def fn(e):
            ins = None
            for (o, l, r, st, sp) in specs:
                ins = e.matmul(o, lhsT=l, rhs=r, start=st, stop=sp)
            return ins
        return self.add("pe", fn, reads=reads, writes=writes)

    def transposes(self, specs, ident, reads, writes):
        def fn(e):
            ins = None
            for (o, i) in specs:
                ins = e.transpose(out=o, in_=i, identity=ident)
            return ins
        return self.add("pe", fn, reads=reads, writes=writes)

    def act(self, out, in_, func, reads, writes, scale=1.0, bias=None, eng="act"):
        if bias is None:
            return self.add(eng, lambda e: e.activation(out=out, in_=in_, func=func, scale=scale), reads=reads, writes=writes)
        return self.add(eng, lambda e: e.activation(out=out, in_=in_, func=func, scale=scale, bias=bias), reads=reads, writes=writes)

    def tt(self, eng, out, in0, in1, op, reads, writes):
        return self.add(eng, lambda e: e.tensor_tensor(out=out, in0=in0, in1=in1, op=op), reads=reads, writes=writes)

    def ts(self, eng, out, in0, s1, s2, op0, op1, reads, writes):
        if s2 is None:
            return self.add(eng, lambda e: e.tensor_scalar(out=out, in0=in0, scalar1=s1, scalar2=None, op0=op0), reads=reads, writes=writes)
        return self.add(eng, lambda e: e.tensor_scalar(out=out, in0=in0, scalar1=s1, scalar2=s2, op0=op0, op1=op1), reads=reads, writes=writes)

    def stt(self, eng, out, in0, scalar, in1, op0, op1, reads, writes):
        return self.add(eng, lambda e: e.scalar_tensor_tensor(out=out, in0=in0, scalar=scalar, in1=in1, op0=op0, op1=op1), reads=reads, writes=writes)

    def copy(self, eng, out, in_, reads, writes):
        if eng == "act":
            return self.add(eng, lambda e: e.copy(out=out, in_=in_), reads=reads, writes=writes)
        return self.add(eng, lambda e: e.tensor_copy(out=out, in_=in_), reads=reads, writes=writes)


def build_program(n_layers=2, debug_out=None):
    nc = bass.Bass("TRN2", target_bir_lowering=False)

    def dram_in(name, shape, dt=F32):
        return nc.dram_tensor(name, list(shape), dt, kind="ExternalInput").ap()

    x_d = dram_in("x", [S_LEN, D])
    pos_d = dram_in("pos", [1, S_LEN], I32)
    w_ret = dram_in("w_ret", [4, D, 1536])
    w_mla = dram_in("w_mla", [D, 1792])
    w_uq = dram_in("w_uq", [256, 2048])
    w_ukv = dram_in("w_ukv", [256, 2048])
    w_oe = dram_in("w_oe", [3072, D])
    w_l1 = dram_in("w_l1", [D, 6144])
    w_oo = dram_in("w_oo", [D, D])
    qn_d = dram_in("qn", [128, 2])
    kvn_d = dram_in("kvn", [128, 2])
    lng_d = dram_in("lng", [2, D])
    lnb_d = dram_in("lnb", [2, D])
    c_ident = dram_in("c_ident", [128, 128])
    c_mask = dram_in("c_mask", [128, 128])
    c_vec = dram_in("c_vec", [128, 16])
    c_xi2 = dram_in("c_xi2", [128, 4])
    c_E = dram_in("c_E", [8, 1024])
    c_past = dram_in("c_past", [128, 128])
    c_own = dram_in("c_own", [128, 128])
    c_negoff = dram_in("c_negoff", [128, 128])
    out_d = nc.dram_tensor("out", [S_LEN, D], F32, kind="ExternalOutput").ap()

    S = Sched(nc)
    top = ExitStack()

    def sb(es, name, shape, dt):
        return es.enter_context(nc.sbuf_tensor("sb_" + name, list(shape), dt))

    yacc = sb(top, "yacc", [128, NT, D], F32)
    xT = sb(top, "xT", [128, 8, S_LEN], BF16)
    wring = sb(top, "wring", [128, 8, 1024], BF16)
    wo = sb(top, "wo", [128, 4, D], BF16)
    ident = sb(top, "ident", [128, 128], BF16)
    maskT = sb(top, "maskT", [128, 128], BF16)
    ones_b = sb(top, "ones_b", [128, 128], BF16)
    cvec = sb(top, "cvec", [128, 16], F32)
    xi2 = sb(top, "xi2", [128, 4], F32)
    qn = sb(top, "qn", [128, 2], F32)
    kvn = sb(top, "kvn", [128, 2], F32)
    Eall = sb(top, "Eall", [8, 1024], BF16)
    past01 = sb(top, "past01", [128, 128], F32)
    own01 = sb(top, "own01", [128, 128], F32)
    negoff = sb(top, "negoff", [128, 128], F32)
    tmpA = sb(top, "tmpA", [128, 512], F32)
    tmpB = sb(top, "tmpB", [128, 512], F32)
    tmpI = sb(top, "tmpI", [128, 512], I32)

    PA = top.enter_context(nc.psum_tensor("PA", [128, 1024], F32))
    PB = top.enter_context(nc.psum_tensor("PB", [128, 1024], F32))
    PC = top.enter_context(nc.psum_tensor("PC", [128, 1024], F32))
    PD = top.enter_context(nc.psum_tensor("PD", [128, 1024], F32))

    inv_r = cvec[:, 0:1]
    inv64 = cvec[:, 1:2]
    sgn64 = cvec[:, 2:3]
    eps_ln = cvec[:, 3:4]
    eps_rms = cvec[:, 4:5]

    ring_pos = [0]

    def load_w(src2d, ncols):
        nsl = ncols // 128
        s0 = ring_pos[0]
        if s0 + nsl > 8:
            s0 = 0
        ring_pos[0] = (s0 + nsl) % 8
        kc = src2d.shape[0] // 128
        keys = [("wr", s) for s in range(s0, s0 + nsl)]
        S.dma("pool", wring[:, 0:kc, s0 * 128:(s0 + nsl) * 128], src2d.rearrange("(c p) n -> p c n", p=128),
              reads=[], writes=keys, key=("wr", s0))
        return s0, keys

    def wslice(s0, kc, c0=0, ncols=128):
        return wring[:, kc, s0 * 128 + c0: s0 * 128 + c0 + ncols]

    S.dma("pool", ident[:], c_ident, [], ["ident"], "c0")
    S.dma("pool", maskT[:], c_mask, [], ["maskT"], "c1")
    S.dma("pool", Eall[:], c_E, [], ["Eall"], "c2")
    S.dma("sp", cvec[:], c_vec, [], ["cvec"], "c3")
    S.dma("sp", xi2[:], c_xi2, [], ["xi2"], "c4")
    S.dma("sp", qn[:], qn_d, [], ["qn"], "c5")
    S.dma("sp", kvn[:], kvn_d, [], ["kvn"], "c6")
    S.dma("sp", past01[:], c_past, [], ["past01"], "c7")
    S.dma("sp", own01[:], c_own, [], ["own01"], "c8")
    S.dma("sp", negoff[:], c_negoff, [], ["negoff"], "c9")
    S.add("pool", lambda e: e.memset(ones_b[:], 1.0), writes=["ones_b"])

    def gen_tables(tab, inv_ap, sgn_ap):
        for j in range(4):
            cs = slice(j * 512, (j + 1) * 512)
            S.dma("sp", tmpI[:], bass.AP(pos_d.tensor, j * 512, [[0, 128], [1, 512]]), [], ["tmpI"], "posld")
            S.copy("dve", tmpA[:], tmpI[:], ["tmpI"], ["tmpA"])
            S.ts("dve", tmpA[:], tmpA[:], inv_ap, None, ALU.mult, None, ["tmpA", "cvec"], ["tmpA"])
            for which in (1, 0):
                shift = 0.0 if which == 1 else 0.25
                S.ts("dve", tmpI[:], tmpA[:], 1.0 / (2 * math.pi), shift, ALU.mult, ALU.add, ["tmpA"], ["tmpI"])
                S.copy("dve", tmpB[:], tmpI[:], ["tmpI"], ["tmpB"])
                S.stt("dve", tmpB[:], tmpB[:], -2 * math.pi, tmpA[:], ALU.mult, ALU.add, ["tmpA", "tmpB"], ["tmpB"])
                if which == 0:
                    S.ts("dve", tmpB[:], tmpB[:], math.pi / 2, None, ALU.add, None, ["tmpB"], ["tmpB"])
                S.ts("dve", tmpB[:], tmpB[:], -3.141592, 3.141592, ALU.max, ALU.min, ["tmpB"], ["tmpB"])
                if which == 1 and sgn_ap is not None:
                    S.act(tab[:, 1, cs], tmpB[:], AF.Sin, ["tmpB", "cvec"], [("tab", j)], scale=sgn_ap)
                else:
                    S.act(tab[:, which, cs], tmpB[:], AF.Sin, ["tmpB"], [("tab", j)])

    with ExitStack() as es0:
        xs = sb(es0, "xs", [128, 2, D], F32)
        xb = sb(es0, "xb", [128, 2, D], BF16)
        for t in range(NT):
            s = t % 2
            S.dma("sp", xs[:, s, :], x_d[t * 128:(t + 1) * 128, :], [], [("xs", s)], ("xs", s))
            S.add("act", (lambda s=s, t=t: (lambda e: e.mul(out=yacc[:, t, :], in_=xs[:, s, :], mul=ALPHA)))(),
                  reads=[("xs", s)], writes=[("yacc", t)])
            S.copy("dve", xb[:, s, :], xs[:, s, :], [("xs", s)], [("xb", s)])
            pt = PA[:].bitcast(BF16) if s == 0 else PB[:].bitcast(BF16)
            S.transposes([(pt[:, c * 128:(c + 1) * 128], xb[:, s, c * 128:(c + 1) * 128]) for c in range(8)], ident[:],
                         reads=[("xb", s), "ident"], writes=[("P", s, 0)])
            S.copy("dve" if s == 0 else "act", xT[:, :, t * 128:(t + 1) * 128],
                   pt[:, 0:1024].rearrange("p (c k) -> p c k", k=128), [("P", s, 0)], [("xT", t)])
        S.barrier()
    XT_ALL = [("xT", t) for t in range(NT)]

    def xt_keys(j):
        return [("xT", 4 * j + i) for i in range(4)]

    def layer_norm(layer, last):
        with ExitStack() as es:
            gb = sb(es, "gb%d" % layer, [128, 2, D], F32)
            mv = sb(es, "mv%d" % layer, [128, NT, 2], F32)
            st = sb(es, "st%d" % layer, [128, 2, 6], F32)
            rstd = sb(es, "rstd%d" % layer, [128, NT], F32)
            nb = sb(es, "nb%d" % layer, [128, NT], F32)
            x1 = sb(es, "x1_%d" % layer, [128, 2, D], F32)
            x1b = sb(es, "x1b_%d" % layer, [128, 2, D], BF16)
            S.dma("sp", gb[:, 0, :], bass.AP(lng_d.tensor, layer * D, [[0, 128], [1, D]]), [], ["gb0"], "gb0")
            S.dma("sp", gb[:, 1, :], bass.AP(lnb_d.tensor, layer * D, [[0, 128], [1, D]]), [], ["gb1"], "gb1")
            for t in range(NT):
                for hh in range(2):
                    S.add("dve", (lambda t=t, hh=hh: (lambda e: e.bn_stats(out=st[:, hh, :], in_=yacc[:, t, hh * 512:(hh + 1) * 512])))(),
                          reads=[("yacc", t)], writes=[("st", hh)])
                S.add("dve", (lambda t=t: (lambda e: e.bn_aggr(out=mv[:, t, :], in_=st[:].rearrange("p a b -> p (a b)"))))(),
                      reads=[("st", 0), ("st", 1)], writes=["mv"])
            S.act(rstd[:], mv[:, :, 1], AF.Sqrt, ["mv", "cvec"], ["rstd"], bias=eps_ln)
            S.add("dve", lambda e: e.reciprocal(out=rstd[:], in_=rstd[:]), reads=["rstd"], writes=["rstd"])
            S.stt("dve", nb[:], mv[:, :, 0], -1.0, rstd[:], ALU.mult, ALU.mult, ["mv", "rstd"], ["nb"])
            for t in range(NT):
                s = t % 2
                S.act(x1[:, s, :], yacc[:, t, :], AF.Identity, [("yacc", t), "rstd", "nb"], [("x1", s), ("x1h", s)],
                      scale=rstd[:, t:t + 1], bias=nb[:, t:t + 1])
                S.tt("dve", x1[:, s, :], x1[:, s, :], gb[:, 0, :], ALU.mult, [("x1", s), "gb0"], [("x1", s), ("x1h", s)])
                S.tt("pool", x1[:, s, 640:1024], x1[:, s, 640:1024], gb[:, 1, 640:1024], ALU.add, [("x1h", s), "gb1"], [("x1h", s)])
                S.tt("dve", x1[:, s, 0:640], x1[:, s, 0:640], gb[:, 1, 0:640], ALU.add, [("x1", s), "gb1"], [("x1", s)])
                if last:
                    S.dma("sp", out_d[t * 128:(t + 1) * 128, :], x1[:, s, :], [("x1", s), ("x1h", s)], [("out", t)], ("out", s))
                else:
                    S.add("act", (lambda s=s, t=t: (lambda e: e.mul(out=yacc[:, t, :], in_=x1[:, s, :], mul=ALPHA)))(),
                          reads=[("x1", s), ("x1h", s)], writes=[("yacc", t)])
                    S.copy("act", x1b[:, s, :], x1[:, s, :], [("x1", s), ("x1h", s)], [("x1b", s)])
                    pt = PA[:].bitcast(BF16) if s == 0 else PB[:].bitcast(BF16)
                    S.transposes([(pt[:, c * 128:(c + 1) * 128], x1b[:, s, c * 128:(c + 1) * 128]) for c in range(8)], ident[:],
                                 reads=[("x1b", s), "ident"], writes=[("P", s, 0)])
                    S.copy("dve", xT[:, :, t * 128:(t + 1) * 128],
                           pt[:, 0:1024].rearrange("p (c k) -> p c k", k=128), [("P", s, 0)], [("xT", t)])
            S.barrier()

    def proj_fm(slot, wkc, rhs_fn, j, out_ps, pkey, rkeys, wkeys, c0=0):
        specs = []
        for kc in range(wkc):
            specs.append((out_ps, wslice(slot, kc, c0), rhs_fn(kc, j), kc == 0, kc == wkc - 1))
        S.mms(specs, reads=list(rkeys) + list(wkeys), writes=[pkey])

    def xT_rhs(kc, j):
        return xT[:, kc, j * 512:(j + 1) * 512]

    def yacc_accumulate(cat, nslots, catkey_fn, wo_keys, t):
        specs = []
        for half in range(2):
            for s in range(nslots):
                specs.append((PA[:, half * 512:(half + 1) * 512], cat[:, s, t * 128:(t + 1) * 128],
                              wo[:, s, half * 512:(half + 1) * 512], s == 0, s == nslots - 1))
        S.mms(specs, reads=[catkey_fn(t)] + list(wo_keys), writes=[("P", 0, 0), ("P", 0, 1)])
        S.tt("dve", yacc[:, t, :], yacc[:, t, :], PA[:], ALU.add, [("P", 0, 0), ("P", 0, 1), ("yacc", t)], [("yacc", t)])

    def attention_core(name, qk_list, bias_fn, Vfn, dv, scale, on_out, on_key, ebuf, rs_t):
        iters = [(jg, kt) for jg in range(4) for kt in range(4 * jg + 4)]

        def acc_bank(jg):
            return (PD, 3) if jg % 2 == 0 else (PA, 0)

        def emit_score(i):
            jg, kt = iters[i]
            c0 = max(0, kt - 4 * jg)
            q_lo = jg * 512 + c0 * 128
            q_hi = (jg + 1) * 512
            half = i % 2
            ps = PC[:, half * 512 + c0 * 128: (half + 1) * 512]
            specs = []
            rk = []
            items = list(qk_list)
            b = bias_fn(kt, jg, q_lo, q_hi) if bias_fn is not None else None
            if b is not None:
                items = items + [b]
            for ii, (kf, qf, keys) in enumerate(items):
                specs.append((ps, kf(kt), qf(q_lo, q_hi), ii == 0, ii == len(items) - 1))
                rk += keys
            S.mms(specs, reads=rk, writes=[("P", 2, half)])
            es_ = i % 3
            S.act(ebuf[:, es_, c0 * 128:512], ps, AF.Exp, [("P", 2, half)], [(name + "e", es_)], scale=scale)
            if kt >= 4 * jg:
                S.tt("dve", ebuf[:, es_, c0 * 128:(c0 + 1) * 128], ebuf[:, es_, c0 * 128:(c0 + 1) * 128], maskT[:], ALU.mult,
                     [(name + "e", es_), "maskT"], [(name + "e", es_)])

        def emit_pv(i):
            jg, kt = iters[i]
            c0 = max(0, kt - 4 * jg)
            es_ = i % 3
            PT, pk = acc_bank(jg)
            pv = []
            for qi in range(c0, 4):
                base = (qi // 2) * 512 + (qi % 2) * (dv + 1)
                pv.append((PT[:, base: base + dv + 1], ebuf[:, es_, qi * 128:(qi + 1) * 128], Vfn(kt),
                           kt == 0 and qi % 2 == 0, kt == 4 * jg + qi and qi % 2 == 1))
            S.mms(pv, reads=[(name + "e", es_), name + "V"], writes=[("P", pk, 0), ("P", pk, 1)])
            if kt == 4 * jg + 3:
                for qi in range(4):
                    base = (qi // 2) * 512 + (qi % 2) * (dv + 1)
                    t = jg * 4 + qi
                    rsl = rs_t[:, (jg % 2) * 4 + qi:(jg % 2) * 4 + qi + 1]
                    S.add("dve", (lambda base=base, rsl=rsl, PT=PT: (lambda e: e.reciprocal(out=rsl, in_=PT[:, base + dv: base + dv + 1])))(),
                          reads=[("P", pk, 0), ("P", pk, 1)], writes=[(name + "rs", jg % 2, qi)])
                    S.act(on_out(t), PT[:, base: base + dv], AF.Copy, [("P", pk, 0), ("P", pk, 1), (name + "rs", jg % 2, qi)], [on_key(t)],
                          scale=rsl)

        emit_score(0)
        for i in range(len(iters)):
            if i + 1 < len(iters):
                emit_score(i + 1)
            emit_pv(i)

    with ExitStack() as esR:
        tabR = sb(esR, "tabR", [128, 2, S_LEN], F32)
        qT = sb(esR, "qT", [128, 2, S_LEN], BF16)
        kT = sb(esR, "kT", [128, 2, S_LEN], BF16)
        vt = sb(esR, "vt", [128, NT, 512], BF16)
        gT = sb(esR, "gT", [128, 4, S_LEN], BF16)
        st32 = sb(esR, "st32", [128, 2, 512], F32)
        stb = sb(esR, "stb", [128, 2, 512], BF16)
        ktok = sb(esR, "ktok", [128, 2, 256], BF16)
        innT = sb(esR, "innT", [128, 2, 128], BF16)
        onb = sb(esR, "onb", [128, 2, 512], BF16)
        catc = sb(esR, "catc", [128, 2, 4, 128], BF16)
        bst = sb(esR, "bst", [128, 2, 6], F32)
        bmv = sb(esR, "bmv", [128, 2, 2], F32)
        sm = sb(esR, "sm", [128, 2, 4], F32)

        gen_tables(tabR, inv_r, None)
        TAB = [("tab", j) for j in range(4)]

        for h in range(4):
            g = 1.0 - 2.0 ** (-5.0 - h)
            gC = g ** 128
            xi_h = cvec[:, 8 + h:9 + h]
            vs_h = cvec[:, 12 + h:13 + h]
            for qi, dst in enumerate((qT, kT)):
                dname = "qT" if qi == 0 else "kT"
                sE, kE = load_w(w_ret[h, :, qi * 256: qi * 256 + 128], 128)
                sO, kO = load_w(w_ret[h, :, qi * 256 + 128: qi * 256 + 256], 128)
                for j in range(4):
                    hf = j % 2
                    psA = PA[:, hf * 512:(hf + 1) * 512]
                    psB = PB[:, hf * 512:(hf + 1) * 512]
                    proj_fm(sE, 8, xT_rhs, j, psA, ("P", 0, hf), xt_keys(j), kE)
                    proj_fm(sO, 8, xT_rhs, j, psB, ("P", 1, hf), xt_keys(j), kO)
                    cs = slice(j * 512, (j + 1) * 512)
                    cosj = tabR[:, 0, cs]
                    sinj = tabR[:, 1, cs]
                    S.tt("dve", tmpA[:], psA, cosj, ALU.mult, [("P", 0, hf), ("tab", j)], ["tmpA"])
                    S.tt("dve", tmpB[:], psB, sinj, ALU.mult, [("P", 1, hf), ("tab", j)], ["tmpB"])
                    S.tt("dve", dst[:, 0, cs], tmpA[:], tmpB[:], ALU.subtract, ["tmpA", "tmpB"], [(dname, 0, j)])
                    S.tt("dve", tmpA[:], psB, cosj, ALU.mult, [("P", 1, hf), ("tab", j)], ["tmpA"])
                    S.tt("dve", tmpB[:], psA, sinj, ALU.mult, [("P", 0, hf), ("tab", j)], ["tmpB"])
                    S.tt("dve", dst[:, 1, cs], tmpA[:], tmpB[:], ALU.add, ["tmpA", "tmpB"], [(dname, 1, j)])
            sV, kV = load_w(w_ret[h, :, 512:1024], 512)
            for t in range(NT):
                hf = t % 2
                ps = PA[:, hf * 512:(hf + 1) * 512]
                S.mms([(ps, xT[:, kc, t * 128:(t + 1) * 128], wring[:, kc, sV * 128:(sV + 4) * 128], kc == 0, kc == 7) for kc in range(8)],
                      reads=[("xT", t)] + kV, writes=[("P", 0, hf)])
                S.act(vt[:, t, :], ps, AF.Copy, [("P", 0, hf), "cvec"], [("vt", t)], scale=vs_h)
            sG, kG = load_w(w_ret[h, :, 1024:1536], 512)
            for c in range(4):
                for j in range(4):
                    hf = j % 2
                    ps = PB[:, hf * 512:(hf + 1) * 512]
                    proj_fm(sG, 8, xT_rhs, j, ps, ("P", 1, hf), xt_keys(j), kG, c0=c * 128)
                    S.act(gT[:, c, j * 512:(j + 1) * 512], ps, AF.Silu, [("P", 1, hf)], [("gT", c, j)])
            S.dma("pool", wo[:], w_oe[h * 512:(h + 1) * 512, :].rearrange("(c p) n -> p c n", p=128), [], ["wo"], "wo")
            PDb = PD[:].bitcast(BF16)

            def o_bank(n):
                return (PC[:, 512:1024], ("P", 2, 1)) if n % 2 == 0 else (PB[:, 512:1024], ("P", 1, 1))

            def head_part(n, h=h, gC=gC, xi_h=xi_h):
                j = n // 4
                cs = slice(n * 128, (n + 1) * 128)
                d2 = n % 2
                qk = [("qT", 0, j), ("qT", 1, j)]
                kk = [("kT", 0, j), ("kT", 1, j)]
                st_ps = PC[:, d2 * 128:(d2 + 1) * 128]
                S.mms([(st_ps, kT[:, c, cs], qT[:, c, cs], c == 0, c == 1) for c in range(2)], reads=qk + kk, writes=[("P", 2, 0, d2)])
                S.tt("dve", innT[:, d2, :], st_ps, maskT[:], ALU.mult, [("P", 2, 0, d2), "maskT"], [("innT", d2)])
                if n < NT - 1:
                    kt_ps = PDb[:, 0:256]
                    S.transposes([(kt_ps[:, c * 128:(c + 1) * 128], kT[:, c, cs]) for c in range(2)], ident[:],
                                 reads=kk + ["ident"], writes=[("P", 3, 0, "kt")])
                    S.act(ktok[:, d2, :], kt_ps, AF.Copy, [("P", 3, 0, "kt")], [("ktok", d2)], scale=gC)
                o_ps, o_key = o_bank(n)
                specs = [(o_ps, innT[:, d2, :], vt[:, n, :], True, n == 0)]
                rk = [("innT", d2), ("vt", n)]
                if n > 0:
                    for c in range(2):
                        specs.append((o_ps, qT[:, c, cs], stb[:, c, :], False, c == 1))
                    rk += qk + [("stb", 0), ("stb", 1)]
                S.mms(specs, reads=rk, writes=[o_key])
                if n < NT - 1:
                    S.mms([(PD[:, 512:1024], ktok[:, d2, 0:128], vt[:, n, :], True, True)], reads=[("ktok", d2), ("vt", n)], writes=[("P", 3, 1)])
                    S.mms([(PB[:, 0:512], ktok[:, d2, 128:256], vt[:, n, :], True, True)], reads=[("ktok", d2), ("vt", n)], writes=[("P", 1, 0)])
                    if n == 0:
                        S.copy("dve", st32[:, 0, :], PD[:, 512:1024], [("P", 3, 1)], [("st32", 0)])
                        S.copy("dve", st32[:, 1, :], PB[:, 0:512], [("P", 1, 0)], [("st32", 1)])
                    else:
                        S.stt("dve", st32[:, 0, :], st32[:, 0, :], gC, PD[:, 512:1024], ALU.mult, ALU.add, [("P", 3, 1), ("st32", 0)], [("st32", 0)])
                        S.stt("dve", st32[:, 1, :], st32[:, 1, :], gC, PB[:, 0:512], ALU.mult, ALU.add, [("P", 1, 0), ("st32", 1)], [("st32", 1)])
                    S.copy("act", stb[:, 0, :], st32[:, 0, :], [("st32", 0)], [("stb", 0)])
                    S.copy("pool", stb[:, 1, :], st32[:, 1, :], [("st32", 1)], [("stb", 1)])
                S.add("dve", (lambda o_ps=o_ps: (lambda e: e.bn_stats(out=bst[:, d2, :], in_=o_ps)))(), reads=[o_key], writes=[("bst", d2)])
                S.add("dve", lambda e: e.bn_aggr(out=bmv[:, d2, :], in_=bst[:, d2, :]), reads=[("bst", d2)], writes=[("bmv", d2)])
                S.ts("dve", sm[:, d2, 0:1], bmv[:, d2, 1:2], xi2[:, h:h + 1], LN_EPS, ALU.mult, ALU.add, [("bmv", d2), "xi2"], [("sm", d2, 0)])
                S.act(sm[:, d2, 1:2], sm[:, d2, 0:1], AF.Sqrt, [("sm", d2, 0)], [("sm", d2, 1)])
                S.add("dve", lambda e: e.reciprocal(out=sm[:, d2, 1:2], in_=sm[:, d2, 1:2]), reads=[("sm", d2, 1)], writes=[("sm", d2, 1)])
                S.ts("dve", sm[:, d2, 2:3], sm[:, d2, 1:2], xi_h, None, ALU.mult, None, [("sm", d2, 1), "cvec"], [("sm", d2, 2)])
                S.stt("dve", sm[:, d2, 3:4], bmv[:, d2, 0:1], -1.0, sm[:, d2, 2:3], ALU.mult, ALU.mult, [("bmv", d2), ("sm", d2, 2)], [("sm", d2, 3)])
                S.act(onb[:, d2, :], o_ps, AF.Identity, [o_key, ("sm", d2, 2), ("sm", d2, 3)], [("onb", d2)],
                      scale=sm[:, d2, 2:3], bias=sm[:, d2, 3:4])

            def tail_part(n):
                j = n // 4
                cs = slice(n * 128, (n + 1) * 128)
                d2 = n % 2
                ct_ps = PDb[:, 512:1024]
                S.transposes([(ct_ps[:, c * 128:(c + 1) * 128], onb[:, d2, c * 128:(c + 1) * 128]) for c in range(4)], ident[:],
                             reads=[("onb", d2), "ident"], writes=[("P", 3, 0, "ct")])
                S.tt("dve", catc[:, d2, :, :], ct_ps.rearrange("p (c k) -> p c k", k=128), gT[:, :, cs], ALU.mult,
                     [("P", 3, 0, "ct")] + [("gT", c, j) for c in range(4)], [("catc", d2)])
                specs = []
                for half in range(2):
                    for c in range(4):
                        specs.append((PA[:, half * 512:(half + 1) * 512], catc[:, d2, c, :], wo[:, c, half * 512:(half + 1) * 512], c == 0, c == 3))
                S.mms(specs, reads=[("catc", d2), "wo"], writes=[("P", 0, 0), ("P", 0, 1)])
                S.tt("dve", yacc[:, n, :], yacc[:, n, :], PA[:], ALU.add, [("P", 0, 0), ("P", 0, 1), ("yacc", n)], [("yacc", n)])

            head_part(0)
            for n in range(NT):
                if n + 1 < NT:
                    head_part(n + 1)
                tail_part(n)
        S.barrier()

    if debug_out == "ret":
        for t in range(NT):
            S.dma("sp", out_d[t * 128:(t + 1) * 128, :], yacc[:, t, :], [("yacc", t)], [("out", t)], "out")
        S.finalize(); top.close(); S.close()
        return nc

    with ExitStack() as esT:
        tab64 = sb(esT, "tab64", [128, 2, S_LEN], F32)
        gen_tables(tab64, inv64, sgn64)
        ebuf = sb(esT, "ebuf", [128, 3, 512], BF16)
        rs_t = sb(esT, "rs_t", [128, 8], F32)
        S.barrier()

        def rope_pair(psA, psB, kA, kB, j, out_ap, out_key, eng_out="dve"):
            cs = slice(j * 512, (j + 1) * 512)
            S.tt("dve", tmpA[:], psA, tab64[:, 0, cs], ALU.mult, [kA, ("tab", j)], ["tmpA"])
            S.tt("dve", tmpB[:], psB, tab64[:, 1, cs], ALU.mult, [kB, ("tab", j)], ["tmpB"])
            S.tt("dve", out_ap, tmpA[:], tmpB[:], ALU.add, ["tmpA", "tmpB"], [out_key])

        with ExitStack() as esM:
            mqT = sb(esM, "mqT", [128, 2, S_LEN], BF16)
            mkvT = sb(esM, "mkvT", [128, 2, S_LEN], BF16)
            kpeT = sb(esM, "kpeT", [128, S_LEN], BF16)
            qpz = [sb(esM, "qpz%d" % i, [128, S_LEN], BF16) for i in range(2)]
            S.add("pool", lambda e: e.memset(qpz[0][64:128, :], 0.0), writes=[("qpz", 0, j) for j in range(4)])
            S.add("pool", lambda e: e.memset(qpz[1][0:64, :], 0.0), writes=[("qpz", 1, j) for j in range(4)])
            qnT = sb(esM, "qnT", [128, S_LEN], BF16)
            knT = sb(esM, "knT", [128, S_LEN], BF16)
            Vaug = sb(esM, "Vaug", [128, NT, 129], BF16)
            gmT = sb(esM, "gmT", [128, S_LEN], BF16)
            catM = sb(esM, "catM", [128, 2, S_LEN], BF16)
            ovl = sb(esM, "ovl", [128, S_LEN], BF16)
            onM = ovl[:].rearrange("p (t k) -> p t k", k=128)
            rawg = ovl[:].bitcast(F32).rearrange("p (c k) -> p c k", k=512)
            sqb = ebuf[:, 0:2, :]
            S.add("pool", lambda e: e.memset(Vaug[:, :, 128:129], 1.0), writes=["VaugOnes"])

            for which, (dstT, gvec, col0) in enumerate(((mqT, qn, 0), (mkvT, kvn, 256))):
                dn = "mqT" if which == 0 else "mkvT"
                s0, k0 = load_w(w_mla[:, col0:col0 + 128], 128)
                s1, k1 = load_w(w_mla[:, col0 + 128:col0 + 256], 128)
                for j in range(4):
                    hf = j % 2
                    psl = [PA[:, hf * 512:(hf + 1) * 512], PB[:, hf * 512:(hf + 1) * 512]]
                    pk = [("P", 0, hf), ("P", 1, hf)]
                    proj_fm(s0, 8, xT_rhs, j, psl[0], pk[0], xt_keys(j), k0)
                    proj_fm(s1, 8, xT_rhs, j, psl[1], pk[1], xt_keys(j), k1)
                    for c in range(2):
                        S.act(sqb[:, c, :], psl[c], AF.Square, [pk[c]], [("sqb", c)])
                        S.act(rawg[:, c, :], psl[c], AF.Copy, [pk[c], "qn", "kvn"], [("rawg", c)], scale=gvec[:, c:c + 1])
                    ss_ps = PC[:, 0:512]
                    S.mms([(ss_ps, ones_b[:], sqb[:, c, :], c == 0, c == 1) for c in range(2)],
                          reads=[("sqb", 0), ("sqb", 1), "ones_b"], writes=[("P", 2, 0)])
                    S.act(tmpA[:], ss_ps, AF.Sqrt, [("P", 2, 0), "cvec"], ["tmpA"], scale=1.0 / 256.0, bias=eps_rms)
                    S.add("dve", lambda e: e.reciprocal(out=tmpA[:], in_=tmpA[:]), reads=["tmpA"], writes=["tmpA"])
                    for c in range(2):
                        S.tt("dve", dstT[:, c, j * 512:(j + 1) * 512], rawg[:, c, :], tmpA[:], ALU.mult,
                             [("rawg", c), "tmpA"], [(dn, j)])
            MQ = [("mqT", j) for j in range(4)]
            MKV = [("mkvT", j) for j in range(4)]
            sA, kA_ = load_w(w_mla[:, 512:640], 128)
            sB, kB_ = load_w(w_mla[:, 640:768], 128)
            for j in range(4):
                hf = j % 2
                psA = PA[:, hf * 512:(hf + 1) * 512]
                psB = PB[:, hf * 512:(hf + 1) * 512]
                proj_fm(sA, 8, xT_rhs, j, psA, ("P", 0, hf), xt_keys(j), kA_)
                proj_fm(sB, 8, xT_rhs, j, psB, ("P", 1, hf), xt_keys(j), kB_)
                rope_pair(psA, psB, ("P", 0, hf), ("P", 1, hf), j, kpeT[:, j * 512:(j + 1) * 512], ("kpeT", j))
            KPE = [("kpeT", j) for j in range(4)]
            S.barrier()

            def mq_rhs(kc, j):
                return mqT[:, kc, j * 512:(j + 1) * 512]

            def mkv_rhs(kc, j):
                return mkvT[:, kc, j * 512:(j + 1) * 512]

            for hp in range(4):
                sA, kA_ = load_w(w_uq[:, 1024 + hp * 128: 1024 + (hp + 1) * 128], 128)
                sB, kB_ = load_w(w_uq[:, 1536 + hp * 128: 1536 + (hp + 1) * 128], 128)
                for j in range(4):
                    hf = j % 2
                    psA = PA[:, hf * 512:(hf + 1) * 512]
                    psB = PB[:, hf * 512:(hf + 1) * 512]
                    proj_fm(sA, 2, mq_rhs, j, psA, ("P", 0, hf), [("mqT", j)], kA_)
                    proj_fm(sB, 2, mq_rhs, j, psB, ("P", 1, hf), [("mqT", j)], kB_)
                    cs = slice(j * 512, (j + 1) * 512)
                    S.tt("dve", tmpA[:], psA, tab64[:, 0, cs], ALU.mult, [("P", 0, hf), ("tab", j)], ["tmpA"])
                    S.tt("dve", tmpB[:], psB, tab64[:, 1, cs], ALU.mult, [("P", 1, hf), ("tab", j)], ["tmpB"])
                    S.tt("dve", qpz[0][0:64, cs], tmpA[0:64, :], tmpB[0:64, :], ALU.add, ["tmpA", "tmpB"], [("qpz", 0, j)])
                    S.tt("dve", qpz[1][64:128, cs], tmpA[64:128, :], tmpB[64:128, :], ALU.add, ["tmpA", "tmpB"], [("qpz", 1, j)])
                S.dma("pool", wo[:, 0:2, :], w_oe[2048 + hp * 256: 2048 + (hp + 1) * 256, :].rearrange("(c p) n -> p c n", p=128),
                      [], ["wo"], "wo")
                for e2 in range(2):
                    h = hp * 2 + e2
                    r0 = 64 * e2
                    sQ, kQ = load_w(w_uq[:, h * 128:(h + 1) * 128], 128)
                    sK, kK = load_w(w_ukv[:, h * 128:(h + 1) * 128], 128)
                    sVv, kVv = load_w(w_ukv[:, 1024 + h * 128: 1024 + (h + 1) * 128], 128)
                    sG, kG = load_w(w_mla[:, 768 + h * 128: 768 + (h + 1) * 128], 128)
                    for j in range(4):
                        hf = j % 2
                        psA = PA[:, hf * 512:(hf + 1) * 512]
                        psB = PB[:, hf * 512:(hf + 1) * 512]
                        proj_fm(sQ, 2, mq_rhs, j, psA, ("P", 0, hf), [("mqT", j)], kQ)
                        S.copy("act", qnT[:, j * 512:(j + 1) * 512], psA, [("P", 0, hf)], [("qnT", j)])
                        proj_fm(sK, 2, mkv_rhs, j, psB, ("P", 1, hf), [("mkvT", j)], kK)
                        S.copy("act", knT[:, j * 512:(j + 1) * 512], psB, [("P", 1, hf)], [("knT", j)])
                    for j in range(4):
                        hf = j % 2
                        psA = PA[:, hf * 512:(hf + 1) * 512]
                        proj_fm(sG, 8, xT_rhs, j, psA, ("P", 0, hf), xt_keys(j), kG)
                        S.act(gmT[:, j * 512:(j + 1) * 512], psA, AF.Silu, [("P", 0, hf)], [("gmT", j)])
                    for t in range(NT):
                        hf = t % 2
                        ps = PB[:, hf * 512: hf * 512 + 128]
                        S.mms([(ps, mkvT[:, kc, t * 128:(t + 1) * 128], wslice(sVv, kc), kc == 0, kc == 1) for kc in range(2)],
                              reads=[("mkvT", t // 4)] + kVv, writes=[("P", 1, hf)])
                        S.copy("act", Vaug[:, t, 0:128], ps, [("P", 1, hf), "VaugOnes"], ["mV"])
                    QN = [("qnT", j) for j in range(4)]
                    KN = [("knT", j) for j in range(4)]
                    qk_list = [
                        (lambda kt: knT[:, kt * 128:(kt + 1) * 128], lambda lo, hi: qnT[:, lo:hi], QN + KN),
                        (lambda kt: kpeT[:, kt * 128:(kt + 1) * 128], lambda lo, hi, e2=e2: qpz[e2][:, lo:hi],
                         [("qpz", e2, j) for j in range(4)] + KPE),
                    ]
                    attention_core("m", qk_list, None, lambda kt: Vaug[:, kt, :], 128, 192.0 ** -0.5,
                                   lambda t: onM[:, t, :], lambda t: ("onM", t), ebuf, rs_t)
                    for jg in range(4):
                        ct_ps = PB[:].bitcast(BF16)[:, 0:512]
                        S.transposes([(ct_ps[:, i * 128:(i + 1) * 128], onM[:, jg * 4 + i, :]) for i in range(4)], ident[:],
                                     reads=[("onM", jg * 4 + i) for i in range(4)] + ["ident"], writes=[("P", 1, 0)])
                        S.tt("dve", catM[:, e2, jg * 512:(jg + 1) * 512], ct_ps, gmT[:, jg * 512:(jg + 1) * 512], ALU.mult,
                             [("P", 1, 0), ("gmT", jg)], [("catM", e2, jg)])
                for t in range(NT):
                    yacc_accumulate(catM, 2, lambda t: ("catM", 0, t // 4), ["wo"] + [("catM", 1, j) for j in range(4)], t)
            S.barrier()

        layer_norm(0, last=(n_layers == 1))
        if n_layers == 1:
            S.finalize(); esT.close(); top.close(); S.close()
            return nc

        with ExitStack() as esL:
            qTl = sb(esL, "qTl", [128, S_LEN], BF16)
            kmh = sb(esL, "kmh", [128, 8], BF16)
            kml = sb(esL, "kml", [128, 8], BF16)
            kmr = sb(esL, "kmr", [128, 8], F32)
            qTb = sb(esL, "qTb", [128, S_LEN], BF16)
            kTb = sb(esL, "kTb", [128, S_LEN], BF16)
            kmean = sb(esL, "kmean", [128, 8], F32)
            Vp = sb(esL, "Vp", [128, NT, 2, 65], BF16)
            gmT = sb(esL, "gmT1", [128, S_LEN], BF16)
            nselT = sb(esL, "nselT", [8, 2, S_LEN], BF16)
            catL = sb(esL, "catL", [128, 2, S_LEN], BF16)
            onL = sb(esL, "onL", [128, NT, 128], BF16)
            gm = sb(esL, "gm", [128, 128], F32)
            sel = sb(esL, "sel", [128, 128], F32)
            top8 = sb(esL, "top8", [128, 8], F32)
            biasb = sb(esL, "biasb", [128, 128], BF16)
            S.add("pool", lambda e: e.memset(Vp[:, :, :, 64:65], 1.0), writes=["VpOnes"])

            for hp in range(8):
                sl = hp % 2
                for qi in range(2):
                    base = qi * 2048
                    sA, kA_ = load_w(w_l1[:, base + hp * 128: base + (hp + 1) * 128], 128)
                    sB, kB_ = load_w(w_l1[:, base + 1024 + hp * 128: base + 1024 + (hp + 1) * 128], 128)
                    for j in range(4):
                        hf = j % 2
                        psA = PA[:, hf * 512:(hf + 1) * 512]
                        psB = PB[:, hf * 512:(hf + 1) * 512]
                        proj_fm(sA, 8, xT_rhs, j, psA, ("P", 0, hf), xt_keys(j), kA_)
                        proj_fm(sB, 8, xT_rhs, j, psB, ("P", 1, hf), xt_keys(j), kB_)
                        cs = slice(j * 512, (j + 1) * 512)
                        if qi == 0:
                            S.tt("dve", tmpA[:], psA, tab64[:, 0, cs], ALU.mult, [("P", 0, hf), ("tab", j)], ["tmpA"])
                            S.tt("dve", tmpB[:], psB, tab64[:, 1, cs], ALU.mult, [("P", 1, hf), ("tab", j)], ["tmpB"])
                            S.tt("dve", tmpA[:], tmpA[:], tmpB[:], ALU.add, ["tmpA", "tmpB"], ["tmpA"])
                            S.copy("act", qTb[:, cs], tmpA[:], ["tmpA"], [("qTb", j)])
                            S.tt("dve", qTl[:, cs], tmpA[:], qTb[:, cs], ALU.subtract, ["tmpA", ("qTb", j)], [("qTl", j)])
                        else:
                            S.tt("dve", tmpA[:], psA, tab64[:, 0, cs], ALU.mult, [("P", 0, hf), ("tab", j)], ["tmpA"])
                            S.tt("dve", tmpB[:], psB, tab64[:, 1, cs], ALU.mult, [("P", 1, hf), ("tab", j)], ["tmpB"])
                            S.tt("dve", tmpA[:], tmpA[:], tmpB[:], ALU.add, ["tmpA", "tmpB"], ["tmpA"])
                            S.copy("act", kTb[:, cs], tmpA[:], ["tmpA"], [("kTb", j)])
                            S.add("dve", (lambda j=j: (lambda e: e.tensor_reduce(out=kmean[:, 2 * j:2 * j + 2],
                                                                                 in_=tmpA[:].rearrange("p (b l) -> p b l", l=256),
                                                                                 axis=AX.X, op=ALU.add)))(),
                                  reads=["tmpA"], writes=["kmean"])
                QB = [("qTb", j) for j in range(4)]
                QL = [("qTl", j) for j in range(4)]
                S.copy("dve", kmh[:], kmean[:], ["kmean"], ["kmh"])
                S.tt("dve", kmr[:], kmean[:], kmh[:], ALU.subtract, ["kmean", "kmh"], ["kmr"])
                S.copy("dve", kml[:], kmr[:], ["kmr"], ["kml"])
                KB = [("kTb", j) for j in range(4)]
                sVv, kVv = load_w(w_l1[:, 4096 + hp * 128: 4096 + (hp + 1) * 128], 128)
                sG, kG = load_w(w_l1[:, 5120 + hp * 128: 5120 + (hp + 1) * 128], 128)
                for t in range(NT):
                    hf = t % 2
                    ps = PB[:, hf * 512: hf * 512 + 128]
                    S.mms([(ps, xT[:, kc, t * 128:(t + 1) * 128], wslice(sVv, kc), kc == 0, kc == 7) for kc in range(8)],
                          reads=[("xT", t)] + kVv, writes=[("P", 1, hf)])
                    S.copy("act", Vp[:, t, :, 0:64], ps.rearrange("p (a b) -> p a b", b=64), [("P", 1, hf), "VpOnes"], ["lV"])
                for j in range(4):
                    hf = j % 2
                    psA = PA[:, hf * 512:(hf + 1) * 512]
                    proj_fm(sG, 8, xT_rhs, j, psA, ("P", 0, hf), xt_keys(j), kG)
                    S.act(gmT[:, j * 512:(j + 1) * 512], psA, AF.Silu, [("P", 0, hf)], [("gmT", j)])
                if hp % 2 == 0:
                    S.dma("pool", wo[:, 0:2, :], w_oo[hp * 128:(hp + 2) * 128, :].rearrange("(c p) n -> p c n", p=128), [], ["wo"], "wo")
                for e2 in range(2):
                    r0 = 64 * e2
                    g_ps = PC[:, 0:128]
                    gspecs = []
                    for t in range(NT):
                        ts_ = slice(t * 128, (t + 1) * 128)
                        gspecs.append((g_ps[:, t * 8:(t + 1) * 8], qTb[r0:r0 + 64, ts_], kmh[r0:r0 + 64, :], True, False))
                        gspecs.append((g_ps[:, t * 8:(t + 1) * 8], qTl[r0:r0 + 64, ts_], kmh[r0:r0 + 64, :], False, False))
                        gspecs.append((g_ps[:, t * 8:(t + 1) * 8], qTb[r0:r0 + 64, ts_], kml[r0:r0 + 64, :], False, True))
                    S.mms(gspecs, reads=QB + QL + ["kmh", "kml"], writes=[("P", 2, 0)])
                    S.tt("dve", gm[:], g_ps, past01[:], ALU.mult, [("P", 2, 0), "past01"], ["gm"])
                    S.tt("dve", gm[:], gm[:], negoff[:], ALU.add, ["gm", "negoff"], ["gm"])
                    for t in range(NT):
                        S.add("dve", (lambda t=t: (lambda e: e.max(out=top8[:], in_=gm[:, t * 8:(t + 1) * 8])))(), reads=["gm"], writes=["top8"])
                        S.ts("dve", sel[:, t * 8:(t + 1) * 8], gm[:, t * 8:(t + 1) * 8], top8[:, 2:3], None, ALU.is_ge, None, ["gm", "top8"], ["sel"])
                    S.tt("dve", sel[:], sel[:], past01[:], ALU.mult, ["sel", "past01"], ["sel"])
                    S.tt("dve", sel[:], sel[:], own01[:], ALU.add, ["sel", "own01"], ["sel"])
                    S.ts("dve", biasb[:], sel[:], NEGB, -NEGB, ALU.mult, ALU.add, ["sel"], ["biasb"])
                    bt_ps = PC[:].bitcast(BF16)[0:8, 1024:2048]
                    for q4 in range(4):
                        S.transposes([(bt_ps[:, i * 128:(i + 1) * 128], biasb[:, (q4 * 4 + i) * 8:(q4 * 4 + i + 1) * 8]) for i in range(4)], ident[:],
                                     reads=["biasb", "ident"], writes=[("P", 2, 1)])
                        S.copy("act", nselT[:, e2, q4 * 512:(q4 + 1) * 512], bt_ps[:, 0:512], [("P", 2, 1)], [("nselT", e2, q4)])
                    NS = [("nselT", e2, j) for j in range(4)]

                    def bias_fn(kt, jg, lo, hi, e2=e2, NS=NS):
                        nb_ = kt // 2
                        if nb_ == 2 * jg + 1:
                            return None
                        return (lambda kt_, nb_=nb_: Eall[0:8, nb_ * 128:(nb_ + 1) * 128],
                                lambda lo_, hi_, e2=e2: nselT[0:8, e2, lo_:hi_], NS + ["Eall"])

                    qk_list = [(lambda kt, r0=r0: kTb[r0:r0 + 64, kt * 128:(kt + 1) * 128],
                                lambda lo, hi, r0=r0: qTb[r0:r0 + 64, lo:hi], QB + KB)]
                    attention_core("l", qk_list, bias_fn, (lambda kt, e2=e2: Vp[:, kt, e2, :]), 64, 0.125,
                                   (lambda t, e2=e2: onL[:, t, e2 * 64:(e2 + 1) * 64]), (lambda t, e2=e2: ("onL", t, e2)), ebuf, rs_t)
                for jg in range(4):
                    ct_ps = PB[:].bitcast(BF16)[:, 0:512]
                    S.transposes([(ct_ps[:, i * 128:(i + 1) * 128], onL[:, jg * 4 + i, :]) for i in range(4)], ident[:],
                                 reads=[("onL", jg * 4 + i, e) for i in range(4) for e in range(2)] + ["ident"], writes=[("P", 1, 0)])
                    S.tt("dve", catL[:, sl, jg * 512:(jg + 1) * 512], ct_ps, gmT[:, jg * 512:(jg + 1) * 512], ALU.mult,
                         [("P", 1, 0), ("gmT", jg)], [("catL", sl, jg)])
                if hp % 2 == 1:
                    for t in range(NT):
                        yacc_accumulate(catL, 2, lambda t: ("catL", 0, t // 4), ["wo"] + [("catL", 1, j) for j in range(4)], t)
            S.barrier()
        layer_norm(1, last=True)

    S.finalize()
    top.close()
    S.close()
    return nc


def _host_constants():
    c = {}
    c["c_ident"] = np.eye(128, dtype=np.float32)
    k = np.arange(128)
    c["c_mask"] = (k[None, :] >= k[:, None]).astype(np.float32)
    vec = np.zeros((128, 16), np.float32)
    vec[:, 0] = 1.0 / (10000.0 ** np.linspace(0.0, 1.0, 128, dtype=np.float32))
    inv64 = 1.0 / (10000.0 ** (np.arange(0, 64, 2, dtype=np.float32) / 64.0))
    vec[:, 1] = inv64[np.arange(128) % 32]
    vec[:, 2] = np.where((np.arange(128) % 64) < 32, -1.0, 1.0)
    vec[:, 3] = LN_EPS
    vec[:, 4] = RMS_EPS
    xi2 = np.zeros((128, 4), np.float32)
    for h in range(4):
        g = 1.0 - 2.0 ** (-5.0 - h)
        xi = (g ** (k + 1.0)) * (256.0 ** -0.5)
        vec[:, 8 + h] = xi
        vec[:, 12 + h] = g ** (-(k + 1.0))
        xi2[:, h] = xi * xi
    c["c_vec"] = vec
    c["c_xi2"] = xi2
    E = np.zeros((8, 8, 128), np.float32)
    for n in range(8):
        E[n, n, :] = 1.0
    c["c_E"] = E.reshape(8, 1024)
    past = np.zeros((16, 8), np.float32)
    own = np.zeros((16, 8), np.float32)
    for t in range(16):
        qb = t // 2
        past[t, :qb] = 1.0
        own[t, qb] = 1.0
    c["c_past"] = np.broadcast_to(past.reshape(1, 128), (128, 128)).copy()
    c["c_own"] = np.broadcast_to(own.reshape(1, 128), (128, 128)).copy()
    c["c_negoff"] = ((c["c_past"] - 1.0) * 1e30).astype(np.float32)
    return c


def _layout_weights(w_in_even, q_norm_even, w_uq_even, kv_norm_even, w_ukv_even, w_out_even, w_in_odd, w_out_odd):
    wi = np.asarray(w_in_even[0])
    o = {}
    ev = np.arange(0, 256, 2)
    od = np.arange(1, 256, 2)
    w_ret = np.empty((4, 1024, 1536), np.float32)
    for h in range(4):
        rq = wi[:, h * 256:(h + 1) * 256]
        rk = wi[:, 1024 + h * 256: 1024 + (h + 1) * 256]
        w_ret[h, :, 0:128] = rq[:, ev]
        w_ret[h, :, 128:256] = rq[:, od]
        w_ret[h, :, 256:384] = rk[:, ev]
        w_ret[h, :, 384:512] = rk[:, od]
        w_ret[h, :, 512:1024] = wi[:, 2048 + h * 512: 2048 + (h + 1) * 512]
        w_ret[h, :, 1024:1536] = wi[:, 4096 + h * 512: 4096 + (h + 1) * 512]
    o["w_ret"] = w_ret
    b = 6144
    mq = wi[:, b:b + 256]
    mkv = wi[:, b + 256:b + 512]
    mkr = wi[:, b + 512:b + 576]
    mg = wi[:, b + 576:b + 1600]
    swap64 = np.concatenate([np.arange(32, 64), np.arange(0, 32)])
    o["w_mla"] = np.ascontiguousarray(np.concatenate([mq, mkv, mkr, mkr, mkr[:, swap64], mkr[:, swap64], mg], axis=1))
    wuq = np.asarray(w_uq_even[0])
    nope = np.concatenate([wuq[:, h * 192: h * 192 + 128] for h in range(8)], axis=1)
    pe1 = np.concatenate([wuq[:, h * 192 + 128: h * 192 + 192] for h in range(8)], axis=1)
    pe2 = np.concatenate([wuq[:, h * 192 + 128: h * 192 + 192][:, swap64] for h in range(8)], axis=1)
    o["w_uq"] = np.ascontiguousarray(np.concatenate([nope, pe1, pe2], axis=1))
    wukv = np.asarray(w_ukv_even[0])
    kn = np.concatenate([wukv[:, h * 256: h * 256 + 128] for h in range(8)], axis=1)
    vv = np.concatenate([wukv[:, h * 256 + 128: h * 256 + 256] for h in range(8)], axis=1)
    o["w_ukv"] = np.ascontiguousarray(np.concatenate([kn, vv], axis=1))
    o["w_oe"] = np.ascontiguousarray(np.asarray(w_out_even[0]))
    wo_ = np.asarray(w_in_odd[0])
    sw = np.concatenate([h * 64 + swap64 for h in range(16)])
    q = wo_[:, 0:1024]
    kk = wo_[:, 1024:2048]
    o["w_l1"] = np.ascontiguousarray(np.concatenate([q, q[:, sw], kk, kk[:, sw], wo_[:, 2048:3072], wo_[:, 3072:4096]], axis=1))
    o["w_oo"] = np.ascontiguousarray(np.asarray(w_out_odd[0]))
    o["qn"] = np.ascontiguousarray(np.asarray(q_norm_even[0]).reshape(2, 128).T)
    o["kvn"] = np.ascontiguousarray(np.asarray(kv_norm_even[0]).reshape(2, 128).T)
    return o


def make_in_maps(x, positions, w_in_even, q_norm_even, w_uq_even, kv_norm_even, w_ukv_even,
                 w_out_even, w_in_odd, w_out_odd, ln_g, ln_b):
    shared = _host_constants()
    shared.update(_layout_weights(w_in_even, q_norm_even, w_uq_even, kv_norm_even, w_ukv_even, w_out_even, w_in_odd, w_out_odd))
    shared["lng"] = np.ascontiguousarray(np.asarray(ln_g, dtype=np.float32))
    shared["lnb"] = np.ascontiguousarray(np.asarray(ln_b, dtype=np.float32))
    x = np.asarray(x)
    positions = np.asarray(positions)
    in_maps = []
    for b in range(8):
        m = dict(shared)
        m["x"] = np.ascontiguousarray(x[b])
        m["pos"] = np.ascontiguousarray(positions[b].astype(np.int32).reshape(1, S_LEN))
        in_maps.append(m)
    return in_maps


def kernel(x, positions, w_in_even, q_norm_even, w_uq_even, kv_norm_even, w_ukv_even,
           w_out_even, w_in_odd, w_out_odd, ln_g, ln_b):
    in_maps = make_in_maps(x, positions, w_in_even, q_norm_even, w_uq_even, kv_norm_even, w_ukv_even,
                           w_out_even, w_in_odd, w_out_odd, ln_g, ln_b)
    nc = build_program(2)
    res = run_bass_kernel_spmd(nc, in_maps, core_ids=list(range(8)))
    out = np.stack([np.asarray(r["out"]) for r in res.results], axis=0).astype(np.float32)
    if os.environ.get("K_DUMP"):
        np.save(os.environ["K_DUMP"], out)
    return out
```

```python
import math
import os
from contextlib import ExitStack

import numpy as np
import concourse.bass as bass
import concourse.mybir as mybir
from concourse.bass_utils import run_bass_kernel_spmd

F32 = mybir.dt.float32
BF16 = mybir.dt.bfloat16
I32 = mybir.dt.int32
ALU = mybir.AluOpType
AF = mybir.ActivationFunctionType
AX = mybir.AxisListType

S_LEN = 2048
D = 1024
NT = 16
ALPHA = 4.0 ** 0.25
LN_EPS = 1e-5
RMS_EPS = 1e-6
NEGB = 30000.0


class Sched:
    ENGS = ("pe", "act", "dve", "pool", "sp")

    def __init__(self, nc):
        self.nc = nc
        self.ops = []
        self.last_w = {}
        self.readers = {}
        self.eng_sems = {}
        self.dma_sems = {}
        self.dma_cnt = {}
        self._stack = []
        self.barrier_deps = set()
        self.last_on_eng = {}
        self.last_dma = {}
        for e in ("pe", "act", "dve", "pool"):
            self.eng_sems[e] = self.sem("s_" + e)

    def sem(self, name):
        cm = self.nc.semaphore(name)
        s = cm.__enter__()
        self._stack.append(cm)
        return s

    def add(self, eng, fn, reads=(), writes=(), dma=None):
        idx = len(self.ops)
        deps = set(self.barrier_deps)
        for k in reads:
            w = self.last_w.get(k)
            if w is not None:
                deps.add(w)
        for k in writes:
            w = self.last_w.get(k)
            if w is not None:
                deps.add(w)
            for r in self.readers.get(k, ()):
                deps.add(r)
        op = dict(eng=eng, fn=fn, deps=deps, dma=dma, idx=idx, signal=False, tok=None)
        self.ops.append(op)
        for k in reads:
            self.readers.setdefault(k, []).append(idx)
        for k in writes:
            self.last_w[k] = idx
            self.readers[k] = []
        if dma is not None:
            self.last_dma[dma] = idx
        else:
            self.last_on_eng[eng] = idx
        return idx

    def barrier(self):
        self.barrier_deps = set(self.last_on_eng.values()) | set(self.last_dma.values())

    def finalize(self):
        ops = self.ops
        for op in ops:
            for d in op["deps"]:
                p = ops[d]
                if p["dma"] is not None:
                    continue
                if p["eng"] == "pe" and op["eng"] == "pe" and op["dma"] is None:
                    continue
                p["signal"] = True
        cnt = {e: 0 for e in self.eng_sems}
        for op in ops:
            if op["dma"] is not None:
                key = op["dma"]
                if key not in self.dma_sems:
                    self.dma_sems[key] = self.sem("d_%d" % len(self.dma_sems))
                    self.dma_cnt[key] = 0
                self.dma_cnt[key] += 16
                op["tok"] = (self.dma_sems[key], self.dma_cnt[key])
            elif op["signal"]:
                cnt[op["eng"]] += 1
                op["tok"] = (self.eng_sems[op["eng"]], cnt[op["eng"]])
        streams = {e: [] for e in self.ENGS}
        for op in ops:
            streams[op["eng"]].append(op)
        sched = self

        def emit(engname, engine):
            known = {}
            for op in streams[engname]:
                need = {}
                for d in op["deps"]:
                    p = ops[d]
                    if p["tok"] is None:
                        continue
                    if p["dma"] is None and p["eng"] == "pe" and engname == "pe" and op["dma"] is None:
                        continue
                    s, v = p["tok"]
                    if need.get(id(s), (None, 0))[1] < v:
                        need[id(s)] = (s, v)
                for sid, (s, v) in need.items():
                    if known.get(sid, 0) >= v:
                        continue
                    engine.wait_ge(s, v)
                    known[sid] = v
                ins = op["fn"](engine)
                if op["dma"] is not None:
                    ins.then_inc(op["tok"][0], 16)
                elif op["signal"]:
                    ins.then_inc(op["tok"][0], 1)
            if engname == "sp":
                for key, s in sched.dma_sems.items():
                    engine.wait_ge(s, sched.dma_cnt[key])
                for e, s in sched.eng_sems.items():
                    if cnt[e] > 0:
                        engine.wait_ge(s, cnt[e])

        with self.nc.Block() as block:
            @block.tensor
            def _(e):
                emit("pe", e)

            @block.scalar
            def _(e):
                emit("act", e)

            @block.vector
            def _(e):
                emit("dve", e)

            @block.gpsimd
            def _(e):
                emit("pool", e)

            @block.sync
            def _(e):
                emit("sp", e)

    def close(self):
        while self._stack:
            self._stack.pop().__exit__(None, None, None)

    def dma(self, q, out, in_, reads, writes, key):
        return self.add(q, lambda e: e.dma_start(out=out, in_=in_), reads=reads, writes=writes, dma=key)

    def mms(self, specs, reads, writes):
        def fn(e):
            ins = None
            for (o, l, r, st, sp) in specs:
                ins = e.matmul(o, lhsT=l, rhs=r, start=st, stop=sp)
            return ins
        return self.add("pe", fn, reads=reads, writes=writes)

    def transposes(self, specs, ident, reads, writes):
        def fn(e):
            ins = None
            for (o, i) in specs:
                ins = e.transpose(out=o, in_=i, identity=ident)
            return ins
        return self.add("pe", fn, reads=reads, writes=writes)

    def act(self, out, in_, func, reads, writes, scale=1.0, bias=None, eng="act"):
        if bias is None:
            return self.add(eng, lambda e: e.activation(out=out, in_=in_, func=func, scale=scale), reads=reads, writes=writes)
        return self.add(eng, lambda e: e.activation(out=out, in_=in_, func=func, scale=scale, bias=bias), reads=reads, writes=writes)

    def tt(self, eng, out, in0, in1, op, reads, writes):
        return self.add(eng, lambda e: e.tensor_tensor(out=out, in0=in0, in1=in1, op=op), reads=reads, writes=writes)

    def ts(self, eng, out, in0, s1, s2, op0, op1, reads, writes):
        if s2 is None:
            return self.add(eng, lambda e: e.tensor_scalar(out=out, in0=in0, scalar1=s1, scalar2=None, op0=op0), reads=reads, writes=writes)
        return self.add(eng, lambda e: e.tensor_scalar(out=out, in0=in0, scalar1=s1, scalar2=s2, op0=op0, op1=op1), reads=reads, writes=writes)

    def stt(self, eng, out, in0, scalar, in1, op0, op1, reads, writes):
        return self.add(eng, lambda e: e.scalar_tensor_tensor(out=out, in0=in0, scalar=scalar, in1=in1, op0=op0, op1=op1), reads=reads, writes=writes)

    def copy(self, eng, out, in_, reads, writes):
        if eng == "act":
            return self.add(eng, lambda e: e.copy(out=out, in_=in_), reads=reads, writes=writes)
        return self.add(eng, lambda e: e.tensor_copy(out=out, in_=in_), reads=reads, writes=writes)


def build_program(n_layers=2, debug_out=None):
    nc = bass.Bass("TRN2", target_bir_lowering=False)

    def dram_in(name, shape, dt=F32):
        return nc.dram_tensor(name, list(shape), dt, kind="ExternalInput").ap()

    x_d = dram_in("x", [S_LEN, D])
    pos_d = dram_in("pos", [1, S_LEN], I32)
    w_ret = dram_in("w_ret", [4, D, 1536])
    w_mla = dram_in("w_mla", [D, 1792])
    w_uq = dram_in("w_uq", [256, 2048])
    w_ukv = dram_in("w_ukv", [256, 2048])
    w_oe = dram_in("w_oe", [3072, D])
    w_l1 = dram_in("w_l1", [D, 6144])
    w_oo = dram_in("w_oo", [D, D])
    qn_d = dram_in("qn", [128, 2])
    kvn_d = dram_in("kvn", [128, 2])
    lng_d = dram_in("lng", [2, D])
    lnb_d = dram_in("lnb", [2, D])
    c_ident = dram_in("c_ident", [128, 128])
    c_mask = dram_in("c_mask", [128, 128])
    c_vec = dram_in("c_vec", [128, 16])
    c_xi2 = dram_in("c_xi2", [128, 4])
    c_E = dram_in("c_E", [8, 1024])
    c_past = dram_in("c_past", [128, 128])
    c_own = dram_in("c_own", [128, 128])
    c_negoff = dram_in("c_negoff", [128, 128])
    out_d = nc.dram_tensor("out", [S_LEN, D], F32, kind="ExternalOutput").ap()

    S = Sched(nc)
    top = ExitStack()

    def sb(es, name, shape, dt):
        return es.enter_context(nc.sbuf_tensor("sb_" + name, list(shape), dt))

    yacc = sb(top, "yacc", [128, NT, D], F32)
    xT = sb(top, "xT", [128, 8, S_LEN], BF16)
    wring = sb(top, "wring", [128, 8, 1024], BF16)
    wo = sb(top, "wo", [128, 4, D], BF16)
    ident = sb(top, "ident", [128, 128], BF16)
    maskT = sb(top, "maskT", [128, 128], BF16)
    ones_b = sb(top, "ones_b", [128, 128], BF16)
    cvec = sb(top, "cvec", [128, 16], F32)
    xi2 = sb(top, "xi2", [128, 4], F32)
    qn = sb(top, "qn", [128, 2], F32)
    kvn = sb(top, "kvn", [128, 2], F32)
    Eall = sb(top, "Eall", [128, 1024], BF16)
    past01 = sb(top, "past01", [128, 128], F32)
    own01 = sb(top, "own01", [128, 128], F32)
    negoff = sb(top, "negoff", [128, 128], F32)
    tmpA = sb(top, "tmpA", [128, 512], F32)
    tmpB = sb(top, "tmpB", [128, 512], F32)
    tmpI = sb(top, "tmpI", [128, 512], I32)

    PA = top.enter_context(nc.psum_tensor("PA", [128, 1024], F32))
    PB = top.enter_context(nc.psum_tensor("PB", [128, 1024], F32))
    PC = top.enter_context(nc.psum_tensor("PC", [128, 1024], F32))
    PD = top.enter_context(nc.psum_tensor("PD", [128, 1024], F32))

    inv_r = cvec[:, 0:1]
    inv64 = cvec[:, 1:2]
    sgn64 = cvec[:, 2:3]
    eps_ln = cvec[:, 3:4]
    eps_rms = cvec[:, 4:5]

    ring_pos = [0]

    def load_w(src2d, ncols):
        nsl = ncols // 128
        s0 = ring_pos[0]
        if s0 + nsl > 8:
            s0 = 0
        ring_pos[0] = (s0 + nsl) % 8
        kc = src2d.shape[0] // 128
        keys = [("wr", s) for s in range(s0, s0 + nsl)]
        S.dma("pool", wring[:, 0:kc, s0 * 128:(s0 + nsl) * 128], src2d.rearrange("(c p) n -> p c n", p=128),
              reads=[], writes=keys, key=("wr", s0))
        return s0, keys

    def wslice(s0, kc, c0=0, ncols=128):
        return wring[:, kc, s0 * 128 + c0: s0 * 128 + c0 + ncols]

    S.dma("pool", ident[:], c_ident, [], ["ident"], "c0")
    S.dma("pool", maskT[:], c_mask, [], ["maskT"], "c1")
    S.add("pool", lambda e: e.memset(Eall[:], 0.0), writes=["Eall"])
    S.dma("pool", Eall[0:8, :], c_E, [], ["Eall"], "c2")
    S.dma("sp", cvec[:], c_vec, [], ["cvec"], "c3")
    S.dma("sp", xi2[:], c_xi2, [], ["xi2"], "c4")
    S.dma("sp", qn[:], qn_d, [], ["qn"], "c5")
    S.dma("sp", kvn[:], kvn_d, [], ["kvn"], "c6")
    S.dma("sp", past01[:], c_past, [], ["past01"], "c7")
    S.dma("sp", own01[:], c_own, [], ["own01"], "c8")
    S.dma("sp", negoff[:], c_negoff, [], ["negoff"], "c9")
    S.add("pool", lambda e: e.memset(ones_b[:], 1.0), writes=["ones_b"])

    def gen_tables(tab, inv_ap, sgn_ap):
        for j in range(4):
            cs = slice(j * 512, (j + 1) * 512)
            S.dma("sp", tmpI[:], bass.AP(pos_d.tensor, j * 512, [[0, 128], [1, 512]]), [], ["tmpI"], "posld")
            S.copy("dve", tmpA[:], tmpI[:], ["tmpI"], ["tmpA"])
            S.ts("dve", tmpA[:], tmpA[:], inv_ap, None, ALU.mult, None, ["tmpA", "cvec"], ["tmpA"])
            for which in (1, 0):
                shift = 0.0 if which == 1 else 0.25
                S.ts("dve", tmpI[:], tmpA[:], 1.0 / (2 * math.pi), shift, ALU.mult, ALU.add, ["tmpA"], ["tmpI"])
                S.copy("dve", tmpB[:], tmpI[:], ["tmpI"], ["tmpB"])
                S.stt("dve", tmpB[:], tmpB[:], -2 * math.pi, tmpA[:], ALU.mult, ALU.add, ["tmpA", "tmpB"], ["tmpB"])
                if which == 0:
                    S.ts("dve", tmpB[:], tmpB[:], math.pi / 2, None, ALU.add, None, ["tmpB"], ["tmpB"])
                S.ts("dve", tmpB[:], tmpB[:], -3.141592, 3.141592, ALU.max, ALU.min, ["tmpB"], ["tmpB"])
                if which == 1 and sgn_ap is not None:
                    S.act(tab[:, 1, cs], tmpB[:], AF.Sin, ["tmpB", "cvec"], [("tab", j)], scale=sgn_ap)
                else:
                    S.act(tab[:, which, cs], tmpB[:], AF.Sin, ["tmpB"], [("tab", j)])

    with ExitStack() as es0:
        xs = sb(es0, "xs", [128, 2, D], F32)
        xb = sb(es0, "xb", [128, 2, D], BF16)
        for t in range(NT):
            s = t % 2
            S.dma("sp", xs[:, s, :], x_d[t * 128:(t + 1) * 128, :], [], [("xs", s)], ("xs", s))
            S.add("act", (lambda s=s, t=t: (lambda e: e.mul(out=yacc[:, t, :], in_=xs[:, s, :], mul=ALPHA)))(),
                  reads=[("xs", s)], writes=[("yacc", t)])
            S.copy("dve", xb[:, s, :], xs[:, s, :], [("xs", s)], [("xb", s)])
            pt = PA[:].bitcast(BF16) if s == 0 else PB[:].bitcast(BF16)
            S.transposes([(pt[:, c * 128:(c + 1) * 128], xb[:, s, c * 128:(c + 1) * 128]) for c in range(8)], ident[:],
                         reads=[("xb", s), "ident"], writes=[("P", s, 0)])
            S.copy("dve" if s == 0 else "act", xT[:, :, t * 128:(t + 1) * 128],
                   pt[:, 0:1024].rearrange("p (c k) -> p c k", k=128), [("P", s, 0)], [("xT", t)])
        S.barrier()
    XT_ALL = [("xT", t) for t in range(NT)]

    def xt_keys(j):
        return [("xT", 4 * j + i) for i in range(4)]

    def layer_norm(layer, last):
        with ExitStack() as es:
            gb = sb(es, "gb%d" % layer, [128, 2, D], F32)
            mv = sb(es, "mv%d" % layer, [128, NT, 2], F32)
            st = sb(es, "st%d" % layer, [128, 2, 6], F32)
            rstd = sb(es, "rstd%d" % layer, [128, NT], F32)
            nb = sb(es, "nb%d" % layer, [128, NT], F32)
            x1 = sb(es, "x1_%d" % layer, [128, 2, D], F32)
            x1b = sb(es, "x1b_%d" % layer, [128, 2, D], BF16)
            S.dma("sp", gb[:, 0, :], bass.AP(lng_d.tensor, layer * D, [[0, 128], [1, D]]), [], ["gb0"], "gb0")
            S.dma("sp", gb[:, 1, :], bass.AP(lnb_d.tensor, layer * D, [[0, 128], [1, D]]), [], ["gb1"], "gb1")
            for t in range(NT):
                for hh in range(2):
                    S.add("dve", (lambda t=t, hh=hh: (lambda e: e.bn_stats(out=st[:, hh, :], in_=yacc[:, t, hh * 512:(hh + 1) * 512])))(),
                          reads=[("yacc", t)], writes=[("st", hh)])
                S.add("dve", (lambda t=t: (lambda e: e.bn_aggr(out=mv[:, t, :], in_=st[:].rearrange("p a b -> p (a b)"))))(),
                      reads=[("st", 0), ("st", 1)], writes=["mv"])
            S.act(rstd[:], mv[:, :, 1], AF.Sqrt, ["mv", "cvec"], ["rstd"], bias=eps_ln)
            S.add("dve", lambda e: e.reciprocal(out=rstd[:], in_=rstd[:]), reads=["rstd"], writes=["rstd"])
            S.stt("dve", nb[:], mv[:, :, 0], -1.0, rstd[:], ALU.mult, ALU.mult, ["mv", "rstd"], ["nb"])
            for t in range(NT):
                s = t % 2
                S.act(x1[:, s, :], yacc[:, t, :], AF.Identity, [("yacc", t), "rstd", "nb"], [("x1", s), ("x1h", s)],
                      scale=rstd[:, t:t + 1], bias=nb[:, t:t + 1])
                S.tt("dve", x1[:, s, :], x1[:, s, :], gb[:, 0, :], ALU.mult, [("x1", s), "gb0"], [("x1", s), ("x1h", s)])
                S.tt("pool", x1[:, s, 640:1024], x1[:, s, 640:1024], gb[:, 1, 640:1024], ALU.add, [("x1h", s), "gb1"], [("x1h", s)])
                S.tt("dve", x1[:, s, 0:640], x1[:, s, 0:640], gb[:, 1, 0:640], ALU.add, [("x1", s), "gb1"], [("x1", s)])
                if last:
                    S.dma("sp", out_d[t * 128:(t + 1) * 128, :], x1[:, s, :], [("x1", s), ("x1h", s)], [("out", t)], ("out", s))
                else:
                    S.add("act", (lambda s=s, t=t: (lambda e: e.mul(out=yacc[:, t, :], in_=x1[:, s, :], mul=ALPHA)))(),
                          reads=[("x1", s), ("x1h", s)], writes=[("yacc", t)])
                    S.copy("act", x1b[:, s, :], x1[:, s, :], [("x1", s), ("x1h", s)], [("x1b", s)])
                    pt = PA[:].bitcast(BF16) if s == 0 else PB[:].bitcast(BF16)
                    S.transposes([(pt[:, c * 128:(c + 1) * 128], x1b[:, s, c * 128:(c + 1) * 128]) for c in range(8)], ident[:],
                                 reads=[("x1b", s), "ident"], writes=[("P", s, 0)])
                    S.copy("dve", xT[:, :, t * 128:(t + 1) * 128],
                           pt[:, 0:1024].rearrange("p (c k) -> p c k", k=128), [("P", s, 0)], [("xT", t)])
            S.barrier()

    def proj_fm(slot, wkc, rhs_fn, j, out_ps, pkey, rkeys, wkeys, c0=0):
        specs = []
        for kc in range(wkc):
            specs.append((out_ps, wslice(slot, kc, c0), rhs_fn(kc, j), kc == 0, kc == wkc - 1))
        S.mms(specs, reads=list(rkeys) + list(wkeys), writes=[pkey])

    def xT_rhs(kc, j):
        return xT[:, kc, j * 512:(j + 1) * 512]

    def yacc_accumulate(cat, nslots, catkey_fn, wo_keys, t):
        specs = []
        for half in range(2):
            for s in range(nslots):
                specs.append((PA[:, half * 512:(half + 1) * 512], cat[:, s, t * 128:(t + 1) * 128],
                              wo[:, s, half * 512:(half + 1) * 512], s == 0, s == nslots - 1))
        S.mms(specs, reads=[catkey_fn(t)] + list(wo_keys), writes=[("P", 0, 0), ("P", 0, 1)])
        S.tt("dve", yacc[:, t, :], yacc[:, t, :], PA[:], ALU.add, [("P", 0, 0), ("P", 0, 1), ("yacc", t)], [("yacc", t)])

    def attention_core(name, qk_list, bias_fn, Vfn, dv, scale, on_out, on_key, ebuf, rs_t):
        iters = [(jg, kt) for jg in range(4) for kt in range(4 * jg + 4)]

        def acc_bank(jg):
            return (PD, 3) if jg % 2 == 0 else (PA, 0)

        def emit_score(i):
            jg, kt = iters[i]
            c0 = max(0, kt - 4 * jg)
            q_lo = jg * 512 + c0 * 128
            q_hi = (jg + 1) * 512
            half = i % 2
            ps = PC[:, half * 512 + c0 * 128: (half + 1) * 512]
            specs = []
            rk = []
            items = list(qk_list)
            b = bias_fn(kt, jg, q_lo, q_hi) if bias_fn is not None else None
            if b is not None:
                items = items + [b]
            for ii, (kf, qf, keys) in enumerate(items):
                specs.append((ps, kf(kt), qf(q_lo, q_hi), ii == 0, ii == len(items) - 1))
                rk += keys
            S.mms(specs, reads=rk, writes=[("P", 2, half)])
            es_ = i % 3
            S.act(ebuf[:, es_, c0 * 128:512], ps, AF.Exp, [("P", 2, half)], [(name + "e", es_)], scale=scale)
            if kt >= 4 * jg:
                S.tt("dve", ebuf[:, es_, c0 * 128:(c0 + 1) * 128], ebuf[:, es_, c0 * 128:(c0 + 1) * 128], maskT[:], ALU.mult,
                     [(name + "e", es_), "maskT"], [(name + "e", es_)])

        def emit_pv(i):
            jg, kt = iters[i]
            c0 = max(0, kt - 4 * jg)
            es_ = i % 3
            PT, pk = acc_bank(jg)
            pv = []
            for qi in range(c0, 4):
                base = (qi // 2) * 512 + (qi % 2) * (dv + 1)
                pv.append((PT[:, base: base + dv + 1], ebuf[:, es_, qi * 128:(qi + 1) * 128], Vfn(kt),
                           kt == 0 and qi % 2 == 0, kt == 4 * jg + qi and qi % 2 == 1))
            S.mms(pv, reads=[(name + "e", es_), name + "V"], writes=[("P", pk, 0), ("P", pk, 1)])
            if kt == 4 * jg + 3:
                for qi in range(4):
                    base = (qi // 2) * 512 + (qi % 2) * (dv + 1)
                    t = jg * 4 + qi
                    rsl = rs_t[:, (jg % 2) * 4 + qi:(jg % 2) * 4 + qi + 1]
                    S.add("dve", (lambda base=base, rsl=rsl, PT=PT: (lambda e: e.reciprocal(out=rsl, in_=PT[:, base + dv: base + dv + 1])))(),
                          reads=[("P", pk, 0), ("P", pk, 1)], writes=[(name + "rs", jg % 2, qi)])
                    S.act(on_out(t), PT[:, base: base + dv], AF.Copy, [("P", pk, 0), ("P", pk, 1), (name + "rs", jg % 2, qi)], [on_key(t)],
                          scale=rsl)

        emit_score(0)
        for i in range(len(iters)):
            if i + 1 < len(iters):
                emit_score(i + 1)
            emit_pv(i)

    with ExitStack() as esR:
        tabR = sb(esR, "tabR", [128, 2, S_LEN], F32)
        qT = sb(esR, "qT", [128, 2, S_LEN], BF16)
        kT = sb(esR, "kT", [128, 2, S_LEN], BF16)
        vt = sb(esR, "vt", [128, NT, 512], BF16)
        gT = sb(esR, "gT", [128, 4, S_LEN], BF16)
        st32 = sb(esR, "st32", [128, 2, 512], F32)
        stb = sb(esR, "stb", [128, 2, 512], BF16)
        ktok = sb(esR, "ktok", [128, 2, 256], BF16)
        innT = sb(esR, "innT", [128, 2, 128], BF16)
        onb = sb(esR, "onb", [128, 2, 512], BF16)
        catc = sb(esR, "catc", [128, 2, 4, 128], BF16)
        bst = sb(esR, "bst", [128, 2, 6], F32)
        bmv = sb(esR, "bmv", [128, 2, 2], F32)
        sm = sb(esR, "sm", [128, 2, 4], F32)

        gen_tables(tabR, inv_r, None)
        TAB = [("tab", j) for j in range(4)]

        for h in range(4):
            g = 1.0 - 2.0 ** (-5.0 - h)
            gC = g ** 128
            xi_h = cvec[:, 8 + h:9 + h]
            vs_h = cvec[:, 12 + h:13 + h]
            for qi, dst in enumerate((qT, kT)):
                dname = "qT" if qi == 0 else "kT"
                sE, kE = load_w(w_ret[h, :, qi * 256: qi * 256 + 128], 128)
                sO, kO = load_w(w_ret[h, :, qi * 256 + 128: qi * 256 + 256], 128)
                for j in range(4):
                    hf = j % 2
                    psA = PA[:, hf * 512:(hf + 1) * 512]
                    psB = PB[:, hf * 512:(hf + 1) * 512]
                    proj_fm(sE, 8, xT_rhs, j, psA, ("P", 0, hf), xt_keys(j), kE)
                    proj_fm(sO, 8, xT_rhs, j, psB, ("P", 1, hf), xt_keys(j), kO)
                    cs = slice(j * 512, (j + 1) * 512)
                    cosj = tabR[:, 0, cs]
                    sinj = tabR[:, 1, cs]
                    S.tt("dve", tmpA[:], psA, cosj, ALU.mult, [("P", 0, hf), ("tab", j)], ["tmpA"])
                    S.tt("dve", tmpB[:], psB, sinj, ALU.mult, [("P", 1, hf), ("tab", j)], ["tmpB"])
                    S.tt("dve", dst[:, 0, cs], tmpA[:], tmpB[:], ALU.subtract, ["tmpA", "tmpB"], [(dname, 0, j)])
                    S.tt("dve", tmpA[:], psB, cosj, ALU.mult, [("P", 1, hf), ("tab", j)], ["tmpA"])
                    S.tt("dve", tmpB[:], psA, sinj, ALU.mult, [("P", 0, hf), ("tab", j)], ["tmpB"])
                    S.tt("dve", dst[:, 1, cs], tmpA[:], tmpB[:], ALU.add, ["tmpA", "tmpB"], [(dname, 1, j)])
            sV, kV = load_w(w_ret[h, :, 512:1024], 512)
            for t in range(NT):
                hf = t % 2
                ps = PA[:, hf * 512:(hf + 1) * 512]
                S.mms([(ps, xT[:, kc, t * 128:(t + 1) * 128], wring[:, kc, sV * 128:(sV + 4) * 128], kc == 0, kc == 7) for kc in range(8)],
                      reads=[("xT", t)] + kV, writes=[("P", 0, hf)])
                S.act(vt[:, t, :], ps, AF.Copy, [("P", 0, hf), "cvec"], [("vt", t)], scale=vs_h)
            sG, kG = load_w(w_ret[h, :, 1024:1536], 512)
            for c in range(4):
                for j in range(4):
                    hf = j % 2
                    ps = PB[:, hf * 512:(hf + 1) * 512]
                    proj_fm(sG, 8, xT_rhs, j, ps, ("P", 1, hf), xt_keys(j), kG, c0=c * 128)
                    S.act(gT[:, c, j * 512:(j + 1) * 512], ps, AF.Silu, [("P", 1, hf)], [("gT", c, j)])
            S.dma("pool", wo[:], w_oe[h * 512:(h + 1) * 512, :].rearrange("(c p) n -> p c n", p=128), [], ["wo"], "wo")
            PDb = PD[:].bitcast(BF16)

            def o_bank(n):
                return (PC[:, 512:1024], ("P", 2, 1)) if n % 2 == 0 else (PB[:, 512:1024], ("P", 1, 1))

            def head_part(n, h=h, gC=gC, xi_h=xi_h):
                j = n // 4
                cs = slice(n * 128, (n + 1) * 128)
                d2 = n % 2
                qk = [("qT", 0, j), ("qT", 1, j)]
                kk = [("kT", 0, j), ("kT", 1, j)]
                st_ps = PC[:, d2 * 128:(d2 + 1) * 128]
                S.mms([(st_ps, kT[:, c, cs], qT[:, c, cs], c == 0, c == 1) for c in range(2)], reads=qk + kk, writes=[("P", 2, 0, d2)])
                S.tt("dve", innT[:, d2, :], st_ps, maskT[:], ALU.mult, [("P", 2, 0, d2), "maskT"], [("innT", d2)])
                if n < NT - 1:
                    kt_ps = PDb[:, 0:256]
                    S.transposes([(kt_ps[:, c * 128:(c + 1) * 128], kT[:, c, cs]) for c in range(2)], ident[:],
                                 reads=kk + ["ident"], writes=[("P", 3, 0, "kt")])
                    S.act(ktok[:, d2, :], kt_ps, AF.Copy, [("P", 3, 0, "kt")], [("ktok", d2)], scale=gC)
                o_ps, o_key = o_bank(n)
                specs = [(o_ps, innT[:, d2, :], vt[:, n, :], True, n == 0)]
                rk = [("innT", d2), ("vt", n)]
                if n > 0:
                    for c in range(2):
                        specs.append((o_ps, qT[:, c, cs], stb[:, c, :], False, c == 1))
                    rk += qk + [("stb", 0), ("stb", 1)]
                S.mms(specs, reads=rk, writes=[o_key])
                if n < NT - 1:
                    S.mms([(PD[:, 512:1024], ktok[:, d2, 0:128], vt[:, n, :], True, True)], reads=[("ktok", d2), ("vt", n)], writes=[("P", 3, 1)])
                    S.mms([(PB[:, 0:512], ktok[:, d2, 128:256], vt[:, n, :], True, True)], reads=[("ktok", d2), ("vt", n)], writes=[("P", 1, 0)])
                    if n == 0:
                        S.copy("dve", st32[:, 0, :], PD[:, 512:1024], [("P", 3, 1)], [("st32", 0)])
                        S.copy("dve", st32[:, 1, :], PB[:, 0:512], [("P", 1, 0)], [("st32", 1)])
                    else:
                        S.stt("dve", st32[:, 0, :], st32[:, 0, :], gC, PD[:, 512:1024], ALU.mult, ALU.add, [("P", 3, 1), ("st32", 0)], [("st32", 0)])
                        S.stt("dve", st32[:, 1, :], st32[:, 1, :], gC, PB[:, 0:512], ALU.mult, ALU.add, [("P", 1, 0), ("st32", 1)], [("st32", 1)])
                    S.copy("act", stb[:, 0, :], st32[:, 0, :], [("st32", 0)], [("stb", 0)])
                    S.copy("pool", stb[:, 1, :], st32[:, 1, :], [("st32", 1)], [("stb", 1)])
                S.add("dve", (lambda o_ps=o_ps: (lambda e: e.bn_stats(out=bst[:, d2, :], in_=o_ps)))(), reads=[o_key], writes=[("bst", d2)])
                S.add("dve", lambda e: e.bn_aggr(out=bmv[:, d2, :], in_=bst[:, d2, :]), reads=[("bst", d2)], writes=[("bmv", d2)])
                S.ts("dve", sm[:, d2, 0:1], bmv[:, d2, 1:2], xi2[:, h:h + 1], LN_EPS, ALU.mult, ALU.add, [("bmv", d2), "xi2"], [("sm", d2, 0)])
                S.act(sm[:, d2, 1:2], sm[:, d2, 0:1], AF.Sqrt, [("sm", d2, 0)], [("sm", d2, 1)])
                S.add("dve", lambda e: e.reciprocal(out=sm[:, d2, 1:2], in_=sm[:, d2, 1:2]), reads=[("sm", d2, 1)], writes=[("sm", d2, 1)])
                S.ts("dve", sm[:, d2, 2:3], sm[:, d2, 1:2], xi_h, None, ALU.mult, None, [("sm", d2, 1), "cvec"], [("sm", d2, 2)])
                S.stt("dve", sm[:, d2, 3:4], bmv[:, d2, 0:1], -1.0, sm[:, d2, 2:3], ALU.mult, ALU.mult, [("bmv", d2), ("sm", d2, 2)], [("sm", d2, 3)])
                S.act(onb[:, d2, :], o_ps, AF.Identity, [o_key, ("sm", d2, 2), ("sm", d2, 3)], [("onb", d2)],
                      scale=sm[:, d2, 2:3], bias=sm[:, d2, 3:4])

            def tail_part(n):
                j = n // 4
                cs = slice(n * 128, (n + 1) * 128)
                d2 = n % 2
                ct_ps = PDb[:, 512:1024]
                S.transposes([(ct_ps[:, c * 128:(c + 1) * 128], onb[:, d2, c * 128:(c + 1) * 128]) for c in range(4)], ident[:],
                             reads=[("onb", d2), "ident"], writes=[("P", 3, 0, "ct")])
                S.tt("dve", catc[:, d2, :, :], ct_ps.rearrange("p (c k) -> p c k", k=128), gT[:, :, cs], ALU.mult,
                     [("P", 3, 0, "ct")] + [("gT", c, j) for c in range(4)], [("catc", d2)])
                specs = []
                for half in range(2):
                    for c in range(4):
                        specs.append((PA[:, half * 512:(half + 1) * 512], catc[:, d2, c, :], wo[:, c, half * 512:(half + 1) * 512], c == 0, c == 3))
                S.mms(specs, reads=[("catc", d2), "wo"], writes=[("P", 0, 0), ("P", 0, 1)])
                S.tt("dve", yacc[:, n, :], yacc[:, n, :], PA[:], ALU.add, [("P", 0, 0), ("P", 0, 1), ("yacc", n)], [("yacc", n)])

            head_part(0)
            for n in range(NT):
                if n + 1 < NT:
                    head_part(n + 1)
                tail_part(n)
        S.barrier()

    if debug_out == "ret":
        for t in range(NT):
            S.dma("sp", out_d[t * 128:(t + 1) * 128, :], yacc[:, t, :], [("yacc", t)], [("out", t)], "out")
        S.finalize(); top.close(); S.close()
        return nc

    with ExitStack() as esT:
        tab64 = sb(esT, "tab64", [128, 2, S_LEN], F32)
        gen_tables(tab64, inv64, sgn64)
        ebuf = sb(esT, "ebuf", [128, 3, 512], BF16)
        rs_t = sb(esT, "rs_t", [128, 8], F32)
        S.barrier()

        def rope_pair(psA, psB, kA, kB, j, out_ap, out_key, eng_out="dve"):
            cs = slice(j * 512, (j + 1) * 512)
            S.tt("dve", tmpA[:], psA, tab64[:, 0, cs], ALU.mult, [kA, ("tab", j)], ["tmpA"])
            S.tt("dve", tmpB[:], psB, tab64[:, 1, cs], ALU.mult, [kB, ("tab", j)], ["tmpB"])
            S.tt("dve", out_ap, tmpA[:], tmpB[:], ALU.add, ["tmpA", "tmpB"], [out_key])

        with ExitStack() as esM:
            mqT = sb(esM, "mqT", [128, 2, S_LEN], BF16)
            mkvT = sb(esM, "mkvT", [128, 2, S_LEN], BF16)
            kpeT = sb(esM, "kpeT", [128, S_LEN], BF16)
            qpz = [sb(esM, "qpz%d" % i, [128, S_LEN], BF16) for i in range(2)]
            S.add("pool", lambda e: e.memset(qpz[0][64:128, :], 0.0), writes=[("qpz", 0, j) for j in range(4)])
            S.add("pool", lambda e: e.memset(qpz[1][0:64, :], 0.0), writes=[("qpz", 1, j) for j in range(4)])
            qnT = sb(esM, "qnT", [128, S_LEN], BF16)
            knT = sb(esM, "knT", [128, S_LEN], BF16)
            Vaug = sb(esM, "Vaug", [128, NT, 129], BF16)
            gmT = sb(esM, "gmT", [128, S_LEN], BF16)
            catM = sb(esM, "catM", [128, 2, S_LEN], BF16)
            ovl = sb(esM, "ovl", [128, S_LEN], BF16)
            onM = ovl[:].rearrange("p (t k) -> p t k", k=128)
            rawg = ovl[:].bitcast(F32).rearrange("p (c k) -> p c k", k=512)
            sqb = ebuf[:, 0:2, :]
            S.add("pool", lambda e: e.memset(Vaug[:, :, 128:129], 1.0), writes=["VaugOnes"])

            for which, (dstT, gvec, col0) in enumerate(((mqT, qn, 0), (mkvT, kvn, 256))):
                dn = "mqT" if which == 0 else "mkvT"
                s0, k0 = load_w(w_mla[:, col0:col0 + 128], 128)
                s1, k1 = load_w(w_mla[:, col0 + 128:col0 + 256], 128)
                for j in range(4):
                    hf = j % 2
                    psl = [PA[:, hf * 512:(hf + 1) * 512], PB[:, hf * 512:(hf + 1) * 512]]
                    pk = [("P", 0, hf), ("P", 1, hf)]
                    proj_fm(s0, 8, xT_rhs, j, psl[0], pk[0], xt_keys(j), k0)
                    proj_fm(s1, 8, xT_rhs, j, psl[1], pk[1], xt_keys(j), k1)
                    for c in range(2):
                        S.act(sqb[:, c, :], psl[c], AF.Square, [pk[c]], [("sqb", c)])
                        S.act(rawg[:, c, :], psl[c], AF.Copy, [pk[c], "qn", "kvn"], [("rawg", c)], scale=gvec[:, c:c + 1])
                    ss_ps = PC[:, 0:512]
                    S.mms([(ss_ps, ones_b[:], sqb[:, c, :], c == 0, c == 1) for c in range(2)],
                          reads=[("sqb", 0), ("sqb", 1), "ones_b"], writes=[("P", 2, 0)])
                    S.act(tmpA[:], ss_ps, AF.Sqrt, [("P", 2, 0), "cvec"], ["tmpA"], scale=1.0 / 256.0, bias=eps_rms)
                    S.add("dve", lambda e: e.reciprocal(out=tmpA[:], in_=tmpA[:]), reads=["tmpA"], writes=["tmpA"])
                    for c in range(2):
                        S.tt("dve", dstT[:, c, j * 512:(j + 1) * 512], rawg[:, c, :], tmpA[:], ALU.mult,
                             [("rawg", c), "tmpA"], [(dn, j)])
            MQ = [("mqT", j) for j in range(4)]
            MKV = [("mkvT", j) for j in range(4)]
            sA, kA_ = load_w(w_mla[:, 512:640], 128)
            sB, kB_ = load_w(w_mla[:, 640:768], 128)
            for j in range(4):
                hf = j % 2
                psA = PA[:, hf * 512:(hf + 1) * 512]
                psB = PB[:, hf * 512:(hf + 1) * 512]
                proj_fm(sA, 8, xT_rhs, j, psA, ("P", 0, hf), xt_keys(j), kA_)
                proj_fm(sB, 8, xT_rhs, j, psB, ("P", 1, hf), xt_keys(j), kB_)
                rope_pair(psA, psB, ("P", 0, hf), ("P", 1, hf), j, kpeT[:, j * 512:(j + 1) * 512], ("kpeT", j))
            KPE = [("kpeT", j) for j in range(4)]
            S.barrier()

            def mq_rhs(kc, j):
                return mqT[:, kc, j * 512:(j + 1) * 512]

            def mkv_rhs(kc, j):
                return mkvT[:, kc, j * 512:(j + 1) * 512]

            for hp in range(4):
                sA, kA_ = load_w(w_uq[:, 1024 + hp * 128: 1024 + (hp + 1) * 128], 128)
                sB, kB_ = load_w(w_uq[:, 1536 + hp * 128: 1536 + (hp + 1) * 128], 128)
                for j in range(4):
                    hf = j % 2
                    psA = PA[:, hf * 512:(hf + 1) * 512]
                    psB = PB[:, hf * 512:(hf + 1) * 512]
                    proj_fm(sA, 2, mq_rhs, j, psA, ("P", 0, hf), [("mqT", j)], kA_)
                    proj_fm(sB, 2, mq_rhs, j, psB, ("P", 1, hf), [("mqT", j)], kB_)
                    cs = slice(j * 512, (j + 1) * 512)
                    S.tt("dve", tmpA[:], psA, tab64[:, 0, cs], ALU.mult, [("P", 0, hf), ("tab", j)], ["tmpA"])
                    S.tt("dve", tmpB[:], psB, tab64[:, 1, cs], ALU.mult, [("P", 1, hf), ("tab", j)], ["tmpB"])
                    S.tt("dve", qpz[0][0:64, cs], tmpA[0:64, :], tmpB[0:64, :], ALU.add, ["tmpA", "tmpB"], [("qpz", 0, j)])
                    S.tt("dve", qpz[1][64:128, cs], tmpA[64:128, :], tmpB[64:128, :], ALU.add, ["tmpA", "tmpB"], [("qpz", 1, j)])
                S.dma("pool", wo[:, 0:2, :], w_oe[2048 + hp * 256: 2048 + (hp + 1) * 256, :].rearrange("(c p) n -> p c n", p=128),
                      [], ["wo"], "wo")
                for e2 in range(2):
                    h = hp * 2 + e2
                    r0 = 64 * e2
                    sQ, kQ = load_w(w_uq[:, h * 128:(h + 1) * 128], 128)
                    sK, kK = load_w(w_ukv[:, h * 128:(h + 1) * 128], 128)
                    sVv, kVv = load_w(w_ukv[:, 1024 + h * 128: 1024 + (h + 1) * 128], 128)
                    sG, kG = load_w(w_mla[:, 768 + h * 128: 768 + (h + 1) * 128], 128)
                    for j in range(4):
                        hf = j % 2
                        psA = PA[:, hf * 512:(hf + 1) * 512]
                        psB = PB[:, hf * 512:(hf + 1) * 512]
                        proj_fm(sQ, 2, mq_rhs, j, psA, ("P", 0, hf), [("mqT", j)], kQ)
                        S.copy("act", qnT[:, j * 512:(j + 1) * 512], psA, [("P", 0, hf)], [("qnT", j)])
                        proj_fm(sK, 2, mkv_rhs, j, psB, ("P", 1, hf), [("mkvT", j)], kK)
                        S.copy("act", knT[:, j * 512:(j + 1) * 512], psB, [("P", 1, hf)], [("knT", j)])
                    for j in range(4):
                        hf = j % 2
                        psA = PA[:, hf * 512:(hf + 1) * 512]
                        proj_fm(sG, 8, xT_rhs, j, psA, ("P", 0, hf), xt_keys(j), kG)
                        S.act(gmT[:, j * 512:(j + 1) * 512], psA, AF.Silu, [("P", 0, hf)], [("gmT", j)])
                    for t in range(NT):
                        hf = t % 2
                        ps = PB[:, hf * 512: hf * 512 + 128]
                        S.mms([(ps, mkvT[:, kc, t * 128:(t + 1) * 128], wslice(sVv, kc), kc == 0, kc == 1) for kc in range(2)],
                              reads=[("mkvT", t // 4)] + kVv, writes=[("P", 1, hf)])
                        S.copy("act", Vaug[:, t, 0:128], ps, [("P", 1, hf), "VaugOnes"], ["mV"])
                    QN = [("qnT", j) for j in range(4)]
                    KN = [("knT", j) for j in range(4)]
                    qk_list = [
                        (lambda kt: knT[:, kt * 128:(kt + 1) * 128], lambda lo, hi: qnT[:, lo:hi], QN + KN),
                        (lambda kt: kpeT[:, kt * 128:(kt + 1) * 128], lambda lo, hi, e2=e2: qpz[e2][:, lo:hi],
                         [("qpz", e2, j) for j in range(4)] + KPE),
                    ]
                    attention_core("m", qk_list, None, lambda kt: Vaug[:, kt, :], 128, 192.0 ** -0.5,
                                   lambda t: onM[:, t, :], lambda t: ("onM", t), ebuf, rs_t)
                    for jg in range(4):
                        ct_ps = PB[:].bitcast(BF16)[:, 0:512]
                        S.transposes([(ct_ps[:, i * 128:(i + 1) * 128], onM[:, jg * 4 + i, :]) for i in range(4)], ident[:],
                                     reads=[("onM", jg * 4 + i) for i in range(4)] + ["ident"], writes=[("P", 1, 0)])
                        S.tt("dve", catM[:, e2, jg * 512:(jg + 1) * 512], ct_ps, gmT[:, jg * 512:(jg + 1) * 512], ALU.mult,
                             [("P", 1, 0), ("gmT", jg)], [("catM", e2, jg)])
                for t in range(NT):
                    yacc_accumulate(catM, 2, lambda t: ("catM", 0, t // 4), ["wo"] + [("catM", 1, j) for j in range(4)], t)
            S.barrier()

        layer_norm(0, last=(n_layers == 1))
        if n_layers == 1:
            S.finalize(); esT.close(); top.close(); S.close()
            return nc

        with ExitStack() as esL:
            qTl = sb(esL, "qTl", [128, S_LEN], BF16)
            kmh = sb(esL, "kmh", [128, 8], BF16)
            kml = sb(esL, "kml", [128, 8], BF16)
            kmr = sb(esL, "kmr", [128, 8], F32)
            qTb = sb(esL, "qTb", [128, S_LEN], BF16)
            kTb = sb(esL, "kTb", [128, S_LEN], BF16)
            kmean = sb(esL, "kmean", [128, 8], F32)
            Vp = sb(esL, "Vp", [128, NT, 2, 65], BF16)
            gmT = sb(esL, "gmT1", [128, S_LEN], BF16)
            nselT = sb(esL, "nselT", [128, 2, S_LEN], BF16)
            S.add("pool", lambda e: e.memset(nselT[:], 0.0), writes=[("nselT", e, j) for e in range(2) for j in range(4)])
            catL = sb(esL, "catL", [128, 2, S_LEN], BF16)
            onL = sb(esL, "onL", [128, NT, 128], BF16)
            gm = sb(esL, "gm", [128, 128], F32)
            sel = sb(esL, "sel", [128, 128], F32)
            top8 = sb(esL, "top8", [128, 8], F32)
            biasb = sb(esL, "biasb", [128, 128], BF16)
            S.add("pool", lambda e: e.memset(Vp[:, :, :, 64:65], 1.0), writes=["VpOnes"])

            for hp in range(8):
                sl = hp % 2
                for qi in range(2):
                    base = qi * 2048
                    sA, kA_ = load_w(w_l1[:, base + hp * 128: base + (hp + 1) * 128], 128)
                    sB, kB_ = load_w(w_l1[:, base + 1024 + hp * 128: base + 1024 + (hp + 1) * 128], 128)
                    for j in range(4):
                        hf = j % 2
                        psA = PA[:, hf * 512:(hf + 1) * 512]
                        psB = PB[:, hf * 512:(hf + 1) * 512]
                        proj_fm(sA, 8, xT_rhs, j, psA, ("P", 0, hf), xt_keys(j), kA_)
                        proj_fm(sB, 8, xT_rhs, j, psB, ("P", 1, hf), xt_keys(j), kB_)
                        cs = slice(j * 512, (j + 1) * 512)
                        if qi == 0:
                            S.tt("dve", tmpA[:], psA, tab64[:, 0, cs], ALU.mult, [("P", 0, hf), ("tab", j)], ["tmpA"])
                            S.tt("dve", tmpB[:], psB, tab64[:, 1, cs], ALU.mult, [("P", 1, hf), ("tab", j)], ["tmpB"])
                            S.tt("dve", tmpA[:], tmpA[:], tmpB[:], ALU.add, ["tmpA", "tmpB"], ["tmpA"])
                            S.copy("act", qTb[:, cs], tmpA[:], ["tmpA"], [("qTb", j)])
                            S.tt("dve", qTl[:, cs], tmpA[:], qTb[:, cs], ALU.subtract, ["tmpA", ("qTb", j)], [("qTl", j)])
                        else:
                            S.tt("dve", tmpA[:], psA, tab64[:, 0, cs], ALU.mult, [("P", 0, hf), ("tab", j)], ["tmpA"])
                            S.tt("dve", tmpB[:], psB, tab64[:, 1, cs], ALU.mult, [("P", 1, hf), ("tab", j)], ["tmpB"])
                            S.tt("dve", tmpA[:], tmpA[:], tmpB[:], ALU.add, ["tmpA", "tmpB"], ["tmpA"])
                            S.copy("act", kTb[:, cs], tmpA[:], ["tmpA"], [("kTb", j)])
                            S.add("dve", (lambda j=j: (lambda e: e.tensor_reduce(out=kmean[:, 2 * j:2 * j + 2],
                                                                                 in_=tmpA[:].rearrange("p (b l) -> p b l", l=256),
                                                                                 axis=AX.X, op=ALU.add)))(),
                                  reads=["tmpA"], writes=["kmean"])
                QB = [("qTb", j) for j in range(4)]
                QL = [("qTl", j) for j in range(4)]
                S.copy("dve", kmh[:], kmean[:], ["kmean"], ["kmh"])
                S.tt("dve", kmr[:], kmean[:], kmh[:], ALU.subtract, ["kmean", "kmh"], ["kmr"])
                S.copy("dve", kml[:], kmr[:], ["kmr"], ["kml"])
                KB = [("kTb", j) for j in range(4)]
                sVv, kVv = load_w(w_l1[:, 4096 + hp * 128: 4096 + (hp + 1) * 128], 128)
                sG, kG = load_w(w_l1[:, 5120 + hp * 128: 5120 + (hp + 1) * 128], 128)
                for t in range(NT):
                    hf = t % 2
                    ps = PB[:, hf * 512: hf * 512 + 128]
                    S.mms([(ps, xT[:, kc, t * 128:(t + 1) * 128], wslice(sVv, kc), kc == 0, kc == 7) for kc in range(8)],
                          reads=[("xT", t)] + kVv, writes=[("P", 1, hf)])
                    S.copy("act", Vp[:, t, :, 0:64], ps.rearrange("p (a b) -> p a b", b=64), [("P", 1, hf), "VpOnes"], ["lV"])
                for j in range(4):
                    hf = j % 2
                    psA = PA[:, hf * 512:(hf + 1) * 512]
                    proj_fm(sG, 8, xT_rhs, j, psA, ("P", 0, hf), xt_keys(j), kG)
                    S.act(gmT[:, j * 512:(j + 1) * 512], psA, AF.Silu, [("P", 0, hf)], [("gmT", j)])
                if hp % 2 == 0:
                    S.dma("pool", wo[:, 0:2, :], w_oo[hp * 128:(hp + 2) * 128, :].rearrange("(c p) n -> p c n", p=128), [], ["wo"], "wo")
                for e2 in range(2):
                    r0 = 64 * e2
                    g_ps = PC[:, 0:128]
                    gspecs = []
                    for t in range(NT):
                        ts_ = slice(t * 128, (t + 1) * 128)
                        gspecs.append((g_ps[:, t * 8:(t + 1) * 8], qTb[r0:r0 + 64, ts_], kmh[r0:r0 + 64, :], True, False))
                        gspecs.append((g_ps[:, t * 8:(t + 1) * 8], qTl[r0:r0 + 64, ts_], kmh[r0:r0 + 64, :], False, False))
                        gspecs.append((g_ps[:, t * 8:(t + 1) * 8], qTb[r0:r0 + 64, ts_], kml[r0:r0 + 64, :], False, True))
                    S.mms(gspecs, reads=QB + QL + ["kmh", "kml"], writes=[("P", 2, 0)])
                    S.tt("dve", gm[:], g_ps, past01[:], ALU.mult, [("P", 2, 0), "past01"], ["gm"])
                    S.tt("dve", gm[:], gm[:], negoff[:], ALU.add, ["gm", "negoff"], ["gm"])
                    for t in range(NT):
                        S.add("dve", (lambda t=t: (lambda e: e.max(out=top8[:], in_=gm[:, t * 8:(t + 1) * 8])))(), reads=["gm"], writes=["top8"])
                        S.ts("dve", sel[:, t * 8:(t + 1) * 8], gm[:, t * 8:(t + 1) * 8], top8[:, 2:3], None, ALU.is_ge, None, ["gm", "top8"], ["sel"])
                    S.tt("dve", sel[:], sel[:], past01[:], ALU.mult, ["sel", "past01"], ["sel"])
                    S.tt("dve", sel[:], sel[:], own01[:], ALU.add, ["sel", "own01"], ["sel"])
                    S.ts("dve", biasb[:], sel[:], NEGB, -NEGB, ALU.mult, ALU.add, ["sel"], ["biasb"])
                    bt_ps = PC[:].bitcast(BF16)[0:8, 1024:2048]
                    for q4 in range(4):
                        S.transposes([(bt_ps[:, i * 128:(i + 1) * 128], biasb[:, (q4 * 4 + i) * 8:(q4 * 4 + i + 1) * 8]) for i in range(4)], ident[:],
                                     reads=["biasb", "ident"], writes=[("P", 2, 1)])
                        S.copy("act", nselT[0:8, e2, q4 * 512:(q4 + 1) * 512], bt_ps[:, 0:512], [("P", 2, 1)], [("nselT", e2, q4)])
                    NS = [("nselT", e2, j) for j in range(4)]

                    def bias_fn(kt, jg, lo, hi, e2=e2, NS=NS):
                        nb_ = kt // 2
                        if nb_ == 2 * jg + 1:
                            return None
                        return (lambda kt_, nb_=nb_: Eall[:, nb_ * 128:(nb_ + 1) * 128],
                                lambda lo_, hi_, e2=e2: nselT[:, e2, lo_:hi_], NS + ["Eall"])

                    qk_list = [(lambda kt, r0=r0: kTb[r0:r0 + 64, kt * 128:(kt + 1) * 128],
                                lambda lo, hi, r0=r0: qTb[r0:r0 + 64, lo:hi], QB + KB)]
                    attention_core("l", qk_list, bias_fn, (lambda kt, e2=e2: Vp[:, kt, e2, :]), 64, 0.125,
                                   (lambda t, e2=e2: onL[:, t, e2 * 64:(e2 + 1) * 64]), (lambda t, e2=e2: ("onL", t, e2)), ebuf, rs_t)
                for jg in range(4):
                    ct_ps = PB[:].bitcast(BF16)[:, 0:512]
                    S.transposes([(ct_ps[:, i * 128:(i + 1) * 128], onL[:, jg * 4 + i, :]) for i in range(4)], ident[:],
                                 reads=[("onL", jg * 4 + i, e) for i in range(4) for e in range(2)] + ["ident"], writes=[("P", 1, 0)])
                    S.tt("dve", catL[:, sl, jg * 512:(jg + 1) * 512], ct_ps, gmT[:, jg * 512:(jg + 1) * 512], ALU.mult,
                         [("P", 1, 0), ("gmT", jg)], [("catL", sl, jg)])
                if hp % 2 == 1:
                    for t in range(NT):
                        yacc_accumulate(catL, 2, lambda t: ("catL", 0, t // 4), ["wo"] + [("catL", 1, j) for j in range(4)], t)
            S.barrier()
        layer_norm(1, last=True)

    S.finalize()
    top.close()
    S.close()
    return nc


def _host_constants():
    c = {}
    c["c_ident"] = np.eye(128, dtype=np.float32)
    k = np.arange(128)
    c["c_mask"] = (k[None, :] >= k[:, None]).astype(np.float32)
    vec = np.zeros((128, 16), np.float32)
    vec[:, 0] = 1.0 / (10000.0 ** np.linspace(0.0, 1.0, 128, dtype=np.float32))
    inv64 = 1.0 / (10000.0 ** (np.arange(0, 64, 2, dtype=np.float32) / 64.0))
    vec[:, 1] = inv64[np.arange(128) % 32]
    vec[:, 2] = np.where((np.arange(128) % 64) < 32, -1.0, 1.0)
    vec[:, 3] = LN_EPS
    vec[:, 4] = RMS_EPS
    xi2 = np.zeros((128, 4), np.float32)
    for h in range(4):
        g = 1.0 - 2.0 ** (-5.0 - h)
        xi = (g ** (k + 1.0)) * (256.0 ** -0.5)
        vec[:, 8 + h] = xi
        vec[:, 12 + h] = g ** (-(k + 1.0))
        xi2[:, h] = xi * xi
    c["c_vec"] = vec
    c["c_xi2"] = xi2
    E = np.zeros((8, 8, 128), np.float32)
    for n in range(8):
        E[n, n, :] = 1.0
    c["c_E"] = E.reshape(8, 1024)
    past = np.zeros((16, 8), np.float32)
    own = np.zeros((16, 8), np.float32)
    for t in range(16):
        qb = t // 2
        past[t, :qb] = 1.0
        own[t, qb] = 1.0
    c["c_past"] = np.broadcast_to(past.reshape(1, 128), (128, 128)).copy()
    c["c_own"] = np.broadcast_to(own.reshape(1, 128), (128, 128)).copy()
    c["c_negoff"] = ((c["c_past"] - 1.0) * 1e30).astype(np.float32)
    return c


def _layout_weights(w_in_even, q_norm_even, w_uq_even, kv_norm_even, w_ukv_even, w_out_even, w_in_odd, w_out_odd):
    wi = np.asarray(w_in_even[0])
    o = {}
    ev = np.arange(0, 256, 2)
    od = np.arange(1, 256, 2)
    w_ret = np.empty((4, 1024, 1536), np.float32)
    for h in range(4):
        rq = wi[:, h * 256:(h + 1) * 256]
        rk = wi[:, 1024 + h * 256: 1024 + (h + 1) * 256]
        w_ret[h, :, 0:128] = rq[:, ev]
        w_ret[h, :, 128:256] = rq[:, od]
        w_ret[h, :, 256:384] = rk[:, ev]
        w_ret[h, :, 384:512] = rk[:, od]
        w_ret[h, :, 512:1024] = wi[:, 2048 + h * 512: 2048 + (h + 1) * 512]
        w_ret[h, :, 1024:1536] = wi[:, 4096 + h * 512: 4096 + (h + 1) * 512]
    o["w_ret"] = w_ret
    b = 6144
    mq = wi[:, b:b + 256]
    mkv = wi[:, b + 256:b + 512]
    mkr = wi[:, b + 512:b + 576]
    mg = wi[:, b + 576:b + 1600]
    swap64 = np.concatenate([np.arange(32, 64), np.arange(0, 32)])
    o["w_mla"] = np.ascontiguousarray(np.concatenate([mq, mkv, mkr, mkr, mkr[:, swap64], mkr[:, swap64], mg], axis=1))
    wuq = np.asarray(w_uq_even[0])
    nope = np.concatenate([wuq[:, h * 192: h * 192 + 128] for h in range(8)], axis=1)
    pe1 = np.concatenate([wuq[:, h * 192 + 128: h * 192 + 192] for h in range(8)], axis=1)
    pe2 = np.concatenate([wuq[:, h * 192 + 128: h * 192 + 192][:, swap64] for h in range(8)], axis=1)
    o["w_uq"] = np.ascontiguousarray(np.concatenate([nope, pe1, pe2], axis=1))
    wukv = np.asarray(w_ukv_even[0])
    kn = np.concatenate([wukv[:, h * 256: h * 256 + 128] for h in range(8)], axis=1)
    vv = np.concatenate([wukv[:, h * 256 + 128: h * 256 + 256] for h in range(8)], axis=1)
    o["w_ukv"] = np.ascontiguousarray(np.concatenate([kn, vv], axis=1))
    o["w_oe"] = np.ascontiguousarray(np.asarray(w_out_even[0]))
    wo_ = np.asarray(w_in_odd[0])
    sw = np.concatenate([h * 64 + swap64 for h in range(16)])
    q = wo_[:, 0:1024]
    kk = wo_[:, 1024:2048]
    o["w_l1"] = np.ascontiguousarray(np.concatenate([q, q[:, sw], kk, kk[:, sw], wo_[:, 2048:3072], wo_[:, 3072:4096]], axis=1))
    o["w_oo"] = np.ascontiguousarray(np.asarray(w_out_odd[0]))
    o["qn"] = np.ascontiguousarray(np.asarray(q_norm_even[0]).reshape(2, 128).T)
    o["kvn"] = np.ascontiguousarray(np.asarray(kv_norm_even[0]).reshape(2, 128).T)
    return o


def make_in_maps(x, positions, w_in_even, q_norm_even, w_uq_even, kv_norm_even, w_ukv_even,
                 w_out_even, w_in_odd, w_out_odd, ln_g, ln_b):
    shared = _host_constants()
    shared.update(_layout_weights(w_in_even, q_norm_even, w_uq_even, kv_norm_even, w_ukv_even, w_out_even, w_in_odd, w_out_odd))
    shared["lng"] = np.ascontiguousarray(np.asarray(ln_g, dtype=np.float32))
    shared["lnb"] = np.ascontiguousarray(np.asarray(ln_b, dtype=np.float32))
    x = np.asarray(x)
    positions = np.asarray(positions)
    in_maps = []
    for b in range(8):
        m = dict(shared)
        m["x"] = np.ascontiguousarray(x[b])
        m["pos"] = np.ascontiguousarray(positions[b].astype(np.int32).reshape(1, S_LEN))
        in_maps.append(m)
    return in_maps


def kernel(x, positions, w_in_even, q_norm_even, w_uq_even, kv_norm_even, w_ukv_even,
           w_out_even, w_in_odd, w_out_odd, ln_g, ln_b):
    in_maps = make_in_maps(x, positions, w_in_even, q_norm_even, w_uq_even, kv_norm_even, w_ukv_even,
                           w_out_even, w_in_odd, w_out_odd, ln_g, ln_b)
    nc = build_program(2)
    res = run_bass_kernel_spmd(nc, in_maps, core_ids=list(range(8)))
    out = np.stack([np.asarray(r["out"]) for r in res.results], axis=0).astype(np.float32)
    if os.environ.get("K_DUMP"):
        np.save(os.environ["K_DUMP"], out)
    return out
```

```python
import math
import os
from contextlib import ExitStack

import numpy as np
import concourse.bass as bass
import concourse.mybir as mybir
from concourse.bass_utils import run_bass_kernel_spmd

F32 = mybir.dt.float32
BF16 = mybir.dt.bfloat16
I32 = mybir.dt.int32
ALU = mybir.AluOpType
AF = mybir.ActivationFunctionType
AX = mybir.AxisListType

S_LEN = 2048
D = 1024
NT = 16
ALPHA = 4.0 ** 0.25
LN_EPS = 1e-5
RMS_EPS = 1e-6
NEGB = 30000.0


class Sched:
    ENGS = ("pe", "act", "dve", "pool", "sp")

    def __init__(self, nc):
        self.nc = nc
        self.ops = []
        self.last_w = {}
        self.readers = {}
        self.eng_sems = {}
        self.dma_sems = {}
        self.dma_cnt = {}
        self._stack = []
        self.barrier_deps = set()
        self.last_on_eng = {}
        self.last_dma = {}
        for e in ("pe", "act", "dve", "pool"):
            self.eng_sems[e] = self.sem("s_" + e)

    def sem(self, name):
        cm = self.nc.semaphore(name)
        s = cm.__enter__()
        self._stack.append(cm)
        return s

    def add(self, eng, fn, reads=(), writes=(), dma=None):
        idx = len(self.ops)
        deps = set(self.barrier_deps)
        for k in reads:
            w = self.last_w.get(k)
            if w is not None:
                deps.add(w)
        for k in writes:
            w = self.last_w.get(k)
            if w is not None:
                deps.add(w)
            for r in self.readers.get(k, ()):
                deps.add(r)
        op = dict(eng=eng, fn=fn, deps=deps, dma=dma, idx=idx, signal=False, tok=None)
        self.ops.append(op)
        for k in reads:
            self.readers.setdefault(k, []).append(idx)
        for k in writes:
            self.last_w[k] = idx
            self.readers[k] = []
        if dma is not None:
            self.last_dma[dma] = idx
        else:
            self.last_on_eng[eng] = idx
        return idx

    def barrier(self):
        self.barrier_deps = set(self.last_on_eng.values()) | set(self.last_dma.values())

    def finalize(self):
        ops = self.ops
        for op in ops:
            for d in op["deps"]:
                p = ops[d]
                if p["dma"] is not None:
                    continue
                if p["eng"] == "pe" and op["eng"] == "pe" and op["dma"] is None:
                    continue
                p["signal"] = True
        cnt = {e: 0 for e in self.eng_sems}
        for op in ops:
            if op["dma"] is not None:
                key = op["dma"]
                if key not in self.dma_sems:
                    self.dma_sems[key] = self.sem("d_%d" % len(self.dma_sems))
                    self.dma_cnt[key] = 0
                self.dma_cnt[key] += 16
                op["tok"] = (self.dma_sems[key], self.dma_cnt[key])
            elif op["signal"]:
                cnt[op["eng"]] += 1
                op["tok"] = (self.eng_sems[op["eng"]], cnt[op["eng"]])
        streams = {e: [] for e in self.ENGS}
        for op in ops:
            streams[op["eng"]].append(op)
        sched = self

        def emit(engname, engine):
            known = {}
            for op in streams[engname]:
                need = {}
                for d in op["deps"]:
                    p = ops[d]
                    if p["tok"] is None:
                        continue
                    if p["dma"] is None and p["eng"] == "pe" and engname == "pe" and op["dma"] is None:
                        continue
                    s, v = p["tok"]
                    if need.get(id(s), (None, 0))[1] < v:
                        need[id(s)] = (s, v)
                for sid, (s, v) in need.items():
                    if known.get(sid, 0) >= v:
                        continue
                    engine.wait_ge(s, v)
                    known[sid] = v
                ins = op["fn"](engine)
                if op["dma"] is not None:
                    ins.then_inc(op["tok"][0], 16)
                elif op["signal"]:
                    ins.then_inc(op["tok"][0], 1)
            if engname == "sp":
                for key, s in sched.dma_sems.items():
                    engine.wait_ge(s, sched.dma_cnt[key])
                for e, s in sched.eng_sems.items():
                    if cnt[e] > 0:
                        engine.wait_ge(s, cnt[e])

        with self.nc.Block() as block:
            @block.tensor
            def _(e):
                emit("pe", e)

            @block.scalar
            def _(e):
                emit("act", e)

            @block.vector
            def _(e):
                emit("dve", e)

            @block.gpsimd
            def _(e):
                emit("pool", e)

            @block.sync
            def _(e):
                emit("sp", e)

    def close(self):
        while self._stack:
            self._stack.pop().__exit__(None, None, None)

    def dma(self, q, out, in_, reads, writes, key):
        return self.add(q, lambda e: e.dma_start(out=out, in_=in_), reads=reads, writes=writes, dma=key)

    def mms(self, specs, reads, writes):
        def fn(e):
            ins = None
            for (o, l, r, st, sp) in specs:
                ins = e.matmul(o, lhsT=l, rhs=r, start=st, stop=sp)
            return ins
        return self.add("pe", fn, reads=reads, writes=writes)

    def transposes(self, specs, ident, reads, writes):
        def fn(e):
            ins = None
            for (o, i) in specs:
                ins = e.transpose(out=o, in_=i, identity=ident)
            return ins
        return self.add("pe", fn, reads=reads, writes=writes)

    def act(self, out, in_, func, reads, writes, scale=1.0, bias=None, eng="act"):
        if bias is None:
            return self.add(eng, lambda e: e.activation(out=out, in_=in_, func=func, scale=scale), reads=reads, writes=writes)
        return self.add(eng, lambda e: e.activation(out=out, in_=in_, func=func, scale=scale, bias=bias), reads=reads, writes=writes)

    def tt(self, eng, out, in0, in1, op, reads, writes):
        return self.add(eng, lambda e: e.tensor_tensor(out=out, in0=in0, in1=in1, op=op), reads=reads, writes=writes)

    def ts(self, eng, out, in0, s1, s2, op0, op1, reads, writes):
        if s2 is None:
            return self.add(eng, lambda e: e.tensor_scalar(out=out, in0=in0, scalar1=s1, scalar2=None, op0=op0), reads=reads, writes=writes)
        return self.add(eng, lambda e: e.tensor_scalar(out=out, in0=in0, scalar1=s1, scalar2=s2, op0=op0, op1=op1), reads=reads, writes=writes)

    def stt(self, eng, out, in0, scalar, in1, op0, op1, reads, writes):
        return self.add(eng, lambda e: e.scalar_tensor_tensor(out=out, in0=in0, scalar=scalar, in1=in1, op0=op0, op1=op1), reads=reads, writes=writes)

    def copy(self, eng, out, in_, reads, writes):
        if eng == "act":
            return self.add(eng, lambda e: e.copy(out=out, in_=in_), reads=reads, writes=writes)
        return self.add(eng, lambda e: e.tensor_copy(out=out, in_=in_), reads=reads, writes=writes)


def build_program(n_layers=2, debug_out=None):
    nc = bass.Bass("TRN2", target_bir_lowering=False)

    def dram_in(name, shape, dt=F32):
        return nc.dram_tensor(name, list(shape), dt, kind="ExternalInput").ap()

    x_d = dram_in("x", [S_LEN, D])
    pos_d = dram_in("pos", [1, S_LEN], I32)
    w_ret = dram_in("w_ret", [4, D, 1536])
    w_mla = dram_in("w_mla", [D, 1792])
    w_uq = dram_in("w_uq", [256, 2048])
    w_ukv = dram_in("w_ukv", [256, 2048])
    w_oe = dram_in("w_oe", [3072, D])
    w_l1 = dram_in("w_l1", [D, 6144])
    w_oo = dram_in("w_oo", [D, D])
    qn_d = dram_in("qn", [128, 2])
    kvn_d = dram_in("kvn", [128, 2])
    lng_d = dram_in("lng", [2, D])
    lnb_d = dram_in("lnb", [2, D])
    c_ident = dram_in("c_ident", [128, 128])
    c_mask = dram_in("c_mask", [128, 128])
    c_vec = dram_in("c_vec", [128, 16])
    c_xi2 = dram_in("c_xi2", [128, 4])
    c_E = dram_in("c_E", [8, 1024])
    c_past = dram_in("c_past", [128, 128])
    c_own = dram_in("c_own", [128, 128])
    c_negoff = dram_in("c_negoff", [128, 128])
    out_d = nc.dram_tensor("out", [S_LEN, D], F32, kind="ExternalOutput").ap()

    S = Sched(nc)
    top = ExitStack()

    def sb(es, name, shape, dt):
        return es.enter_context(nc.sbuf_tensor("sb_" + name, list(shape), dt))

    yacc = sb(top, "yacc", [128, NT, D], F32)
    xT = sb(top, "xT", [128, 8, S_LEN], BF16)
    wring = sb(top, "wring", [128, 8, 1024], BF16)
    wo = sb(top, "wo", [128, 4, D], BF16)
    ident = sb(top, "ident", [128, 128], BF16)
    maskT = sb(top, "maskT", [128, 128], BF16)
    ones_b = sb(top, "ones_b", [128, 128], BF16)
    cvec = sb(top, "cvec", [128, 16], F32)
    xi2 = sb(top, "xi2", [128, 4], F32)
    qn = sb(top, "qn", [128, 2], F32)
    kvn = sb(top, "kvn", [128, 2], F32)
    Eall = sb(top, "Eall", [128, 1024], BF16)
    past01 = sb(top, "past01", [128, 128], F32)
    own01 = sb(top, "own01", [128, 128], F32)
    negoff = sb(top, "negoff", [128, 128], F32)
    tmpA = sb(top, "tmpA", [128, 512], F32)
    tmpB = sb(top, "tmpB", [128, 512], F32)
    tmpI = sb(top, "tmpI", [128, 512], I32)

    PA = top.enter_context(nc.psum_tensor("PA", [128, 1024], F32))
    PB = top.enter_context(nc.psum_tensor("PB", [128, 1024], F32))
    PC = top.enter_context(nc.psum_tensor("PC", [128, 1024], F32))
    PD = top.enter_context(nc.psum_tensor("PD", [128, 1024], F32))

    inv_r = cvec[:, 0:1]
    inv64 = cvec[:, 1:2]
    sgn64 = cvec[:, 2:3]
    eps_ln = cvec[:, 3:4]
    eps_rms = cvec[:, 4:5]

    ring_pos = [0]

    def load_w(src2d, ncols):
        nsl = ncols // 128
        s0 = ring_pos[0]
        if s0 + nsl > 8:
            s0 = 0
        ring_pos[0] = (s0 + nsl) % 8
        kc = src2d.shape[0] // 128
        keys = [("wr", s) for s in range(s0, s0 + nsl)]
        S.dma("pool", wring[:, 0:kc, s0 * 128:(s0 + nsl) * 128], src2d.rearrange("(c p) n -> p c n", p=128),
              reads=[], writes=keys, key=("wr", s0))
        return s0, keys

    def wslice(s0, kc, c0=0, ncols=128):
        return wring[:, kc, s0 * 128 + c0: s0 * 128 + c0 + ncols]

    S.dma("pool", ident[:], c_ident, [], ["ident"], "c0")
    S.dma("pool", maskT[:], c_mask, [], ["maskT"], "c1")
    S.add("pool", lambda e: e.memset(Eall[:], 0.0), writes=["Eall"])
    S.dma("pool", Eall[0:8, :], c_E, [], ["Eall"], "c2")
    S.dma("sp", cvec[:], c_vec, [], ["cvec"], "c3")
    S.dma("sp", xi2[:], c_xi2, [], ["xi2"], "c4")
    S.dma("sp", qn[:], qn_d, [], ["qn"], "c5")
    S.dma("sp", kvn[:], kvn_d, [], ["kvn"], "c6")
    S.dma("sp", past01[:], c_past, [], ["past01"], "c7")
    S.dma("sp", own01[:], c_own, [], ["own01"], "c8")
    S.dma("sp", negoff[:], c_negoff, [], ["negoff"], "c9")
    S.add("pool", lambda e: e.memset(ones_b[:], 1.0), writes=["ones_b"])

    def gen_tables(tab, inv_ap, sgn_ap):
        for j in range(4):
            cs = slice(j * 512, (j + 1) * 512)
            S.dma("sp", tmpI[:], bass.AP(pos_d.tensor, j * 512, [[0, 128], [1, 512]]), [], ["tmpI"], "posld")
            S.copy("dve", tmpA[:], tmpI[:], ["tmpI"], ["tmpA"])
            S.ts("dve", tmpA[:], tmpA[:], inv_ap, None, ALU.mult, None, ["tmpA", "cvec"], ["tmpA"])
            for which in (1, 0):
                shift = 0.0 if which == 1 else 0.25
                S.ts("dve", tmpI[:], tmpA[:], 1.0 / (2 * math.pi), shift, ALU.mult, ALU.add, ["tmpA"], ["tmpI"])
                S.copy("dve", tmpB[:], tmpI[:], ["tmpI"], ["tmpB"])
                S.stt("dve", tmpB[:], tmpB[:], -2 * math.pi, tmpA[:], ALU.mult, ALU.add, ["tmpA", "tmpB"], ["tmpB"])
                if which == 0:
                    S.ts("dve", tmpB[:], tmpB[:], math.pi / 2, None, ALU.add, None, ["tmpB"], ["tmpB"])
                S.ts("dve", tmpB[:], tmpB[:], -3.141592, 3.141592, ALU.max, ALU.min, ["tmpB"], ["tmpB"])
                if which == 1 and sgn_ap is not None:
                    S.act(tab[:, 1, cs], tmpB[:], AF.Sin, ["tmpB", "cvec"], [("tab", j)], scale=sgn_ap)
                else:
                    S.act(tab[:, which, cs], tmpB[:], AF.Sin, ["tmpB"], [("tab", j)])

    with ExitStack() as es0:
        xs = sb(es0, "xs", [128, 2, D], F32)
        xb = sb(es0, "xb", [128, 2, D], BF16)
        for t in range(NT):
            s = t % 2
            S.dma("sp", xs[:, s, :], x_d[t * 128:(t + 1) * 128, :], [], [("xs", s)], ("xs", s))
            S.add("act", (lambda s=s, t=t: (lambda e: e.mul(out=yacc[:, t, :], in_=xs[:, s, :], mul=ALPHA)))(),
                  reads=[("xs", s)], writes=[("yacc", t)])
            S.copy("dve", xb[:, s, :], xs[:, s, :], [("xs", s)], [("xb", s)])
            pt = PA[:].bitcast(BF16) if s == 0 else PB[:].bitcast(BF16)
            S.transposes([(pt[:, c * 128:(c + 1) * 128], xb[:, s, c * 128:(c + 1) * 128]) for c in range(8)], ident[:],
                         reads=[("xb", s), "ident"], writes=[("P", s, 0)])
            S.copy("dve" if s == 0 else "act", xT[:, :, t * 128:(t + 1) * 128],
                   pt[:, 0:1024].rearrange("p (c k) -> p c k", k=128), [("P", s, 0)], [("xT", t)])
        S.barrier()
    XT_ALL = [("xT", t) for t in range(NT)]

    def xt_keys(j):
        return [("xT", 4 * j + i) for i in range(4)]

    def layer_norm(layer, last):
        with ExitStack() as es:
            gb = sb(es, "gb%d" % layer, [128, 2, D], F32)
            mv = sb(es, "mv%d" % layer, [128, NT, 2], F32)
            st = sb(es, "st%d" % layer, [128, 2, 6], F32)
            rstd = sb(es, "rstd%d" % layer, [128, NT], F32)
            nb = sb(es, "nb%d" % layer, [128, NT], F32)
            x1 = sb(es, "x1_%d" % layer, [128, 2, D], F32)
            x1b = sb(es, "x1b_%d" % layer, [128, 2, D], BF16)
            S.dma("sp", gb[:, 0, :], bass.AP(lng_d.tensor, layer * D, [[0, 128], [1, D]]), [], ["gb0"], "gb0")
            S.dma("sp", gb[:, 1, :], bass.AP(lnb_d.tensor, layer * D, [[0, 128], [1, D]]), [], ["gb1"], "gb1")
            for t in range(NT):
                for hh in range(2):
                    S.add("dve", (lambda t=t, hh=hh: (lambda e: e.bn_stats(out=st[:, hh, :], in_=yacc[:, t, hh * 512:(hh + 1) * 512])))(),
                          reads=[("yacc", t)], writes=[("st", hh)])
                S.add("dve", (lambda t=t: (lambda e: e.bn_aggr(out=mv[:, t, :], in_=st[:].rearrange("p a b -> p (a b)"))))(),
                      reads=[("st", 0), ("st", 1)], writes=["mv"])
            S.act(rstd[:], mv[:, :, 1], AF.Sqrt, ["mv", "cvec"], ["rstd"], bias=eps_ln)
            S.add("dve", lambda e: e.reciprocal(out=rstd[:], in_=rstd[:]), reads=["rstd"], writes=["rstd"])
            S.stt("dve", nb[:], mv[:, :, 0], -1.0, rstd[:], ALU.mult, ALU.mult, ["mv", "rstd"], ["nb"])
            for t in range(NT):
                s = t % 2
                S.act(x1[:, s, :], yacc[:, t, :], AF.Identity, [("yacc", t), "rstd", "nb"], [("x1", s), ("x1h", s)],
                      scale=rstd[:, t:t + 1], bias=nb[:, t:t + 1])
                S.tt("dve", x1[:, s, :], x1[:, s, :], gb[:, 0, :], ALU.mult, [("x1", s), "gb0"], [("x1", s), ("x1h", s)])
                S.tt("pool", x1[:, s, 640:1024], x1[:, s, 640:1024], gb[:, 1, 640:1024], ALU.add, [("x1h", s), "gb1"], [("x1h", s)])
                S.tt("dve", x1[:, s, 0:640], x1[:, s, 0:640], gb[:, 1, 0:640], ALU.add, [("x1", s), "gb1"], [("x1", s)])
                if last:
                    S.dma("sp", out_d[t * 128:(t + 1) * 128, :], x1[:, s, :], [("x1", s), ("x1h", s)], [("out", t)], ("out", s))
                else:
                    S.add("act", (lambda s=s, t=t: (lambda e: e.mul(out=yacc[:, t, :], in_=x1[:, s, :], mul=ALPHA)))(),
                          reads=[("x1", s), ("x1h", s)], writes=[("yacc", t)])
                    S.copy("act", x1b[:, s, :], x1[:, s, :], [("x1", s), ("x1h", s)], [("x1b", s)])
                    pt = PA[:].bitcast(BF16) if s == 0 else PB[:].bitcast(BF16)
                    S.transposes([(pt[:, c * 128:(c + 1) * 128], x1b[:, s, c * 128:(c + 1) * 128]) for c in range(8)], ident[:],
                                 reads=[("x1b", s), "ident"], writes=[("P", s, 0)])
                    S.copy("dve", xT[:, :, t * 128:(t + 1) * 128],
                           pt[:, 0:1024].rearrange("p (c k) -> p c k", k=128), [("P", s, 0)], [("xT", t)])
            S.barrier()

    def proj_fm(slot, wkc, rhs_fn, j, out_ps, pkey, rkeys, wkeys, c0=0):
        specs = []
        for kc in range(wkc):
            specs.append((out_ps, wslice(slot, kc, c0), rhs_fn(kc, j), kc == 0, kc == wkc - 1))
        S.mms(specs, reads=list(rkeys) + list(wkeys), writes=[pkey])

    def xT_rhs(kc, j):
        return xT[:, kc, j * 512:(j + 1) * 512]

    def yacc_accumulate(cat, nslots, catkey_fn, wo_keys, t):
        specs = []
        for half in range(2):
            for s in range(nslots):
                specs.append((PA[:, half * 512:(half + 1) * 512], cat[:, s, t * 128:(t + 1) * 128],
                              wo[:, s, half * 512:(half + 1) * 512], s == 0, s == nslots - 1))
        S.mms(specs, reads=[catkey_fn(t)] + list(wo_keys), writes=[("P", 0, 0), ("P", 0, 1)])
        S.tt("dve", yacc[:, t, :], yacc[:, t, :], PA[:], ALU.add, [("P", 0, 0), ("P", 0, 1), ("yacc", t)], [("yacc", t)])

    def attention_core(name, qk_list, bias_fn, Vfn, dv, scale, on_out, on_key, ebuf, rs_t):
        iters = [(jg, kt) for jg in range(4) for kt in range(4 * jg + 4)]

        def acc_bank(jg):
            return (PD, 3) if jg % 2 == 0 else (PA, 0)

        def emit_score(i):
            jg, kt = iters[i]
            c0 = max(0, kt - 4 * jg)
            q_lo = jg * 512 + c0 * 128
            q_hi = (jg + 1) * 512
            half = i % 2
            ps = PC[:, half * 512 + c0 * 128: (half + 1) * 512]
            specs = []
            rk = []
            items = list(qk_list)
            b = bias_fn(kt, jg, q_lo, q_hi) if bias_fn is not None else None
            if b is not None:
                items = [b] + items
            for ii, (kf, qf, keys) in enumerate(items):
                specs.append((ps, kf(kt), qf(q_lo, q_hi), ii == 0, ii == len(items) - 1))
                rk += keys
            S.mms(specs, reads=rk, writes=[("P", 2, half)])
            es_ = i % 3
            S.act(ebuf[:, es_, c0 * 128:512], ps, AF.Exp, [("P", 2, half)], [(name + "e", es_)], scale=scale)
            if kt >= 4 * jg:
                S.tt("dve", ebuf[:, es_, c0 * 128:(c0 + 1) * 128], ebuf[:, es_, c0 * 128:(c0 + 1) * 128], maskT[:], ALU.mult,
                     [(name + "e", es_), "maskT"], [(name + "e", es_)])

        def emit_pv(i):
            jg, kt = iters[i]
            c0 = max(0, kt - 4 * jg)
            es_ = i % 3
            PT, pk = acc_bank(jg)
            pv = []
            for qi in range(c0, 4):
                base = (qi // 2) * 512 + (qi % 2) * (dv + 1)
                pv.append((PT[:, base: base + dv + 1], ebuf[:, es_, qi * 128:(qi + 1) * 128], Vfn(kt),
                           kt == 0 and qi % 2 == 0, kt == 4 * jg + qi and qi % 2 == 1))
            S.mms(pv, reads=[(name + "e", es_), name + "V"], writes=[("P", pk, 0), ("P", pk, 1)])
            if kt == 4 * jg + 3:
                for qi in range(4):
                    base = (qi // 2) * 512 + (qi % 2) * (dv + 1)
                    t = jg * 4 + qi
                    rsl = rs_t[:, (jg % 2) * 4 + qi:(jg % 2) * 4 + qi + 1]
                    S.add("dve", (lambda base=base, rsl=rsl, PT=PT: (lambda e: e.reciprocal(out=rsl, in_=PT[:, base + dv: base + dv + 1])))(),
                          reads=[("P", pk, 0), ("P", pk, 1)], writes=[(name + "rs", jg % 2, qi)])
                    S.act(on_out(t), PT[:, base: base + dv], AF.Copy, [("P", pk, 0), ("P", pk, 1), (name + "rs", jg % 2, qi)], [on_key(t)],
                          scale=rsl)

        emit_score(0)
        for i in range(len(iters)):
            if i + 1 < len(iters):
                emit_score(i + 1)
            emit_pv(i)

    with ExitStack() as esR:
        tabR = sb(esR, "tabR", [128, 2, S_LEN], F32)
        qT = sb(esR, "qT", [128, 2, S_LEN], BF16)
        kT = sb(esR, "kT", [128, 2, S_LEN], BF16)
        vt = sb(esR, "vt", [128, NT, 512], BF16)
        gT = sb(esR, "gT", [128, 4, S_LEN], BF16)
        st32 = sb(esR, "st32", [128, 2, 512], F32)
        stb = sb(esR, "stb", [128, 2, 512], BF16)
        ktok = sb(esR, "ktok", [128, 2, 256], BF16)
        innT = sb(esR, "innT", [128, 2, 128], BF16)
        onb = sb(esR, "onb", [128, 2, 512], BF16)
        catc = sb(esR, "catc", [128, 2, 4, 128], BF16)
        bst = sb(esR, "bst", [128, 2, 6], F32)
        bmv = sb(esR, "bmv", [128, 2, 2], F32)
        sm = sb(esR, "sm", [128, 2, 4], F32)

        gen_tables(tabR, inv_r, None)
        TAB = [("tab", j) for j in range(4)]

        for h in range(4):
            g = 1.0 - 2.0 ** (-5.0 - h)
            gC = g ** 128
            xi_h = cvec[:, 8 + h:9 + h]
            vs_h = cvec[:, 12 + h:13 + h]
            for qi, dst in enumerate((qT, kT)):
                dname = "qT" if qi == 0 else "kT"
                sE, kE = load_w(w_ret[h, :, qi * 256: qi * 256 + 128], 128)
                sO, kO = load_w(w_ret[h, :, qi * 256 + 128: qi * 256 + 256], 128)
                for j in range(4):
                    hf = j % 2
                    psA = PA[:, hf * 512:(hf + 1) * 512]
                    psB = PB[:, hf * 512:(hf + 1) * 512]
                    proj_fm(sE, 8, xT_rhs, j, psA, ("P", 0, hf), xt_keys(j), kE)
                    proj_fm(sO, 8, xT_rhs, j, psB, ("P", 1, hf), xt_keys(j), kO)
                    cs = slice(j * 512, (j + 1) * 512)
                    cosj = tabR[:, 0, cs]
                    sinj = tabR[:, 1, cs]
                    S.tt("dve", tmpA[:], psA, cosj, ALU.mult, [("P", 0, hf), ("tab", j)], ["tmpA"])
                    S.tt("dve", tmpB[:], psB, sinj, ALU.mult, [("P", 1, hf), ("tab", j)], ["tmpB"])
                    S.tt("dve", dst[:, 0, cs], tmpA[:], tmpB[:], ALU.subtract, ["tmpA", "tmpB"], [(dname, 0, j)])
                    S.tt("dve", tmpA[:], psB, cosj, ALU.mult, [("P", 1, hf), ("tab", j)], ["tmpA"])
                    S.tt("dve", tmpB[:], psA, sinj, ALU.mult, [("P", 0, hf), ("tab", j)], ["tmpB"])
                    S.tt("dve", dst[:, 1, cs], tmpA[:], tmpB[:], ALU.add, ["tmpA", "tmpB"], [(dname, 1, j)])
            sV, kV = load_w(w_ret[h, :, 512:1024], 512)
            for t in range(NT):
                hf = t % 2
                ps = PA[:, hf * 512:(hf + 1) * 512]
                S.mms([(ps, xT[:, kc, t * 128:(t + 1) * 128], wring[:, kc, sV * 128:(sV + 4) * 128], kc == 0, kc == 7) for kc in range(8)],
                      reads=[("xT", t)] + kV, writes=[("P", 0, hf)])
                S.act(vt[:, t, :], ps, AF.Copy, [("P", 0, hf), "cvec"], [("vt", t)], scale=vs_h)
            sG, kG = load_w(w_ret[h, :, 1024:1536], 512)
            for c in range(4):
                for j in range(4):
                    hf = j % 2
                    ps = PB[:, hf * 512:(hf + 1) * 512]
                    proj_fm(sG, 8, xT_rhs, j, ps, ("P", 1, hf), xt_keys(j), kG, c0=c * 128)
                    S.act(gT[:, c, j * 512:(j + 1) * 512], ps, AF.Silu, [("P", 1, hf)], [("gT", c, j)])
            S.dma("pool", wo[:], w_oe[h * 512:(h + 1) * 512, :].rearrange("(c p) n -> p c n", p=128), [], ["wo"], "wo")
            PDb = PD[:].bitcast(BF16)

            def o_bank(n):
                return (PC[:, 512:1024], ("P", 2, 1)) if n % 2 == 0 else (PB[:, 512:1024], ("P", 1, 1))

            def head_part(n, h=h, gC=gC, xi_h=xi_h):
                j = n // 4
                cs = slice(n * 128, (n + 1) * 128)
                d2 = n % 2
                qk = [("qT", 0, j), ("qT", 1, j)]
                kk = [("kT", 0, j), ("kT", 1, j)]
                st_ps = PC[:, d2 * 128:(d2 + 1) * 128]
                S.mms([(st_ps, kT[:, c, cs], qT[:, c, cs], c == 0, c == 1) for c in range(2)], reads=qk + kk, writes=[("P", 2, 0, d2)])
                S.tt("dve", innT[:, d2, :], st_ps, maskT[:], ALU.mult, [("P", 2, 0, d2), "maskT"], [("innT", d2)])
                if n < NT - 1:
                    kt_ps = PDb[:, 0:256]
                    S.transposes([(kt_ps[:, c * 128:(c + 1) * 128], kT[:, c, cs]) for c in range(2)], ident[:],
                                 reads=kk + ["ident"], writes=[("P", 3, 0, "kt")])
                    S.act(ktok[:, d2, :], kt_ps, AF.Copy, [("P", 3, 0, "kt")], [("ktok", d2)], scale=gC)
                o_ps, o_key = o_bank(n)
                specs = [(o_ps, innT[:, d2, :], vt[:, n, :], True, n == 0)]
                rk = [("innT", d2), ("vt", n)]
                if n > 0:
                    for c in range(2):
                        specs.append((o_ps, qT[:, c, cs], stb[:, c, :], False, c == 1))
                    rk += qk + [("stb", 0), ("stb", 1)]
                S.mms(specs, reads=rk, writes=[o_key])
                if n < NT - 1:
                    S.mms([(PD[:, 512:1024], ktok[:, d2, 0:128], vt[:, n, :], True, True)], reads=[("ktok", d2), ("vt", n)], writes=[("P", 3, 1)])
                    S.mms([(PB[:, 0:512], ktok[:, d2, 128:256], vt[:, n, :], True, True)], reads=[("ktok", d2), ("vt", n)], writes=[("P", 1, 0)])
                    if n == 0:
                        S.copy("dve", st32[:, 0, :], PD[:, 512:1024], [("P", 3, 1)], [("st32", 0)])
                        S.copy("dve", st32[:, 1, :], PB[:, 0:512], [("P", 1, 0)], [("st32", 1)])
                    else:
                        S.stt("dve", st32[:, 0, :], st32[:, 0, :], gC, PD[:, 512:1024], ALU.mult, ALU.add, [("P", 3, 1), ("st32", 0)], [("st32", 0)])
                        S.stt("dve", st32[:, 1, :], st32[:, 1, :], gC, PB[:, 0:512], ALU.mult, ALU.add, [("P", 1, 0), ("st32", 1)], [("st32", 1)])
                    S.copy("act", stb[:, 0, :], st32[:, 0, :], [("st32", 0)], [("stb", 0)])
                    S.copy("pool", stb[:, 1, :], st32[:, 1, :], [("st32", 1)], [("stb", 1)])
                S.add("dve", (lambda o_ps=o_ps: (lambda e: e.bn_stats(out=bst[:, d2, :], in_=o_ps)))(), reads=[o_key], writes=[("bst", d2)])
                S.add("dve", lambda e: e.bn_aggr(out=bmv[:, d2, :], in_=bst[:, d2, :]), reads=[("bst", d2)], writes=[("bmv", d2)])
                S.ts("dve", sm[:, d2, 0:1], bmv[:, d2, 1:2], xi2[:, h:h + 1], LN_EPS, ALU.mult, ALU.add, [("bmv", d2), "xi2"], [("sm", d2, 0)])
                S.act(sm[:, d2, 1:2], sm[:, d2, 0:1], AF.Sqrt, [("sm", d2, 0)], [("sm", d2, 1)])
                S.add("dve", lambda e: e.reciprocal(out=sm[:, d2, 1:2], in_=sm[:, d2, 1:2]), reads=[("sm", d2, 1)], writes=[("sm", d2, 1)])
                S.ts("dve", sm[:, d2, 2:3], sm[:, d2, 1:2], xi_h, None, ALU.mult, None, [("sm", d2, 1), "cvec"], [("sm", d2, 2)])
                S.stt("dve", sm[:, d2, 3:4], bmv[:, d2, 0:1], -1.0, sm[:, d2, 2:3], ALU.mult, ALU.mult, [("bmv", d2), ("sm", d2, 2)], [("sm", d2, 3)])
                S.act(onb[:, d2, :], o_ps, AF.Identity, [o_key, ("sm", d2, 2), ("sm", d2, 3)], [("onb", d2)],
                      scale=sm[:, d2, 2:3], bias=sm[:, d2, 3:4])

            def tail_part(n):
                j = n // 4
                cs = slice(n * 128, (n + 1) * 128)
                d2 = n % 2
                ct_ps = PDb[:, 512:1024]
                S.transposes([(ct_ps[:, c * 128:(c + 1) * 128], onb[:, d2, c * 128:(c + 1) * 128]) for c in range(4)], ident[:],
                             reads=[("onb", d2), "ident"], writes=[("P", 3, 0, "ct")])
                S.tt("dve", catc[:, d2, :, :], ct_ps.rearrange("p (c k) -> p c k", k=128), gT[:, :, cs], ALU.mult,
                     [("P", 3, 0, "ct")] + [("gT", c, j) for c in range(4)], [("catc", d2)])
                specs = []
                for half in range(2):
                    for c in range(4):
                        specs.append((PA[:, half * 512:(half + 1) * 512], catc[:, d2, c, :], wo[:, c, half * 512:(half + 1) * 512], c == 0, c == 3))
                S.mms(specs, reads=[("catc", d2), "wo"], writes=[("P", 0, 0), ("P", 0, 1)])
                S.tt("dve", yacc[:, n, :], yacc[:, n, :], PA[:], ALU.add, [("P", 0, 0), ("P", 0, 1), ("yacc", n)], [("yacc", n)])

            head_part(0)
            for n in range(NT):
                if n + 1 < NT:
                    head_part(n + 1)
                tail_part(n)
        S.barrier()

    if debug_out == "ret":
        for t in range(NT):
            S.dma("sp", out_d[t * 128:(t + 1) * 128, :], yacc[:, t, :], [("yacc", t)], [("out", t)], "out")
        S.finalize(); top.close(); S.close()
        return nc

    with ExitStack() as esT:
        tab64 = sb(esT, "tab64", [128, 2, S_LEN], F32)
        gen_tables(tab64, inv64, sgn64)
        ebuf = sb(esT, "ebuf", [128, 3, 512], BF16)
        rs_t = sb(esT, "rs_t", [128, 8], F32)
        S.barrier()

        def rope_pair(psA, psB, kA, kB, j, out_ap, out_key, eng_out="dve"):
            cs = slice(j * 512, (j + 1) * 512)
            S.tt("dve", tmpA[:], psA, tab64[:, 0, cs], ALU.mult, [kA, ("tab", j)], ["tmpA"])
            S.tt("dve", tmpB[:], psB, tab64[:, 1, cs], ALU.mult, [kB, ("tab", j)], ["tmpB"])
            S.tt("dve", out_ap, tmpA[:], tmpB[:], ALU.add, ["tmpA", "tmpB"], [out_key])

        with ExitStack() as esM:
            mqT = sb(esM, "mqT", [128, 2, S_LEN], BF16)
            mkvT = sb(esM, "mkvT", [128, 2, S_LEN], BF16)
            kpeT = sb(esM, "kpeT", [128, S_LEN], BF16)
            qpz = [sb(esM, "qpz%d" % i, [128, S_LEN], BF16) for i in range(2)]
            S.add("pool", lambda e: e.memset(qpz[0][64:128, :], 0.0), writes=[("qpz", 0, j) for j in range(4)])
            S.add("pool", lambda e: e.memset(qpz[1][0:64, :], 0.0), writes=[("qpz", 1, j) for j in range(4)])
            qnT = sb(esM, "qnT", [128, S_LEN], BF16)
            knT = sb(esM, "knT", [128, S_LEN], BF16)
            Vaug = sb(esM, "Vaug", [128, NT, 129], BF16)
            gmT = sb(esM, "gmT", [128, S_LEN], BF16)
            catM = sb(esM, "catM", [128, 2, S_LEN], BF16)
            ovl = sb(esM, "ovl", [128, S_LEN], BF16)
            onM = ovl[:].rearrange("p (t k) -> p t k", k=128)
            rawg = ovl[:].bitcast(F32).rearrange("p (c k) -> p c k", k=512)
            sqb = ebuf[:, 0:2, :]
            S.add("pool", lambda e: e.memset(Vaug[:, :, 128:129], 1.0), writes=["VaugOnes"])

            for which, (dstT, gvec, col0) in enumerate(((mqT, qn, 0), (mkvT, kvn, 256))):
                dn = "mqT" if which == 0 else "mkvT"
                s0, k0 = load_w(w_mla[:, col0:col0 + 128], 128)
                s1, k1 = load_w(w_mla[:, col0 + 128:col0 + 256], 128)
                for j in range(4):
                    hf = j % 2
                    psl = [PA[:, hf * 512:(hf + 1) * 512], PB[:, hf * 512:(hf + 1) * 512]]
                    pk = [("P", 0, hf), ("P", 1, hf)]
                    proj_fm(s0, 8, xT_rhs, j, psl[0], pk[0], xt_keys(j), k0)
                    proj_fm(s1, 8, xT_rhs, j, psl[1], pk[1], xt_keys(j), k1)
                    for c in range(2):
                        S.act(sqb[:, c, :], psl[c], AF.Square, [pk[c]], [("sqb", c)])
                        S.act(rawg[:, c, :], psl[c], AF.Copy, [pk[c], "qn", "kvn"], [("rawg", c)], scale=gvec[:, c:c + 1])
                    ss_ps = PC[:, 0:512]
                    S.mms([(ss_ps, ones_b[:], sqb[:, c, :], c == 0, c == 1) for c in range(2)],
                          reads=[("sqb", 0), ("sqb", 1), "ones_b"], writes=[("P", 2, 0)])
                    S.act(tmpA[:], ss_ps, AF.Sqrt, [("P", 2, 0), "cvec"], ["tmpA"], scale=1.0 / 256.0, bias=eps_rms)
                    S.add("dve", lambda e: e.reciprocal(out=tmpA[:], in_=tmpA[:]), reads=["tmpA"], writes=["tmpA"])
                    for c in range(2):
                        S.tt("dve", dstT[:, c, j * 512:(j + 1) * 512], rawg[:, c, :], tmpA[:], ALU.mult,
                             [("rawg", c), "tmpA"], [(dn, j)])
            MQ = [("mqT", j) for j in range(4)]
            MKV = [("mkvT", j) for j in range(4)]
            sA, kA_ = load_w(w_mla[:, 512:640], 128)
            sB, kB_ = load_w(w_mla[:, 640:768], 128)
            for j in range(4):
                hf = j % 2
                psA = PA[:, hf * 512:(hf + 1) * 512]
                psB = PB[:, hf * 512:(hf + 1) * 512]
                proj_fm(sA, 8, xT_rhs, j, psA, ("P", 0, hf), xt_keys(j), kA_)
                proj_fm(sB, 8, xT_rhs, j, psB, ("P", 1, hf), xt_keys(j), kB_)
                rope_pair(psA, psB, ("P", 0, hf), ("P", 1, hf), j, kpeT[:, j * 512:(j + 1) * 512], ("kpeT", j))
            KPE = [("kpeT", j) for j in range(4)]
            S.barrier()

            def mq_rhs(kc, j):
                return mqT[:, kc, j * 512:(j + 1) * 512]

            def mkv_rhs(kc, j):
                return mkvT[:, kc, j * 512:(j + 1) * 512]

            for hp in range(4):
                sA, kA_ = load_w(w_uq[:, 1024 + hp * 128: 1024 + (hp + 1) * 128], 128)
                sB, kB_ = load_w(w_uq[:, 1536 + hp * 128: 1536 + (hp + 1) * 128], 128)
                for j in range(4):
                    hf = j % 2
                    psA = PA[:, hf * 512:(hf + 1) * 512]
                    psB = PB[:, hf * 512:(hf + 1) * 512]
                    proj_fm(sA, 2, mq_rhs, j, psA, ("P", 0, hf), [("mqT", j)], kA_)
                    proj_fm(sB, 2, mq_rhs, j, psB, ("P", 1, hf), [("mqT", j)], kB_)
                    cs = slice(j * 512, (j + 1) * 512)
                    S.tt("dve", tmpA[:], psA, tab64[:, 0, cs], ALU.mult, [("P", 0, hf), ("tab", j)], ["tmpA"])
                    S.tt("dve", tmpB[:], psB, tab64[:, 1, cs], ALU.mult, [("P", 1, hf), ("tab", j)], ["tmpB"])
                    S.tt("dve", qpz[0][0:64, cs], tmpA[0:64, :], tmpB[0:64, :], ALU.add, ["tmpA", "tmpB"], [("qpz", 0, j)])
                    S.tt("dve", qpz[1][64:128, cs], tmpA[64:128, :], tmpB[64:128, :], ALU.add, ["tmpA", "tmpB"], [("qpz", 1, j)])
                S.dma("pool", wo[:, 0:2, :], w_oe[2048 + hp * 256: 2048 + (hp + 1) * 256, :].rearrange("(c p) n -> p c n", p=128),
                      [], ["wo"], "wo")
                for e2 in range(2):
                    h = hp * 2 + e2
                    r0 = 64 * e2
                    sQ, kQ = load_w(w_uq[:, h * 128:(h + 1) * 128], 128)
                    sK, kK = load_w(w_ukv[:, h * 128:(h + 1) * 128], 128)
                    sVv, kVv = load_w(w_ukv[:, 1024 + h * 128: 1024 + (h + 1) * 128], 128)
                    sG, kG = load_w(w_mla[:, 768 + h * 128: 768 + (h + 1) * 128], 128)
                    for j in range(4):
                        hf = j % 2
                        psA = PA[:, hf * 512:(hf + 1) * 512]
                        psB = PB[:, hf * 512:(hf + 1) * 512]
                        proj_fm(sQ, 2, mq_rhs, j, psA, ("P", 0, hf), [("mqT", j)], kQ)
                        S.copy("act", qnT[:, j * 512:(j + 1) * 512], psA, [("P", 0, hf)], [("qnT", j)])
                        proj_fm(sK, 2, mkv_rhs, j, psB, ("P", 1, hf), [("mkvT", j)], kK)
                        S.copy("act", knT[:, j * 512:(j + 1) * 512], psB, [("P", 1, hf)], [("knT", j)])
                    for j in range(4):
                        hf = j % 2
                        psA = PA[:, hf * 512:(hf + 1) * 512]
                        proj_fm(sG, 8, xT_rhs, j, psA, ("P", 0, hf), xt_keys(j), kG)
                        S.act(gmT[:, j * 512:(j + 1) * 512], psA, AF.Silu, [("P", 0, hf)], [("gmT", j)])
                    for t in range(NT):
                        hf = t % 2
                        ps = PB[:, hf * 512: hf * 512 + 128]
                        S.mms([(ps, mkvT[:, kc, t * 128:(t + 1) * 128], wslice(sVv, kc), kc == 0, kc == 1) for kc in range(2)],
                              reads=[("mkvT", t // 4)] + kVv, writes=[("P", 1, hf)])
                        S.copy("act", Vaug[:, t, 0:128], ps, [("P", 1, hf), "VaugOnes"], ["mV"])
                    QN = [("qnT", j) for j in range(4)]
                    KN = [("knT", j) for j in range(4)]
                    qk_list = [
                        (lambda kt: knT[:, kt * 128:(kt + 1) * 128], lambda lo, hi: qnT[:, lo:hi], QN + KN),
                        (lambda kt: kpeT[:, kt * 128:(kt + 1) * 128], lambda lo, hi, e2=e2: qpz[e2][:, lo:hi],
                         [("qpz", e2, j) for j in range(4)] + KPE),
                    ]
                    attention_core("m", qk_list, None, lambda kt: Vaug[:, kt, :], 128, 192.0 ** -0.5,
                                   lambda t: onM[:, t, :], lambda t: ("onM", t), ebuf, rs_t)
                    for jg in range(4):
                        ct_ps = PB[:].bitcast(BF16)[:, 0:512]
                        S.transposes([(ct_ps[:, i * 128:(i + 1) * 128], onM[:, jg * 4 + i, :]) for i in range(4)], ident[:],
                                     reads=[("onM", jg * 4 + i) for i in range(4)] + ["ident"], writes=[("P", 1, 0)])
                        S.tt("dve", catM[:, e2, jg * 512:(jg + 1) * 512], ct_ps, gmT[:, jg * 512:(jg + 1) * 512], ALU.mult,
                             [("P", 1, 0), ("gmT", jg)], [("catM", e2, jg)])
                for t in range(NT):
                    yacc_accumulate(catM, 2, lambda t: ("catM", 0, t // 4), ["wo"] + [("catM", 1, j) for j in range(4)], t)
            S.barrier()

        layer_norm(0, last=(n_layers == 1))
        if n_layers == 1:
            S.finalize(); esT.close(); top.close(); S.close()
            return nc

        with ExitStack() as esL:
            qTl = sb(esL, "qTl", [128, S_LEN], BF16)
            kmh = sb(esL, "kmh", [128, 8], BF16)
            kml = sb(esL, "kml", [128, 8], BF16)
            kmr = sb(esL, "kmr", [128, 8], F32)
            qTb = sb(esL, "qTb", [128, S_LEN], BF16)
            kTb = sb(esL, "kTb", [128, S_LEN], BF16)
            kmean = sb(esL, "kmean", [128, 8], F32)
            Vp = sb(esL, "Vp", [128, NT, 2, 65], BF16)
            gmT = sb(esL, "gmT1", [128, S_LEN], BF16)
            nselT = sb(esL, "nselT", [128, 2, S_LEN], BF16)
            S.add("pool", lambda e: e.memset(nselT[:], 0.0), writes=[("nselT", e, j) for e in range(2) for j in range(4)])
            catL = sb(esL, "catL", [128, 2, S_LEN], BF16)
            onL = sb(esL, "onL", [128, NT, 128], BF16)
            gm = sb(esL, "gm", [128, 128], F32)
            sel = sb(esL, "sel", [128, 128], F32)
            top8 = sb(esL, "top8", [128, 8], F32)
            biasb = sb(esL, "biasb", [128, 128], BF16)
            S.add("pool", lambda e: e.memset(Vp[:, :, :, 64:65], 1.0), writes=["VpOnes"])

            for hp in range(8):
                sl = hp % 2
                for qi in range(2):
                    base = qi * 2048
                    sA, kA_ = load_w(w_l1[:, base + hp * 128: base + (hp + 1) * 128], 128)
                    sB, kB_ = load_w(w_l1[:, base + 1024 + hp * 128: base + 1024 + (hp + 1) * 128], 128)
                    for j in range(4):
                        hf = j % 2
                        psA = PA[:, hf * 512:(hf + 1) * 512]
                        psB = PB[:, hf * 512:(hf + 1) * 512]
                        proj_fm(sA, 8, xT_rhs, j, psA, ("P", 0, hf), xt_keys(j), kA_)
                        proj_fm(sB, 8, xT_rhs, j, psB, ("P", 1, hf), xt_keys(j), kB_)
                        cs = slice(j * 512, (j + 1) * 512)
                        if qi == 0:
                            S.tt("dve", tmpA[:], psA, tab64[:, 0, cs], ALU.mult, [("P", 0, hf), ("tab", j)], ["tmpA"])
                            S.tt("dve", tmpB[:], psB, tab64[:, 1, cs], ALU.mult, [("P", 1, hf), ("tab", j)], ["tmpB"])
                            S.tt("dve", tmpA[:], tmpA[:], tmpB[:], ALU.add, ["tmpA", "tmpB"], ["tmpA"])
                            S.copy("act", qTb[:, cs], tmpA[:], ["tmpA"], [("qTb", j)])
                            S.tt("dve", qTl[:, cs], tmpA[:], qTb[:, cs], ALU.subtract, ["tmpA", ("qTb", j)], [("qTl", j)])
                        else:
                            S.tt("dve", tmpA[:], psA, tab64[:, 0, cs], ALU.mult, [("P", 0, hf), ("tab", j)], ["tmpA"])
                            S.tt("dve", tmpB[:], psB, tab64[:, 1, cs], ALU.mult, [("P", 1, hf), ("tab", j)], ["tmpB"])
                            S.tt("dve", tmpA[:], tmpA[:], tmpB[:], ALU.add, ["tmpA", "tmpB"], ["tmpA"])
                            S.copy("act", kTb[:, cs], tmpA[:], ["tmpA"], [("kTb", j)])
                            S.add("dve", (lambda j=j: (lambda e: e.tensor_reduce(out=kmean[:, 2 * j:2 * j + 2],
                                                                                 in_=tmpA[:].rearrange("p (b l) -> p b l", l=256),
                                                                                 axis=AX.X, op=ALU.add)))(),
                                  reads=["tmpA"], writes=["kmean"])
                QB = [("qTb", j) for j in range(4)]
                QL = [("qTl", j) for j in range(4)]
                S.copy("dve", kmh[:], kmean[:], ["kmean"], ["kmh"])
                S.tt("dve", kmr[:], kmean[:], kmh[:], ALU.subtract, ["kmean", "kmh"], ["kmr"])
                S.copy("dve", kml[:], kmr[:], ["kmr"], ["kml"])
                KB = [("kTb", j) for j in range(4)]
                sVv, kVv = load_w(w_l1[:, 4096 + hp * 128: 4096 + (hp + 1) * 128], 128)
                sG, kG = load_w(w_l1[:, 5120 + hp * 128: 5120 + (hp + 1) * 128], 128)
                for t in range(NT):
                    hf = t % 2
                    ps = PB[:, hf * 512: hf * 512 + 128]
                    S.mms([(ps, xT[:, kc, t * 128:(t + 1) * 128], wslice(sVv, kc), kc == 0, kc == 7) for kc in range(8)],
                          reads=[("xT", t)] + kVv, writes=[("P", 1, hf)])
                    S.copy("act", Vp[:, t, :, 0:64], ps.rearrange("p (a b) -> p a b", b=64), [("P", 1, hf), "VpOnes"], ["lV"])
                for j in range(4):
                    hf = j % 2
                    psA = PA[:, hf * 512:(hf + 1) * 512]
                    proj_fm(sG, 8, xT_rhs, j, psA, ("P", 0, hf), xt_keys(j), kG)
                    S.act(gmT[:, j * 512:(j + 1) * 512], psA, AF.Silu, [("P", 0, hf)], [("gmT", j)])
                if hp % 2 == 0:
                    S.dma("pool", wo[:, 0:2, :], w_oo[hp * 128:(hp + 2) * 128, :].rearrange("(c p) n -> p c n", p=128), [], ["wo"], "wo")
                for e2 in range(2):
                    r0 = 64 * e2
                    g_ps = PC[:, 0:128]
                    gspecs = []
                    for t in range(NT):
                        ts_ = slice(t * 128, (t + 1) * 128)
                        gspecs.append((g_ps[:, t * 8:(t + 1) * 8], qTb[r0:r0 + 64, ts_], kmh[r0:r0 + 64, :], True, False))
                        gspecs.append((g_ps[:, t * 8:(t + 1) * 8], qTl[r0:r0 + 64, ts_], kmh[r0:r0 + 64, :], False, False))
                        gspecs.append((g_ps[:, t * 8:(t + 1) * 8], qTb[r0:r0 + 64, ts_], kml[r0:r0 + 64, :], False, True))
                    S.mms(gspecs, reads=QB + QL + ["kmh", "kml"], writes=[("P", 2, 0)])
                    S.tt("dve", gm[:], g_ps, past01[:], ALU.mult, [("P", 2, 0), "past01"], ["gm"])
                    S.tt("dve", gm[:], gm[:], negoff[:], ALU.add, ["gm", "negoff"], ["gm"])
                    for t in range(NT):
                        S.add("dve", (lambda t=t: (lambda e: e.max(out=top8[:], in_=gm[:, t * 8:(t + 1) * 8])))(), reads=["gm"], writes=["top8"])
                        S.ts("dve", sel[:, t * 8:(t + 1) * 8], gm[:, t * 8:(t + 1) * 8], top8[:, 2:3], None, ALU.is_ge, None, ["gm", "top8"], ["sel"])
                    S.tt("dve", sel[:], sel[:], past01[:], ALU.mult, ["sel", "past01"], ["sel"])
                    S.tt("dve", sel[:], sel[:], own01[:], ALU.add, ["sel", "own01"], ["sel"])
                    S.ts("dve", biasb[:], sel[:], NEGB, -NEGB, ALU.mult, ALU.add, ["sel"], ["biasb"])
                    bt_ps = PC[:].bitcast(BF16)[0:8, 1024:2048]
                    for q4 in range(4):
                        S.transposes([(bt_ps[:, i * 128:(i + 1) * 128], biasb[:, (q4 * 4 + i) * 8:(q4 * 4 + i + 1) * 8]) for i in range(4)], ident[:],
                                     reads=["biasb", "ident"], writes=[("P", 2, 1)])
                        S.copy("act", nselT[0:8, e2, q4 * 512:(q4 + 1) * 512], bt_ps[:, 0:512], [("P", 2, 1)], [("nselT", e2, q4)])
                    NS = [("nselT", e2, j) for j in range(4)]

                    def bias_fn(kt, jg, lo, hi, e2=e2, NS=NS):
                        nb_ = kt // 2
                        if nb_ == 2 * jg + 1:
                            return None
                        return (lambda kt_, nb_=nb_: Eall[:, nb_ * 128:(nb_ + 1) * 128],
                                lambda lo_, hi_, e2=e2: nselT[:, e2, lo_:hi_], NS + ["Eall"])

                    qk_list = [(lambda kt, r0=r0: kTb[r0:r0 + 64, kt * 128:(kt + 1) * 128],
                                lambda lo, hi, r0=r0: qTb[r0:r0 + 64, lo:hi], QB + KB)]
                    attention_core("l", qk_list, bias_fn, (lambda kt, e2=e2: Vp[:, kt, e2, :]), 64, 0.125,
                                   (lambda t, e2=e2: onL[:, t, e2 * 64:(e2 + 1) * 64]), (lambda t, e2=e2: ("onL", t, e2)), ebuf, rs_t)
                for jg in range(4):
                    ct_ps = PB[:].bitcast(BF16)[:, 0:512]
                    S.transposes([(ct_ps[:, i * 128:(i + 1) * 128], onL[:, jg * 4 + i, :]) for i in range(4)], ident[:],
                                 reads=[("onL", jg * 4 + i, e) for i in range(4) for e in range(2)] + ["ident"], writes=[("P", 1, 0)])
                    S.tt("dve", catL[:, sl, jg * 512:(jg + 1) * 512], ct_ps, gmT[:, jg * 512:(jg + 1) * 512], ALU.mult,
                         [("P", 1, 0), ("gmT", jg)], [("catL", sl, jg)])
                if hp % 2 == 1:
                    for t in range(NT):
                        yacc_accumulate(catL, 2, lambda t: ("catL", 0, t // 4), ["wo"] + [("catL", 1, j) for j in range(4)], t)
            S.barrier()
        layer_norm(1, last=True)

    S.finalize()
    top.close()
    S.close()
    return nc


def _host_constants():
    c = {}
    c["c_ident"] = np.eye(128, dtype=np.float32)
    k = np.arange(128)
    c["c_mask"] = (k[None, :] >= k[:, None]).astype(np.float32)
    vec = np.zeros((128, 16), np.float32)
    vec[:, 0] = 1.0 / (10000.0 ** np.linspace(0.0, 1.0, 128, dtype=np.float32))
    inv64 = 1.0 / (10000.0 ** (np.arange(0, 64, 2, dtype=np.float32) / 64.0))
    vec[:, 1] = inv64[np.arange(128) % 32]
    vec[:, 2] = np.where((np.arange(128) % 64) < 32, -1.0, 1.0)
    vec[:, 3] = LN_EPS
    vec[:, 4] = RMS_EPS
    xi2 = np.zeros((128, 4), np.float32)
    for h in range(4):
        g = 1.0 - 2.0 ** (-5.0 - h)
        xi = (g ** (k + 1.0)) * (256.0 ** -0.5)
        vec[:, 8 + h] = xi
        vec[:, 12 + h] = g ** (-(k + 1.0))
        xi2[:, h] = xi * xi
    c["c_vec"] = vec
    c["c_xi2"] = xi2
    E = np.zeros((8, 8, 128), np.float32)
    for n in range(8):
        E[n, n, :] = 1.0
    c["c_E"] = E.reshape(8, 1024)
    past = np.zeros((16, 8), np.float32)
    own = np.zeros((16, 8), np.float32)
    for t in range(16):
        qb = t // 2
        past[t, :qb] = 1.0
        own[t, qb] = 1.0
    c["c_past"] = np.broadcast_to(past.reshape(1, 128), (128, 128)).copy()
    c["c_own"] = np.broadcast_to(own.reshape(1, 128), (128, 128)).copy()
    c["c_negoff"] = ((c["c_past"] - 1.0) * 1e30).astype(np.float32)
    return c


def _layout_weights(w_in_even, q_norm_even, w_uq_even, kv_norm_even, w_ukv_even, w_out_even, w_in_odd, w_out_odd):
    wi = np.asarray(w_in_even[0])
    o = {}
    ev = np.arange(0, 256, 2)
    od = np.arange(1, 256, 2)
    w_ret = np.empty((4, 1024, 1536), np.float32)
    for h in range(4):
        rq = wi[:, h * 256:(h + 1) * 256]
        rk = wi[:, 1024 + h * 256: 1024 + (h + 1) * 256]
        w_ret[h, :, 0:128] = rq[:, ev]
        w_ret[h, :, 128:256] = rq[:, od]
        w_ret[h, :, 256:384] = rk[:, ev]
        w_ret[h, :, 384:512] = rk[:, od]
        w_ret[h, :, 512:1024] = wi[:, 2048 + h * 512: 2048 + (h + 1) * 512]
        w_ret[h, :, 1024:1536] = wi[:, 4096 + h * 512: 4096 + (h + 1) * 512]
    o["w_ret"] = w_ret
    b = 6144
    mq = wi[:, b:b + 256]
    mkv = wi[:, b + 256:b + 512]
    mkr = wi[:, b + 512:b + 576]
    mg = wi[:, b + 576:b + 1600]
    swap64 = np.concatenate([np.arange(32, 64), np.arange(0, 32)])
    o["w_mla"] = np.ascontiguousarray(np.concatenate([mq, mkv, mkr, mkr, mkr[:, swap64], mkr[:, swap64], mg], axis=1))
    wuq = np.asarray(w_uq_even[0])
    nope = np.concatenate([wuq[:, h * 192: h * 192 + 128] for h in range(8)], axis=1)
    pe1 = np.concatenate([wuq[:, h * 192 + 128: h * 192 + 192] for h in range(8)], axis=1)
    pe2 = np.concatenate([wuq[:, h * 192 + 128: h * 192 + 192][:, swap64] for h in range(8)], axis=1)
    o["w_uq"] = np.ascontiguousarray(np.concatenate([nope, pe1, pe2], axis=1))
    wukv = np.asarray(w_ukv_even[0])
    kn = np.concatenate([wukv[:, h * 256: h * 256 + 128] for h in range(8)], axis=1)
    vv = np.concatenate([wukv[:, h * 256 + 128: h * 256 + 256] for h in range(8)], axis=1)
    o["w_ukv"] = np.ascontiguousarray(np.concatenate([kn, vv], axis=1))
    o["w_oe"] = np.ascontiguousarray(np.asarray(w_out_even[0]))
    wo_ = np.asarray(w_in_odd[0])
    sw = np.concatenate([h * 64 + swap64 for h in range(16)])
    q = wo_[:, 0:1024]
    kk = wo_[:, 1024:2048]
    o["w_l1"] = np.ascontiguousarray(np.concatenate([q, q[:, sw], kk, kk[:, sw], wo_[:, 2048:3072], wo_[:, 3072:4096]], axis=1))
    o["w_oo"] = np.ascontiguousarray(np.asarray(w_out_odd[0]))
    o["qn"] = np.ascontiguousarray(np.asarray(q_norm_even[0]).reshape(2, 128).T)
    o["kvn"] = np.ascontiguousarray(np.asarray(kv_norm_even[0]).reshape(2, 128).T)
    return o


def make_in_maps(x, positions, w_in_even, q_norm_even, w_uq_even, kv_norm_even, w_ukv_even,
                 w_out_even, w_in_odd, w_out_odd, ln_g, ln_b):
    shared = _host_constants()
    shared.update(_layout_weights(w_in_even, q_norm_even, w_uq_even, kv_norm_even, w_ukv_even, w_out_even, w_in_odd, w_out_odd))
    shared["lng"] = np.ascontiguousarray(np.asarray(ln_g, dtype=np.float32))
    shared["lnb"] = np.ascontiguousarray(np.asarray(ln_b, dtype=np.float32))
    x = np.asarray(x)
    positions = np.asarray(positions)
    in_maps = []
    for b in range(8):
        m = dict(shared)
        m["x"] = np.ascontiguousarray(x[b])
        m["pos"] = np.ascontiguousarray(positions[b].astype(np.int32).reshape(1, S_LEN))
        in_maps.append(m)
    return in_maps


def kernel(x, positions, w_in_even, q_norm_even, w_uq_even, kv_norm_even, w_ukv_even,
           w_out_even, w_in_odd, w_out_odd, ln_g, ln_b):
    in_maps = make_in_maps(x, positions, w_in_even, q_norm_even, w_uq_even, kv_norm_even, w_ukv_even,
                           w_out_even, w_in_odd, w_out_odd, ln_g, ln_b)
    nc = build_program(2)
    res = run_bass_kernel_spmd(nc, in_maps, core_ids=list(range(8)))
    out = np.stack([np.asarray(r["out"]) for r in res.results], axis=0).astype(np.float32)
    if os.environ.get("K_DUMP"):
        np.save(os.environ["K_DUMP"], out)
    return out
```

```python
import math
import os
from contextlib import ExitStack

import numpy as np
import concourse.bass as bass
import concourse.mybir as mybir
from concourse.bass_utils import run_bass_kernel_spmd

F32 = mybir.dt.float32
BF16 = mybir.dt.bfloat16
I32 = mybir.dt.int32
ALU = mybir.AluOpType
AF = mybir.ActivationFunctionType
AX = mybir.AxisListType

S_LEN = 2048
D = 1024
NT = 16
ALPHA = 4.0 ** 0.25
LN_EPS = 1e-5
RMS_EPS = 1e-6
NEGB = 30000.0


class Sched:
    ENGS = ("pe", "act", "dve", "pool", "sp")

    def __init__(self, nc):
        self.nc = nc
        self.ops = []
        self.last_w = {}
        self.readers = {}
        self.eng_sems = {}
        self.dma_sems = {}
        self.dma_cnt = {}
        self._stack = []
        self.barrier_deps = set()
        self.last_on_eng = {}
        self.last_dma = {}
        for e in ("pe", "act", "dve", "pool"):
            self.eng_sems[e] = self.sem("s_" + e)

    def sem(self, name):
        cm = self.nc.semaphore(name)
        s = cm.__enter__()
        self._stack.append(cm)
        return s

    def add(self, eng, fn, reads=(), writes=(), dma=None):
        idx = len(self.ops)
        deps = set(self.barrier_deps)
        for k in reads:
            w = self.last_w.get(k)
            if w is not None:
                deps.add(w)
        for k in writes:
            w = self.last_w.get(k)
            if w is not None:
                deps.add(w)
            for r in self.readers.get(k, ()):
                deps.add(r)
        op = dict(eng=eng, fn=fn, deps=deps, dma=dma, idx=idx, signal=False, tok=None)
        self.ops.append(op)
        for k in reads:
            self.readers.setdefault(k, []).append(idx)
        for k in writes:
            self.last_w[k] = idx
            self.readers[k] = []
        if dma is not None:
            self.last_dma[dma] = idx
        else:
            self.last_on_eng[eng] = idx
        return idx

    def barrier(self):
        self.barrier_deps = set(self.last_on_eng.values()) | set(self.last_dma.values())

    def finalize(self):
        ops = self.ops
        for op in ops:
            for d in op["deps"]:
                p = ops[d]
                if p["dma"] is not None:
                    continue
                if p["eng"] == "pe" and op["eng"] == "pe" and op["dma"] is None:
                    continue
                p["signal"] = True
        cnt = {e: 0 for e in self.eng_sems}
        for op in ops:
            if op["dma"] is not None:
                key = op["dma"]
                if key not in self.dma_sems:
                    self.dma_sems[key] = self.sem("d_%d" % len(self.dma_sems))
                    self.dma_cnt[key] = 0
                self.dma_cnt[key] += 16
                op["tok"] = (self.dma_sems[key], self.dma_cnt[key])
            elif op["signal"]:
                cnt[op["eng"]] += 1
                op["tok"] = (self.eng_sems[op["eng"]], cnt[op["eng"]])
        streams = {e: [] for e in self.ENGS}
        for op in ops:
            streams[op["eng"]].append(op)
        sched = self

        def emit(engname, engine):
            known = {}
            for op in streams[engname]:
                need = {}
                for d in op["deps"]:
                    p = ops[d]
                    if p["tok"] is None:
                        continue
                    if p["dma"] is None and p["eng"] == "pe" and engname == "pe" and op["dma"] is None:
                        continue
                    s, v = p["tok"]
                    if need.get(id(s), (None, 0))[1] < v:
                        need[id(s)] = (s, v)
                for sid, (s, v) in need.items():
                    if known.get(sid, 0) >= v:
                        continue
                    engine.wait_ge(s, v)
                    known[sid] = v
                ins = op["fn"](engine)
                if op["dma"] is not None:
                    ins.then_inc(op["tok"][0], 16)
                elif op["signal"]:
                    ins.then_inc(op["tok"][0], 1)
            if engname == "sp":
                for key, s in sched.dma_sems.items():
                    engine.wait_ge(s, sched.dma_cnt[key])
                for e, s in sched.eng_sems.items():
                    if cnt[e] > 0:
                        engine.wait_ge(s, cnt[e])

        with self.nc.Block() as block:
            @block.tensor
            def _(e):
                emit("pe", e)

            @block.scalar
            def _(e):
                emit("act", e)

            @block.vector
            def _(e):
                emit("dve", e)

            @block.gpsimd
            def _(e):
                emit("pool", e)

            @block.sync
            def _(e):
                emit("sp", e)

    def close(self):
        while self._stack:
            self._stack.pop().__exit__(None, None, None)

    def dma(self, q, out, in_, reads, writes, key):
        return self.add(q, lambda e: e.dma_start(out=out, in_=in_), reads=reads, writes=writes, dma=key)

    def mms(self, specs, reads, writes):
        def fn(e):
            ins = None
            for (o, l, r, st, sp) in specs:
                ins = e.matmul(o, lhsT=l, rhs=r, start=st, stop=sp)
            return ins
        return self.add("pe", fn, reads=reads, writes=writes)

    def transposes(self, specs, ident, reads, writes):
        def fn(e):
            ins = None
            for (o, i) in specs:
                ins = e.transpose(out=o, in_=i, identity=ident)
            return ins
        return self.add("pe", fn, reads=reads, writes=writes)

    def act(self, out, in_, func, reads, writes, scale=1.0, bias=None, eng="act"):
        if bias is None:
            return self.add(eng, lambda e: e.activation(out=out, in_=in_, func=func, scale=scale), reads=reads, writes=writes)
        return self.add(eng, lambda e: e.activation(out=out, in_=in_, func=func, scale=scale, bias=bias), reads=reads, writes=writes)

    def tt(self, eng, out, in0, in1, op, reads, writes):
        return self.add(eng, lambda e: e.tensor_tensor(out=out, in0=in0, in1=in1, op=op), reads=reads, writes=writes)

    def ts(self, eng, out, in0, s1, s2, op0, op1, reads, writes):
        if s2 is None:
            return self.add(eng, lambda e: e.tensor_scalar(out=out, in0=in0, scalar1=s1, scalar2=None, op0=op0), reads=reads, writes=writes)
        return self.add(eng, lambda e: e.tensor_scalar(out=out, in0=in0, scalar1=s1, scalar2=s2, op0=op0, op1=op1), reads=reads, writes=writes)

    def stt(self, eng, out, in0, scalar, in1, op0, op1, reads, writes):
        return self.add(eng, lambda e: e.scalar_tensor_tensor(out=out, in0=in0, scalar=scalar, in1=in1, op0=op0, op1=op1), reads=reads, writes=writes)

    def copy(self, eng, out, in_, reads, writes):
        if eng == "act":
            return self.add(eng, lambda e: e.copy(out=out, in_=in_), reads=reads, writes=writes)
        return self.add(eng, lambda e: e.tensor_copy(out=out, in_=in_), reads=reads, writes=writes)


def build_program(n_layers=2, debug_out=None):
    nc = bass.Bass("TRN2", target_bir_lowering=False)

    def dram_in(name, shape, dt=F32):
        return nc.dram_tensor(name, list(shape), dt, kind="ExternalInput").ap()

    x_d = dram_in("x", [S_LEN, D])
    pos_d = dram_in("pos", [1, S_LEN], I32)
    w_ret = dram_in("w_ret", [4, D, 1536])
    w_mla = dram_in("w_mla", [D, 1792])
    w_uq = dram_in("w_uq", [256, 2048])
    w_ukv = dram_in("w_ukv", [256, 2048])
    w_oe = dram_in("w_oe", [3072, D])
    w_l1 = dram_in("w_l1", [D, 6144])
    w_oo = dram_in("w_oo", [D, D])
    qn_d = dram_in("qn", [128, 2])
    kvn_d = dram_in("kvn", [128, 2])
    lng_d = dram_in("lng", [2, D])
    lnb_d = dram_in("lnb", [2, D])
    c_ident = dram_in("c_ident", [128, 128])
    c_mask = dram_in("c_mask", [128, 128])
    c_vec = dram_in("c_vec", [128, 16])
    c_xi2 = dram_in("c_xi2", [128, 4])
    c_E = dram_in("c_E", [8, 1024])
    c_past = dram_in("c_past", [128, 128])
    c_own = dram_in("c_own", [128, 128])
    c_negoff = dram_in("c_negoff", [128, 128])
    out_d = nc.dram_tensor("out", [S_LEN, D], F32, kind="ExternalOutput").ap()

    S = Sched(nc)
    top = ExitStack()

    def sb(es, name, shape, dt):
        return es.enter_context(nc.sbuf_tensor("sb_" + name, list(shape), dt))

    yacc = sb(top, "yacc", [128, NT, D], F32)
    xT = sb(top, "xT", [128, 8, S_LEN], BF16)
    wring = sb(top, "wring", [128, 8, 1024], BF16)
    wo = sb(top, "wo", [128, 4, D], BF16)
    ident = sb(top, "ident", [128, 128], BF16)
    maskT = sb(top, "maskT", [128, 128], BF16)
    ones_b = sb(top, "ones_b", [128, 128], BF16)
    cvec = sb(top, "cvec", [128, 16], F32)
    xi2 = sb(top, "xi2", [128, 4], F32)
    qn = sb(top, "qn", [128, 2], F32)
    kvn = sb(top, "kvn", [128, 2], F32)
    Eall = sb(top, "Eall", [128, 1024], BF16)
    past01 = sb(top, "past01", [128, 128], F32)
    own01 = sb(top, "own01", [128, 128], F32)
    negoff = sb(top, "negoff", [128, 128], F32)
    tmpA = sb(top, "tmpA", [128, 512], F32)
    tmpB = sb(top, "tmpB", [128, 512], F32)
    tmpI = sb(top, "tmpI", [128, 512], I32)

    PA = top.enter_context(nc.psum_tensor("PA", [128, 1024], F32))
    PB = top.enter_context(nc.psum_tensor("PB", [128, 1024], F32))
    PC = top.enter_context(nc.psum_tensor("PC", [128, 1024], F32))
    PD = top.enter_context(nc.psum_tensor("PD", [128, 1024], F32))

    inv_r = cvec[:, 0:1]
    inv64 = cvec[:, 1:2]
    sgn64 = cvec[:, 2:3]
    eps_ln = cvec[:, 3:4]
    eps_rms = cvec[:, 4:5]

    ring_pos = [0]

    def load_w(src2d, ncols):
        nsl = ncols // 128
        s0 = ring_pos[0]
        if s0 + nsl > 8:
            s0 = 0
        ring_pos[0] = (s0 + nsl) % 8
        kc = src2d.shape[0] // 128
        keys = [("wr", s) for s in range(s0, s0 + nsl)]
        S.dma("pool", wring[:, 0:kc, s0 * 128:(s0 + nsl) * 128], src2d.rearrange("(c p) n -> p c n", p=128),
              reads=[], writes=keys, key=("wr", s0))
        return s0, keys

    def wslice(s0, kc, c0=0, ncols=128):
        return wring[:, kc, s0 * 128 + c0: s0 * 128 + c0 + ncols]

    S.dma("pool", ident[:], c_ident, [], ["ident"], "c0")
    S.dma("pool", maskT[:], c_mask, [], ["maskT"], "c1")
    S.add("pool", lambda e: e.memset(Eall[:], 0.0), writes=["Eall"])
    S.dma("pool", Eall[0:8, :], c_E, [], ["Eall"], "c2")
    S.dma("sp", cvec[:], c_vec, [], ["cvec"], "c3")
    S.dma("sp", xi2[:], c_xi2, [], ["xi2"], "c4")
    S.dma("sp", qn[:], qn_d, [], ["qn"], "c5")
    S.dma("sp", kvn[:], kvn_d, [], ["kvn"], "c6")
    S.dma("sp", past01[:], c_past, [], ["past01"], "c7")
    S.dma("sp", own01[:], c_own, [], ["own01"], "c8")
    S.dma("sp", negoff[:], c_negoff, [], ["negoff"], "c9")
    S.add("pool", lambda e: e.memset(ones_b[:], 1.0), writes=["ones_b"])

    def gen_tables(tab, inv_ap, sgn_ap):
        for j in range(4):
            cs = slice(j * 512, (j + 1) * 512)
            S.dma("sp", tmpI[:], bass.AP(pos_d.tensor, j * 512, [[0, 128], [1, 512]]), [], ["tmpI"], "posld")
            S.copy("dve", tmpA[:], tmpI[:], ["tmpI"], ["tmpA"])
            S.ts("dve", tmpA[:], tmpA[:], inv_ap, None, ALU.mult, None, ["tmpA", "cvec"], ["tmpA"])
            for which in (1, 0):
                shift = 0.0 if which == 1 else 0.25
                S.ts("dve", tmpI[:], tmpA[:], 1.0 / (2 * math.pi), shift, ALU.mult, ALU.add, ["tmpA"], ["tmpI"])
                S.copy("dve", tmpB[:], tmpI[:], ["tmpI"], ["tmpB"])
                S.stt("dve", tmpB[:], tmpB[:], -2 * math.pi, tmpA[:], ALU.mult, ALU.add, ["tmpA", "tmpB"], ["tmpB"])
                if which == 0:
                    S.ts("dve", tmpB[:], tmpB[:], math.pi / 2, None, ALU.add, None, ["tmpB"], ["tmpB"])
                S.ts("dve", tmpB[:], tmpB[:], -3.141592, 3.141592, ALU.max, ALU.min, ["tmpB"], ["tmpB"])
                if which == 1 and sgn_ap is not None:
                    S.act(tab[:, 1, cs], tmpB[:], AF.Sin, ["tmpB", "cvec"], [("tab", j)], scale=sgn_ap)
                else:
                    S.act(tab[:, which, cs], tmpB[:], AF.Sin, ["tmpB"], [("tab", j)])

    with ExitStack() as es0:
        xs = sb(es0, "xs", [128, 2, D], F32)
        xb = sb(es0, "xb", [128, 2, D], BF16)
        for t in range(NT):
            s = t % 2
            S.dma("sp", xs[:, s, :], x_d[t * 128:(t + 1) * 128, :], [], [("xs", s)], ("xs", s))
            S.add("act", (lambda s=s, t=t: (lambda e: e.mul(out=yacc[:, t, :], in_=xs[:, s, :], mul=ALPHA)))(),
                  reads=[("xs", s)], writes=[("yacc", t)])
            S.copy("dve", xb[:, s, :], xs[:, s, :], [("xs", s)], [("xb", s)])
            pt = PA[:].bitcast(BF16) if s == 0 else PB[:].bitcast(BF16)
            S.transposes([(pt[:, c * 128:(c + 1) * 128], xb[:, s, c * 128:(c + 1) * 128]) for c in range(8)], ident[:],
                         reads=[("xb", s), "ident"], writes=[("P", s, 0)])
            S.copy("dve" if s == 0 else "act", xT[:, :, t * 128:(t + 1) * 128],
                   pt[:, 0:1024].rearrange("p (c k) -> p c k", k=128), [("P", s, 0)], [("xT", t)])
        S.barrier()
    XT_ALL = [("xT", t) for t in range(NT)]

    def xt_keys(j):
        return [("xT", 4 * j + i) for i in range(4)]

    def layer_norm(layer, last):
        with ExitStack() as es:
            gb = sb(es, "gb%d" % layer, [128, 2, D], F32)
            mv = sb(es, "mv%d" % layer, [128, NT, 2], F32)
            st = sb(es, "st%d" % layer, [128, 2, 6], F32)
            rstd = sb(es, "rstd%d" % layer, [128, NT], F32)
            nb = sb(es, "nb%d" % layer, [128, NT], F32)
            x1 = sb(es, "x1_%d" % layer, [128, 2, D], F32)
            x1b = sb(es, "x1b_%d" % layer, [128, 2, D], BF16)
            S.dma("sp", gb[:, 0, :], bass.AP(lng_d.tensor, layer * D, [[0, 128], [1, D]]), [], ["gb0"], "gb0")
            S.dma("sp", gb[:, 1, :], bass.AP(lnb_d.tensor, layer * D, [[0, 128], [1, D]]), [], ["gb1"], "gb1")
            for t in range(NT):
                for hh in range(2):
                    S.add("dve", (lambda t=t, hh=hh: (lambda e: e.bn_stats(out=st[:, hh, :], in_=yacc[:, t, hh * 512:(hh + 1) * 512])))(),
                          reads=[("yacc", t)], writes=[("st", hh)])
                S.add("dve", (lambda t=t: (lambda e: e.bn_aggr(out=mv[:, t, :], in_=st[:].rearrange("p a b -> p (a b)"))))(),
                      reads=[("st", 0), ("st", 1)], writes=["mv"])
            S.act(rstd[:], mv[:, :, 1], AF.Sqrt, ["mv", "cvec"], ["rstd"], bias=eps_ln)
            S.add("dve", lambda e: e.reciprocal(out=rstd[:], in_=rstd[:]), reads=["rstd"], writes=["rstd"])
            S.stt("dve", nb[:], mv[:, :, 0], -1.0, rstd[:], ALU.mult, ALU.mult, ["mv", "rstd"], ["nb"])
            for t in range(NT):
                s = t % 2
                S.act(x1[:, s, :], yacc[:, t, :], AF.Identity, [("yacc", t), "rstd", "nb"], [("x1", s), ("x1h", s)],
                      scale=rstd[:, t:t + 1], bias=nb[:, t:t + 1])
                S.tt("dve", x1[:, s, :], x1[:, s, :], gb[:, 0, :], ALU.mult, [("x1", s), "gb0"], [("x1", s), ("x1h", s)])
                S.tt("pool", x1[:, s, 640:1024], x1[:, s, 640:1024], gb[:, 1, 640:1024], ALU.add, [("x1h", s), "gb1"], [("x1h", s)])
                S.tt("dve", x1[:, s, 0:640], x1[:, s, 0:640], gb[:, 1, 0:640], ALU.add, [("x1", s), "gb1"], [("x1", s)])
                if last:
                    S.dma("sp", out_d[t * 128:(t + 1) * 128, :], x1[:, s, :], [("x1", s), ("x1h", s)], [("out", t)], ("out", s))
                else:
                    S.add("act", (lambda s=s, t=t: (lambda e: e.mul(out=yacc[:, t, :], in_=x1[:, s, :], mul=ALPHA)))(),
                          reads=[("x1", s), ("x1h", s)], writes=[("yacc", t)])
                    S.copy("act", x1b[:, s, :], x1[:, s, :], [("x1", s), ("x1h", s)], [("x1b", s)])
                    pt = PA[:].bitcast(BF16) if s == 0 else PB[:].bitcast(BF16)
                    S.transposes([(pt[:, c * 128:(c + 1) * 128], x1b[:, s, c * 128:(c + 1) * 128]) for c in range(8)], ident[:],
                                 reads=[("x1b", s), "ident"], writes=[("P", s, 0)])
                    S.copy("dve", xT[:, :, t * 128:(t + 1) * 128],
                           pt[:, 0:1024].rearrange("p (c k) -> p c k", k=128), [("P", s, 0)], [("xT", t)])
            S.barrier()

    def proj_fm(slot, wkc, rhs_fn, j, out_ps, pkey, rkeys, wkeys, c0=0):
        specs = []
        for kc in range(wkc):
            specs.append((out_ps, wslice(slot, kc, c0), rhs_fn(kc, j), kc == 0, kc == wkc - 1))
        S.mms(specs, reads=list(rkeys) + list(wkeys), writes=[pkey])

    def xT_rhs(kc, j):
        return xT[:, kc, j * 512:(j + 1) * 512]

    def yacc_accumulate(cat, nslots, catkey_fn, wo_keys, t):
        specs = []
        for half in range(2):
            for s in range(nslots):
                specs.append((PA[:, half * 512:(half + 1) * 512], cat[:, s, t * 128:(t + 1) * 128],
                              wo[:, s, half * 512:(half + 1) * 512], s == 0, s == nslots - 1))
        S.mms(specs, reads=[catkey_fn(t)] + list(wo_keys), writes=[("P", 0, 0), ("P", 0, 1)])
        S.tt("dve", yacc[:, t, :], yacc[:, t, :], PA[:], ALU.add, [("P", 0, 0), ("P", 0, 1), ("yacc", t)], [("yacc", t)])

    def attention_core(name, qk_list, bias_fn, Vfn, dv, scale, on_out, on_key, ebuf, rs_t):
        iters = [(jg, kt) for jg in range(4) for kt in range(4 * jg + 4)]

        def acc_bank(jg):
            return (PD, 3) if jg % 2 == 0 else (PA, 0)

        def emit_score(i):
            jg, kt = iters[i]
            c0 = max(0, kt - 4 * jg)
            q_lo = jg * 512 + c0 * 128
            q_hi = (jg + 1) * 512
            half = i % 2
            ps = PC[:, half * 512 + c0 * 128: (half + 1) * 512]
            specs = []
            rk = []
            items = list(qk_list)
            b = bias_fn(kt, jg, q_lo, q_hi) if bias_fn is not None else None
            if b is not None:
                items = [b] + items
            for ii, (kf, qf, keys) in enumerate(items):
                specs.append((ps, kf(kt), qf(q_lo, q_hi), ii == 0, ii == len(items) - 1))
                rk += keys
            S.mms(specs, reads=rk, writes=[("P", 2, half)])
            es_ = i % 3
            S.act(ebuf[:, es_, c0 * 128:512], ps, AF.Exp, [("P", 2, half)], [(name + "e", es_)], scale=scale)
            if kt >= 4 * jg:
                S.tt("dve", ebuf[:, es_, c0 * 128:(c0 + 1) * 128], ebuf[:, es_, c0 * 128:(c0 + 1) * 128], maskT[:], ALU.mult,
                     [(name + "e", es_), "maskT"], [(name + "e", es_)])

        def emit_pv(i):
            jg, kt = iters[i]
            c0 = max(0, kt - 4 * jg)
            es_ = i % 3
            PT, pk = acc_bank(jg)
            pv = []
            for qi in range(c0, 4):
                base = (qi // 2) * 512 + (qi % 2) * (dv + 1)
                pv.append((PT[:, base: base + dv + 1], ebuf[:, es_, qi * 128:(qi + 1) * 128], Vfn(kt),
                           kt == 0 and qi % 2 == 0, kt == 4 * jg + qi and qi % 2 == 1))
            S.mms(pv, reads=[(name + "e", es_), name + "V"], writes=[("P", pk, 0), ("P", pk, 1)])
            if kt == 4 * jg + 3:
                for qi in range(4):
                    base = (qi // 2) * 512 + (qi % 2) * (dv + 1)
                    t = jg * 4 + qi
                    rsl = rs_t[:, (jg % 2) * 4 + qi:(jg % 2) * 4 + qi + 1]
                    S.add("dve", (lambda base=base, rsl=rsl, PT=PT: (lambda e: e.reciprocal(out=rsl, in_=PT[:, base + dv: base + dv + 1])))(),
                          reads=[("P", pk, 0), ("P", pk, 1)], writes=[(name + "rs", jg % 2, qi)])
                    S.act(on_out(t), PT[:, base: base + dv], AF.Copy, [("P", pk, 0), ("P", pk, 1), (name + "rs", jg % 2, qi)], [on_key(t)],
                          scale=rsl)

        emit_score(0)
        for i in range(len(iters)):
            if i + 1 < len(iters):
                emit_score(i + 1)
            emit_pv(i)

    with ExitStack() as esR:
        tabR = sb(esR, "tabR", [128, 2, S_LEN], F32)
        qT = sb(esR, "qT", [128, 2, S_LEN], BF16)
        kT = sb(esR, "kT", [128, 2, S_LEN], BF16)
        vt = sb(esR, "vt", [128, NT, 512], BF16)
        gT = sb(esR, "gT", [128, 4, S_LEN], BF16)
        st32 = sb(esR, "st32", [128, 2, 512], F32)
        stb = sb(esR, "stb", [128, 2, 512], BF16)
        ktok = sb(esR, "ktok", [128, 2, 256], BF16)
        innT = sb(esR, "innT", [128, 2, 128], BF16)
        onb = sb(esR, "onb", [128, 2, 512], BF16)
        catc = sb(esR, "catc", [128, 2, 4, 128], BF16)
        bst = sb(esR, "bst", [128, 2, 6], F32)
        bmv = sb(esR, "bmv", [128, 2, 2], F32)
        sm = sb(esR, "sm", [128, 2, 4], F32)

        gen_tables(tabR, inv_r, None)
        TAB = [("tab", j) for j in range(4)]

        for h in range(4):
            g = 1.0 - 2.0 ** (-5.0 - h)
            gC = g ** 128
            xi_h = cvec[:, 8 + h:9 + h]
            vs_h = cvec[:, 12 + h:13 + h]
            for qi, dst in enumerate((qT, kT)):
                dname = "qT" if qi == 0 else "kT"
                sE, kE = load_w(w_ret[h, :, qi * 256: qi * 256 + 128], 128)
                sO, kO = load_w(w_ret[h, :, qi * 256 + 128: qi * 256 + 256], 128)
                for j in range(4):
                    hf = j % 2
                    psA = PA[:, hf * 512:(hf + 1) * 512]
                    psB = PB[:, hf * 512:(hf + 1) * 512]
                    proj_fm(sE, 8, xT_rhs, j, psA, ("P", 0, hf), xt_keys(j), kE)
                    proj_fm(sO, 8, xT_rhs, j, psB, ("P", 1, hf), xt_keys(j), kO)
                    cs = slice(j * 512, (j + 1) * 512)
                    cosj = tabR[:, 0, cs]
                    sinj = tabR[:, 1, cs]
                    S.tt("dve", tmpA[:], psA, cosj, ALU.mult, [("P", 0, hf), ("tab", j)], ["tmpA"])
                    S.tt("dve", tmpB[:], psB, sinj, ALU.mult, [("P", 1, hf), ("tab", j)], ["tmpB"])
                    S.tt("dve", dst[:, 0, cs], tmpA[:], tmpB[:], ALU.subtract, ["tmpA", "tmpB"], [(dname, 0, j)])
                    S.tt("dve", tmpA[:], psB, cosj, ALU.mult, [("P", 1, hf), ("tab", j)], ["tmpA"])
                    S.tt("dve", tmpB[:], psA, sinj, ALU.mult, [("P", 0, hf), ("tab", j)], ["tmpB"])
                    S.tt("dve", dst[:, 1, cs], tmpA[:], tmpB[:], ALU.add, ["tmpA", "tmpB"], [(dname, 1, j)])
            sV, kV = load_w(w_ret[h, :, 512:1024], 512)
            for t in range(NT):
                hf = t % 2
                ps = PA[:, hf * 512:(hf + 1) * 512]
                S.mms([(ps, xT[:, kc, t * 128:(t + 1) * 128], wring[:, kc, sV * 128:(sV + 4) * 128], kc == 0, kc == 7) for kc in range(8)],
                      reads=[("xT", t)] + kV, writes=[("P", 0, hf)])
                S.act(vt[:, t, :], ps, AF.Copy, [("P", 0, hf), "cvec"], [("vt", t)], scale=vs_h)
            sG, kG = load_w(w_ret[h, :, 1024:1536], 512)
            for c in range(4):
                for j in range(4):
                    hf = j % 2
                    ps = PB[:, hf * 512:(hf + 1) * 512]
                    proj_fm(sG, 8, xT_rhs, j, ps, ("P", 1, hf), xt_keys(j), kG, c0=c * 128)
                    S.act(gT[:, c, j * 512:(j + 1) * 512], ps, AF.Silu, [("P", 1, hf)], [("gT", c, j)])
            S.dma("pool", wo[:], w_oe[h * 512:(h + 1) * 512, :].rearrange("(c p) n -> p c n", p=128), [], ["wo"], "wo")
            PDb = PD[:].bitcast(BF16)

            def o_bank(n):
                return (PC[:, 512:1024], ("P", 2, 1)) if n % 2 == 0 else (PB[:, 512:1024], ("P", 1, 1))

            def head_part(n, h=h, gC=gC, xi_h=xi_h):
                j = n // 4
                cs = slice(n * 128, (n + 1) * 128)
                d2 = n % 2
                qk = [("qT", 0, j), ("qT", 1, j)]
                kk = [("kT", 0, j), ("kT", 1, j)]
                st_ps = PC[:, d2 * 128:(d2 + 1) * 128]
                S.mms([(st_ps, kT[:, c, cs], qT[:, c, cs], c == 0, c == 1) for c in range(2)], reads=qk + kk, writes=[("P", 2, 0, d2)])
                S.tt("dve", innT[:, d2, :], st_ps, maskT[:], ALU.mult, [("P", 2, 0, d2), "maskT"], [("innT", d2)])
                if n < NT - 1:
                    kt_ps = PDb[:, 0:256]
                    S.transposes([(kt_ps[:, c * 128:(c + 1) * 128], kT[:, c, cs]) for c in range(2)], ident[:],
                                 reads=kk + ["ident"], writes=[("P", 3, 0, "kt")])
                    S.act(ktok[:, d2, :], kt_ps, AF.Copy, [("P", 3, 0, "kt")], [("ktok", d2)], scale=gC)
                o_ps, o_key = o_bank(n)
                specs = [(o_ps, innT[:, d2, :], vt[:, n, :], True, n == 0)]
                rk = [("innT", d2), ("vt", n)]
                if n > 0:
                    for c in range(2):
                        specs.append((o_ps, qT[:, c, cs], stb[:, c, :], False, c == 1))
                    rk += qk + [("stb", 0), ("stb", 1)]
                S.mms(specs, reads=rk, writes=[o_key])
                if n < NT - 1:
                    S.mms([(PD[:, 512:1024], ktok[:, d2, 0:128], vt[:, n, :], True, True)], reads=[("ktok", d2), ("vt", n)], writes=[("P", 3, 1)])
                    S.mms([(PB[:, 0:512], ktok[:, d2, 128:256], vt[:, n, :], True, True)], reads=[("ktok", d2), ("vt", n)], writes=[("P", 1, 0)])
                    if n == 0:
                        S.copy("dve", st32[:, 0, :], PD[:, 512:1024], [("P", 3, 1)], [("st32", 0)])
                        S.copy("dve", st32[:, 1, :], PB[:, 0:512], [("P", 1, 0)], [("st32", 1)])
                    else:
                        S.stt("dve", st32[:, 0, :], st32[:, 0, :], gC, PD[:, 512:1024], ALU.mult, ALU.add, [("P", 3, 1), ("st32", 0)], [("st32", 0)])
                        S.stt("dve", st32[:, 1, :], st32[:, 1, :], gC, PB[:, 0:512], ALU.mult, ALU.add, [("P", 1, 0), ("st32", 1)], [("st32", 1)])
                    S.copy("act", stb[:, 0, :], st32[:, 0, :], [("st32", 0)], [("stb", 0)])
                    S.copy("pool", stb[:, 1, :], st32[:, 1, :], [("st32", 1)], [("stb", 1)])
                S.add("dve", (lambda o_ps=o_ps: (lambda e: e.bn_stats(out=bst[:, d2, :], in_=o_ps)))(), reads=[o_key], writes=[("bst", d2)])
                S.add("dve", lambda e: e.bn_aggr(out=bmv[:, d2, :], in_=bst[:, d2, :]), reads=[("bst", d2)], writes=[("bmv", d2)])
                S.ts("dve", sm[:, d2, 0:1], bmv[:, d2, 1:2], xi2[:, h:h + 1], LN_EPS, ALU.mult, ALU.add, [("bmv", d2), "xi2"], [("sm", d2, 0)])
                S.act(sm[:, d2, 1:2], sm[:, d2, 0:1], AF.Sqrt, [("sm", d2, 0)], [("sm", d2, 1)])
                S.add("dve", lambda e: e.reciprocal(out=sm[:, d2, 1:2], in_=sm[:, d2, 1:2]), reads=[("sm", d2, 1)], writes=[("sm", d2, 1)])
                S.ts("dve", sm[:, d2, 2:3], sm[:, d2, 1:2], xi_h, None, ALU.mult, None, [("sm", d2, 1), "cvec"], [("sm", d2, 2)])
                S.stt("dve", sm[:, d2, 3:4], bmv[:, d2, 0:1], -1.0, sm[:, d2, 2:3], ALU.mult, ALU.mult, [("bmv", d2), ("sm", d2, 2)], [("sm", d2, 3)])
                S.act(onb[:, d2, :], o_ps, AF.Identity, [o_key, ("sm", d2, 2), ("sm", d2, 3)], [("onb", d2)],
                      scale=sm[:, d2, 2:3], bias=sm[:, d2, 3:4])

            def tail_part(n):
                j = n // 4
                cs = slice(n * 128, (n + 1) * 128)
                d2 = n % 2
                ct_ps = PDb[:, 512:1024]
                S.transposes([(ct_ps[:, c * 128:(c + 1) * 128], onb[:, d2, c * 128:(c + 1) * 128]) for c in range(4)], ident[:],
                             reads=[("onb", d2), "ident"], writes=[("P", 3, 0, "ct")])
                S.tt("dve", catc[:, d2, :, :], ct_ps.rearrange("p (c k) -> p c k", k=128), gT[:, :, cs], ALU.mult,
                     [("P", 3, 0, "ct")] + [("gT", c, j) for c in range(4)], [("catc", d2)])
                specs = []
                for half in range(2):
                    for c in range(4):
                        specs.append((PA[:, half * 512:(half + 1) * 512], catc[:, d2, c, :], wo[:, c, half * 512:(half + 1) * 512], c == 0, c == 3))
                S.mms(specs, reads=[("catc", d2), "wo"], writes=[("P", 0, 0), ("P", 0, 1)])
                S.tt("dve", yacc[:, n, :], yacc[:, n, :], PA[:], ALU.add, [("P", 0, 0), ("P", 0, 1), ("yacc", n)], [("yacc", n)])

            head_part(0)
            for n in range(NT):
                if n + 1 < NT:
                    head_part(n + 1)
                tail_part(n)
        S.barrier()

    if debug_out == "ret":
        for t in range(NT):
            S.dma("sp", out_d[t * 128:(t + 1) * 128, :], yacc[:, t, :], [("yacc", t)], [("out", t)], "out")
        S.finalize(); top.close(); S.close()
        return nc

    with ExitStack() as esT:
        tab64 = sb(esT, "tab64", [128, 2, S_LEN], F32)
        gen_tables(tab64, inv64, sgn64)
        ebuf = sb(esT, "ebuf", [128, 3, 512], BF16)
        rs_t = sb(esT, "rs_t", [128, 8], F32)
        S.barrier()

        def rope_pair(psA, psB, kA, kB, j, out_ap, out_key, eng_out="dve"):
            cs = slice(j * 512, (j + 1) * 512)
            S.tt("dve", tmpA[:], psA, tab64[:, 0, cs], ALU.mult, [kA, ("tab", j)], ["tmpA"])
            S.tt("dve", tmpB[:], psB, tab64[:, 1, cs], ALU.mult, [kB, ("tab", j)], ["tmpB"])
            S.tt("dve", out_ap, tmpA[:], tmpB[:], ALU.add, ["tmpA", "tmpB"], [out_key])

        with ExitStack() as esM:
            mqT = sb(esM, "mqT", [128, 2, S_LEN], BF16)
            mkvT = sb(esM, "mkvT", [128, 2, S_LEN], BF16)
            kpeT = sb(esM, "kpeT", [128, S_LEN], BF16)
            qpz = [sb(esM, "qpz%d" % i, [128, S_LEN], BF16) for i in range(2)]
            S.add("pool", lambda e: e.memset(qpz[0][64:128, :], 0.0), writes=[("qpz", 0, j) for j in range(4)])
            S.add("pool", lambda e: e.memset(qpz[1][0:64, :], 0.0), writes=[("qpz", 1, j) for j in range(4)])
            qnT = sb(esM, "qnT", [128, S_LEN], BF16)
            knT = sb(esM, "knT", [128, S_LEN], BF16)
            Vaug = sb(esM, "Vaug", [128, NT, 129], BF16)
            gmT = sb(esM, "gmT", [128, S_LEN], BF16)
            catM = sb(esM, "catM", [128, 2, S_LEN], BF16)
            ovl = sb(esM, "ovl", [128, S_LEN], BF16)
            onM = ovl[:].rearrange("p (t k) -> p t k", k=128)
            rawg = ovl[:].bitcast(F32).rearrange("p (c k) -> p c k", k=512)
            sqb = ebuf[:, 0:2, :]
            S.add("pool", lambda e: e.memset(Vaug[:, :, 128:129], 1.0), writes=["VaugOnes"])

            for which, (dstT, gvec, col0) in enumerate(((mqT, qn, 0), (mkvT, kvn, 256))):
                dn = "mqT" if which == 0 else "mkvT"
                s0, k0 = load_w(w_mla[:, col0:col0 + 128], 128)
                s1, k1 = load_w(w_mla[:, col0 + 128:col0 + 256], 128)
                for j in range(4):
                    hf = j % 2
                    psl = [PA[:, hf * 512:(hf + 1) * 512], PB[:, hf * 512:(hf + 1) * 512]]
                    pk = [("P", 0, hf), ("P", 1, hf)]
                    proj_fm(s0, 8, xT_rhs, j, psl[0], pk[0], xt_keys(j), k0)
                    proj_fm(s1, 8, xT_rhs, j, psl[1], pk[1], xt_keys(j), k1)
                    for c in range(2):
                        S.act(sqb[:, c, :], psl[c], AF.Square, [pk[c]], [("sqb", c)])
                        S.act(rawg[:, c, :], psl[c], AF.Copy, [pk[c], "qn", "kvn"], [("rawg", c)], scale=gvec[:, c:c + 1])
                    ss_ps = PC[:, 0:512]
                    S.mms([(ss_ps, ones_b[:], sqb[:, c, :], c == 0, c == 1) for c in range(2)],
                          reads=[("sqb", 0), ("sqb", 1), "ones_b"], writes=[("P", 2, 0)])
                    S.act(tmpA[:], ss_ps, AF.Sqrt, [("P", 2, 0), "cvec"], ["tmpA"], scale=1.0 / 256.0, bias=eps_rms)
                    S.add("dve", lambda e: e.reciprocal(out=tmpA[:], in_=tmpA[:]), reads=["tmpA"], writes=["tmpA"])
                    for c in range(2):
                        S.tt("dve", dstT[:, c, j * 512:(j + 1) * 512], rawg[:, c, :], tmpA[:], ALU.mult,
                             [("rawg", c), "tmpA"], [(dn, j)])
            MQ = [("mqT", j) for j in range(4)]
            MKV = [("mkvT", j) for j in range(4)]
            sA, kA_ = load_w(w_mla[:, 512:640], 128)
            sB, kB_ = load_w(w_mla[:, 640:768], 128)
            for j in range(4):
                hf = j % 2
                psA = PA[:, hf * 512:(hf + 1) * 512]
                psB = PB[:, hf * 512:(hf + 1) * 512]
                proj_fm(sA, 8, xT_rhs, j, psA, ("P", 0, hf), xt_keys(j), kA_)
                proj_fm(sB, 8, xT_rhs, j, psB, ("P", 1, hf), xt_keys(j), kB_)
                rope_pair(psA, psB, ("P", 0, hf), ("P", 1, hf), j, kpeT[:, j * 512:(j + 1) * 512], ("kpeT", j))
            KPE = [("kpeT", j) for j in range(4)]
            S.barrier()

            def mq_rhs(kc, j):
                return mqT[:, kc, j * 512:(j + 1) * 512]

            def mkv_rhs(kc, j):
                return mkvT[:, kc, j * 512:(j + 1) * 512]

            for hp in range(4):
                sA, kA_ = load_w(w_uq[:, 1024 + hp * 128: 1024 + (hp + 1) * 128], 128)
                sB, kB_ = load_w(w_uq[:, 1536 + hp * 128: 1536 + (hp + 1) * 128], 128)
                for j in range(4):
                    hf = j % 2
                    psA = PA[:, hf * 512:(hf + 1) * 512]
                    psB = PB[:, hf * 512:(hf + 1) * 512]
                    proj_fm(sA, 2, mq_rhs, j, psA, ("P", 0, hf), [("mqT", j)], kA_)
                    proj_fm(sB, 2, mq_rhs, j, psB, ("P", 1, hf), [("mqT", j)], kB_)
                    cs = slice(j * 512, (j + 1) * 512)
                    S.tt("dve", tmpA[:], psA, tab64[:, 0, cs], ALU.mult, [("P", 0, hf), ("tab", j)], ["tmpA"])
                    S.tt("dve", tmpB[:], psB, tab64[:, 1, cs], ALU.mult, [("P", 1, hf), ("tab", j)], ["tmpB"])
                    S.tt("dve", qpz[0][0:64, cs], tmpA[0:64, :], tmpB[0:64, :], ALU.add, ["tmpA", "tmpB"], [("qpz", 0, j)])
                    S.tt("dve", qpz[1][64:128, cs], tmpA[64:128, :], tmpB[64:128, :], ALU.add, ["tmpA", "tmpB"], [("qpz", 1, j)])
                S.dma("pool", wo[:, 0:2, :], w_oe[2048 + hp * 256: 2048 + (hp + 1) * 256, :].rearrange("(c p) n -> p c n", p=128),
                      [], ["wo"], "wo")
                for e2 in range(2):
                    h = hp * 2 + e2
                    r0 = 64 * e2
                    sQ, kQ = load_w(w_uq[:, h * 128:(h + 1) * 128], 128)
                    sK, kK = load_w(w_ukv[:, h * 128:(h + 1) * 128], 128)
                    sVv, kVv = load_w(w_ukv[:, 1024 + h * 128: 1024 + (h + 1) * 128], 128)
                    sG, kG = load_w(w_mla[:, 768 + h * 128: 768 + (h + 1) * 128], 128)
                    for j in range(4):
                        hf = j % 2
                        psA = PA[:, hf * 512:(hf + 1) * 512]
                        psB = PB[:, hf * 512:(hf + 1) * 512]
                        proj_fm(sQ, 2, mq_rhs, j, psA, ("P", 0, hf), [("mqT", j)], kQ)
                        S.copy("act", qnT[:, j * 512:(j + 1) * 512], psA, [("P", 0, hf)], [("qnT", j)])
                        proj_fm(sK, 2, mkv_rhs, j, psB, ("P", 1, hf), [("mkvT", j)], kK)
                        S.copy("act", knT[:, j * 512:(j + 1) * 512], psB, [("P", 1, hf)], [("knT", j)])
                    for j in range(4):
                        hf = j % 2
                        psA = PA[:, hf * 512:(hf + 1) * 512]
                        proj_fm(sG, 8, xT_rhs, j, psA, ("P", 0, hf), xt_keys(j), kG)
                        S.act(gmT[:, j * 512:(j + 1) * 512], psA, AF.Silu, [("P", 0, hf)], [("gmT", j)])
                    for t in range(NT):
                        hf = t % 2
                        ps = PB[:, hf * 512: hf * 512 + 128]
                        S.mms([(ps, mkvT[:, kc, t * 128:(t + 1) * 128], wslice(sVv, kc), kc == 0, kc == 1) for kc in range(2)],
                              reads=[("mkvT", t // 4)] + kVv, writes=[("P", 1, hf)])
                        S.copy("act", Vaug[:, t, 0:128], ps, [("P", 1, hf), "VaugOnes"], ["mV"])
                    QN = [("qnT", j) for j in range(4)]
                    KN = [("knT", j) for j in range(4)]
                    qk_list = [
                        (lambda kt: knT[:, kt * 128:(kt + 1) * 128], lambda lo, hi: qnT[:, lo:hi], QN + KN),
                        (lambda kt: kpeT[:, kt * 128:(kt + 1) * 128], lambda lo, hi, e2=e2: qpz[e2][:, lo:hi],
                         [("qpz", e2, j) for j in range(4)] + KPE),
                    ]
                    attention_core("m", qk_list, None, lambda kt: Vaug[:, kt, :], 128, 192.0 ** -0.5,
                                   lambda t: onM[:, t, :], lambda t: ("onM", t), ebuf, rs_t)
                    for jg in range(4):
                        ct_ps = PB[:].bitcast(BF16)[:, 0:512]
                        S.transposes([(ct_ps[:, i * 128:(i + 1) * 128], onM[:, jg * 4 + i, :]) for i in range(4)], ident[:],
                                     reads=[("onM", jg * 4 + i) for i in range(4)] + ["ident"], writes=[("P", 1, 0)])
                        S.tt("dve", catM[:, e2, jg * 512:(jg + 1) * 512], ct_ps, gmT[:, jg * 512:(jg + 1) * 512], ALU.mult,
                             [("P", 1, 0), ("gmT", jg)], [("catM", e2, jg)])
                for t in range(NT):
                    yacc_accumulate(catM, 2, lambda t: ("catM", 0, t // 4), ["wo"] + [("catM", 1, j) for j in range(4)], t)
            S.barrier()

        layer_norm(0, last=(n_layers == 1))
        if n_layers == 1:
            S.finalize(); esT.close(); top.close(); S.close()
            return nc

        with ExitStack() as esL:
            qTl = sb(esL, "qTl", [128, S_LEN], BF16)
            kmh = sb(esL, "kmh", [128, 8], BF16)
            kml = sb(esL, "kml", [128, 8], BF16)
            kmr = sb(esL, "kmr", [128, 8], F32)
            qz = [sb(esL, "qz%d" % i, [128, S_LEN], BF16) for i in range(2)]
            S.add("pool", lambda e: e.memset(qz[0][64:128, :], 0.0), writes=[("qz", 0, j) for j in range(4)])
            S.add("pool", lambda e: e.memset(qz[1][0:64, :], 0.0), writes=[("qz", 1, j) for j in range(4)])
            kTb = sb(esL, "kTb", [128, S_LEN], BF16)
            kmean = sb(esL, "kmean", [128, 8], F32)
            Vp = sb(esL, "Vp", [128, NT, 2, 65], BF16)
            gmT = sb(esL, "gmT1", [128, S_LEN], BF16)
            nselT = sb(esL, "nselT", [128, 2, S_LEN], BF16)
            S.add("pool", lambda e: e.memset(nselT[:], 0.0), writes=[("nselT", e, j) for e in range(2) for j in range(4)])
            catL = sb(esL, "catL", [128, 2, S_LEN], BF16)
            onL = sb(esL, "onL", [128, NT, 128], BF16)
            gm = sb(esL, "gm", [128, 128], F32)
            sel = sb(esL, "sel", [128, 128], F32)
            top8 = sb(esL, "top8", [128, 8], F32)
            biasb = sb(esL, "biasb", [128, 128], BF16)
            S.add("pool", lambda e: e.memset(Vp[:, :, :, 64:65], 1.0), writes=["VpOnes"])

            for hp in range(8):
                sl = hp % 2
                for qi in range(2):
                    base = qi * 2048
                    sA, kA_ = load_w(w_l1[:, base + hp * 128: base + (hp + 1) * 128], 128)
                    sB, kB_ = load_w(w_l1[:, base + 1024 + hp * 128: base + 1024 + (hp + 1) * 128], 128)
                    for j in range(4):
                        hf = j % 2
                        psA = PA[:, hf * 512:(hf + 1) * 512]
                        psB = PB[:, hf * 512:(hf + 1) * 512]
                        proj_fm(sA, 8, xT_rhs, j, psA, ("P", 0, hf), xt_keys(j), kA_)
                        proj_fm(sB, 8, xT_rhs, j, psB, ("P", 1, hf), xt_keys(j), kB_)
                        cs = slice(j * 512, (j + 1) * 512)
                        if qi == 0:
                            S.tt("dve", tmpA[:], psA, tab64[:, 0, cs], ALU.mult, [("P", 0, hf), ("tab", j)], ["tmpA"])
                            S.tt("dve", tmpB[:], psB, tab64[:, 1, cs], ALU.mult, [("P", 1, hf), ("tab", j)], ["tmpB"])
                            S.tt("dve", tmpA[:], tmpA[:], tmpB[:], ALU.add, ["tmpA", "tmpB"], ["tmpA"])
                            S.copy("dve", qz[0][0:64, cs], tmpA[0:64, :], ["tmpA"], [("qz", 0, j)])
                            S.copy("dve", qz[1][64:128, cs], tmpA[64:128, :], ["tmpA"], [("qz", 1, j)])
                            S.tt("dve", qTl[0:64, cs], tmpA[0:64, :], qz[0][0:64, cs], ALU.subtract, ["tmpA", ("qz", 0, j)], [("qTl", j)])
                            S.tt("dve", qTl[64:128, cs], tmpA[64:128, :], qz[1][64:128, cs], ALU.subtract, ["tmpA", ("qz", 1, j)], [("qTl", j)])
                        else:
                            S.tt("dve", tmpA[:], psA, tab64[:, 0, cs], ALU.mult, [("P", 0, hf), ("tab", j)], ["tmpA"])
                            S.tt("dve", tmpB[:], psB, tab64[:, 1, cs], ALU.mult, [("P", 1, hf), ("tab", j)], ["tmpB"])
                            S.tt("dve", tmpA[:], tmpA[:], tmpB[:], ALU.add, ["tmpA", "tmpB"], ["tmpA"])
                            S.copy("act", kTb[:, cs], tmpA[:], ["tmpA"], [("kTb", j)])
                            S.add("dve", (lambda j=j: (lambda e: e.tensor_reduce(out=kmean[:, 2 * j:2 * j + 2],
                                                                                 in_=tmpA[:].rearrange("p (b l) -> p b l", l=256),
                                                                                 axis=AX.X, op=ALU.add)))(),
                                  reads=["tmpA"], writes=["kmean"])
                QL = [("qTl", j) for j in range(4)]
                S.copy("dve", kmh[:], kmean[:], ["kmean"], ["kmh"])
                S.tt("dve", kmr[:], kmean[:], kmh[:], ALU.subtract, ["kmean", "kmh"], ["kmr"])
                S.copy("dve", kml[:], kmr[:], ["kmr"], ["kml"])
                KB = [("kTb", j) for j in range(4)]
                sVv, kVv = load_w(w_l1[:, 4096 + hp * 128: 4096 + (hp + 1) * 128], 128)
                sG, kG = load_w(w_l1[:, 5120 + hp * 128: 5120 + (hp + 1) * 128], 128)
                for t in range(NT):
                    hf = t % 2
                    ps = PB[:, hf * 512: hf * 512 + 128]
                    S.mms([(ps, xT[:, kc, t * 128:(t + 1) * 128], wslice(sVv, kc), kc == 0, kc == 7) for kc in range(8)],
                          reads=[("xT", t)] + kVv, writes=[("P", 1, hf)])
                    S.copy("act", Vp[:, t, :, 0:64], ps.rearrange("p (a b) -> p a b", b=64), [("P", 1, hf), "VpOnes"], ["lV"])
                for j in range(4):
                    hf = j % 2
                    psA = PA[:, hf * 512:(hf + 1) * 512]
                    proj_fm(sG, 8, xT_rhs, j, psA, ("P", 0, hf), xt_keys(j), kG)
                    S.act(gmT[:, j * 512:(j + 1) * 512], psA, AF.Silu, [("P", 0, hf)], [("gmT", j)])
                if hp % 2 == 0:
                    S.dma("pool", wo[:, 0:2, :], w_oo[hp * 128:(hp + 2) * 128, :].rearrange("(c p) n -> p c n", p=128), [], ["wo"], "wo")
                for e2 in range(2):
                    r0 = 64 * e2
                    g_ps = PC[:, 0:128]
                    gspecs = []
                    for t in range(NT):
                        ts_ = slice(t * 128, (t + 1) * 128)
                        gspecs.append((g_ps[:, t * 8:(t + 1) * 8], qz[e2][r0:r0 + 64, ts_], kmh[r0:r0 + 64, :], True, False))
                        gspecs.append((g_ps[:, t * 8:(t + 1) * 8], qTl[r0:r0 + 64, ts_], kmh[r0:r0 + 64, :], False, False))
                        gspecs.append((g_ps[:, t * 8:(t + 1) * 8], qz[e2][r0:r0 + 64, ts_], kml[r0:r0 + 64, :], False, True))
                    S.mms(gspecs, reads=[("qz", e2, j) for j in range(4)] + QL + ["kmh", "kml"], writes=[("P", 2, 0)])
                    S.tt("dve", gm[:], g_ps, past01[:], ALU.mult, [("P", 2, 0), "past01"], ["gm"])
                    S.tt("dve", gm[:], gm[:], negoff[:], ALU.add, ["gm", "negoff"], ["gm"])
                    for t in range(NT):
                        S.add("dve", (lambda t=t: (lambda e: e.max(out=top8[:], in_=gm[:, t * 8:(t + 1) * 8])))(), reads=["gm"], writes=["top8"])
                        S.ts("dve", sel[:, t * 8:(t + 1) * 8], gm[:, t * 8:(t + 1) * 8], top8[:, 2:3], None, ALU.is_ge, None, ["gm", "top8"], ["sel"])
                    S.tt("dve", sel[:], sel[:], past01[:], ALU.mult, ["sel", "past01"], ["sel"])
                    S.tt("dve", sel[:], sel[:], own01[:], ALU.add, ["sel", "own01"], ["sel"])
                    S.ts("dve", biasb[:], sel[:], NEGB, -NEGB, ALU.mult, ALU.add, ["sel"], ["biasb"])
                    bt_ps = PC[:].bitcast(BF16)[0:8, 1024:2048]
                    for q4 in range(4):
                        S.transposes([(bt_ps[:, i * 128:(i + 1) * 128], biasb[:, (q4 * 4 + i) * 8:(q4 * 4 + i + 1) * 8]) for i in range(4)], ident[:],
                                     reads=["biasb", "ident"], writes=[("P", 2, 1)])
                        S.copy("act", nselT[0:8, e2, q4 * 512:(q4 + 1) * 512], bt_ps[:, 0:512], [("P", 2, 1)], [("nselT", e2, q4)])
                    NS = [("nselT", e2, j) for j in range(4)]

                    def bias_fn(kt, jg, lo, hi, e2=e2, NS=NS):
                        nb_ = kt // 2
                        if nb_ == 2 * jg + 1:
                            return None
                        return (lambda kt_, nb_=nb_: Eall[:, nb_ * 128:(nb_ + 1) * 128],
                                lambda lo_, hi_, e2=e2: nselT[:, e2, lo_:hi_], NS + ["Eall"])

                    qk_list = [(lambda kt: kTb[:, kt * 128:(kt + 1) * 128],
                                lambda lo, hi, e2=e2: qz[e2][:, lo:hi], [("qz", e2, j) for j in range(4)] + KB)]
                    attention_core("l", qk_list, bias_fn, (lambda kt, e2=e2: Vp[:, kt, e2, :]), 64, 0.125,
                                   (lambda t, e2=e2: onL[:, t, e2 * 64:(e2 + 1) * 64]), (lambda t, e2=e2: ("onL", t, e2)), ebuf, rs_t)
                for jg in range(4):
                    ct_ps = PB[:].bitcast(BF16)[:, 0:512]
                    S.transposes([(ct_ps[:, i * 128:(i + 1) * 128], onL[:, jg * 4 + i, :]) for i in range(4)], ident[:],
                                 reads=[("onL", jg * 4 + i, e) for i in range(4) for e in range(2)] + ["ident"], writes=[("P", 1, 0)])
                    S.tt("dve", catL[:, sl, jg * 512:(jg + 1) * 512], ct_ps, gmT[:, jg * 512:(jg + 1) * 512], ALU.mult,
                         [("P", 1, 0), ("gmT", jg)], [("catL", sl, jg)])
                if hp % 2 == 1:
                    for t in range(NT):
                        yacc_accumulate(catL, 2, lambda t: ("catL", 0, t // 4), ["wo"] + [("catL", 1, j) for j in range(4)], t)
            S.barrier()
        layer_norm(1, last=True)

    S.finalize()
    top.close()
    S.close()
    return nc


def _host_constants():
    c = {}
    c["c_ident"] = np.eye(128, dtype=np.float32)
    k = np.arange(128)
    c["c_mask"] = (k[None, :] >= k[:, None]).astype(np.float32)
    vec = np.zeros((128, 16), np.float32)
    vec[:, 0] = 1.0 / (10000.0 ** np.linspace(0.0, 1.0, 128, dtype=np.float32))
    inv64 = 1.0 / (10000.0 ** (np.arange(0, 64, 2, dtype=np.float32) / 64.0))
    vec[:, 1] = inv64[np.arange(128) % 32]
    vec[:, 2] = np.where((np.arange(128) % 64) < 32, -1.0, 1.0)
    vec[:, 3] = LN_EPS
    vec[:, 4] = RMS_EPS
    xi2 = np.zeros((128, 4), np.float32)
    for h in range(4):
        g = 1.0 - 2.0 ** (-5.0 - h)
        xi = (g ** (k + 1.0)) * (256.0 ** -0.5)
        vec[:, 8 + h] = xi
        vec[:, 12 + h] = g ** (-(k + 1.0))
        xi2[:, h] = xi * xi
    c["c_vec"] = vec
    c["c_xi2"] = xi2
    E = np.zeros((8, 8, 128), np.float32)
    for n in range(8):
        E[n, n, :] = 1.0
    c["c_E"] = E.reshape(8, 1024)
    past = np.zeros((16, 8), np.float32)
    own = np.zeros((16, 8), np.float32)
    for t in range(16):
        qb = t // 2
        past[t, :qb] = 1.0
        own[t, qb] = 1.0
    c["c_past"] = np.broadcast_to(past.reshape(1, 128), (128, 128)).copy()
    c["c_own"] = np.broadcast_to(own.reshape(1, 128), (128, 128)).copy()
    c["c_negoff"] = ((c["c_past"] - 1.0) * 1e30).astype(np.float32)
    return c


def _layout_weights(w_in_even, q_norm_even, w_uq_even, kv_norm_even, w_ukv_even, w_out_even, w_in_odd, w_out_odd):
    wi = np.asarray(w_in_even[0])
    o = {}
    ev = np.arange(0, 256, 2)
    od = np.arange(1, 256, 2)
    w_ret = np.empty((4, 1024, 1536), np.float32)
    for h in range(4):
        rq = wi[:, h * 256:(h + 1) * 256]
        rk = wi[:, 1024 + h * 256: 1024 + (h + 1) * 256]
        w_ret[h, :, 0:128] = rq[:, ev]
        w_ret[h, :, 128:256] = rq[:, od]
        w_ret[h, :, 256:384] = rk[:, ev]
        w_ret[h, :, 384:512] = rk[:, od]
        w_ret[h, :, 512:1024] = wi[:, 2048 + h * 512: 2048 + (h + 1) * 512]
        w_ret[h, :, 1024:1536] = wi[:, 4096 + h * 512: 4096 + (h + 1) * 512]
    o["w_ret"] = w_ret
    b = 6144
    mq = wi[:, b:b + 256]
    mkv = wi[:, b + 256:b + 512]
    mkr = wi[:, b + 512:b + 576]
    mg = wi[:, b + 576:b + 1600]
    swap64 = np.concatenate([np.arange(32, 64), np.arange(0, 32)])
    o["w_mla"] = np.ascontiguousarray(np.concatenate([mq, mkv, mkr, mkr, mkr[:, swap64], mkr[:, swap64], mg], axis=1))
    wuq = np.asarray(w_uq_even[0])
    nope = np.concatenate([wuq[:, h * 192: h * 192 + 128] for h in range(8)], axis=1)
    pe1 = np.concatenate([wuq[:, h * 192 + 128: h * 192 + 192] for h in range(8)], axis=1)
    pe2 = np.concatenate([wuq[:, h * 192 + 128: h * 192 + 192][:, swap64] for h in range(8)], axis=1)
    o["w_uq"] = np.ascontiguousarray(np.concatenate([nope, pe1, pe2], axis=1))
    wukv = np.asarray(w_ukv_even[0])
    kn = np.concatenate([wukv[:, h * 256: h * 256 + 128] for h in range(8)], axis=1)
    vv = np.concatenate([wukv[:, h * 256 + 128: h * 256 + 256] for h in range(8)], axis=1)
    o["w_ukv"] = np.ascontiguousarray(np.concatenate([kn, vv], axis=1))
    o["w_oe"] = np.ascontiguousarray(np.asarray(w_out_even[0]))
    wo_ = np.asarray(w_in_odd[0])
    sw = np.concatenate([h * 64 + swap64 for h in range(16)])
    q = wo_[:, 0:1024]
    kk = wo_[:, 1024:2048]
    o["w_l1"] = np.ascontiguousarray(np.concatenate([q, q[:, sw], kk, kk[:, sw], wo_[:, 2048:3072], wo_[:, 3072:4096]], axis=1))
    o["w_oo"] = np.ascontiguousarray(np.asarray(w_out_odd[0]))
    o["qn"] = np.ascontiguousarray(np.asarray(q_norm_even[0]).reshape(2, 128).T)
    o["kvn"] = np.ascontiguousarray(np.asarray(kv_norm_even[0]).reshape(2, 128).T)
    return o


def make_in_maps(x, positions, w_in_even, q_norm_even, w_uq_even, kv_norm_even, w_ukv_even,
                 w_out_even, w_in_odd, w_out_odd, ln_g, ln_b):
    shared = _host_constants()
    shared.update(_layout_weights(w_in_even, q_norm_even, w_uq_even, kv_norm_even, w_ukv_even, w_out_even, w_in_odd, w_out_odd))
    shared["lng"] = np.ascontiguousarray(np.asarray(ln_g, dtype=np.float32))
    shared["lnb"] = np.ascontiguousarray(np.asarray(ln_b, dtype=np.float32))
    x = np.asarray(x)
    positions = np.asarray(positions)
    in_maps = []
    for b in range(8):
        m = dict(shared)
        m["x"] = np.ascontiguousarray(x[b])
        m["pos"] = np.ascontiguousarray(positions[b].astype(np.int32).reshape(1, S_LEN))
        in_maps.append(m)
    return in_maps


def kernel(x, positions, w_in_even, q_norm_even, w_uq_even, kv_norm_even, w_ukv_even,
           w_out_even, w_in_odd, w_out_odd, ln_g, ln_b):
    in_maps = make_in_maps(x, positions, w_in_even, q_norm_even, w_uq_even, kv_norm_even, w_ukv_even,
                           w_out_even, w_in_odd, w_out_odd, ln_g, ln_b)
    nc = build_program(2)
    res = run_bass_kernel_spmd(nc, in_maps, core_ids=list(range(8)))
    out = np.stack([np.asarray(r["out"]) for r in res.results], axis=0).astype(np.float32)
    if os.environ.get("K_DUMP"):
        np.save(os.environ["K_DUMP"], out)
    return out
```
